# Optimizing a Trainium2 kernel written in Bass

```python
import jax, jax.numpy as jnp
from jax import lax
import numpy as np

D_MODEL = 2048
BATCH = 4
SEQ = 2048
DEPTH = 4

N_HEADS = 16
HEAD_DIM = D_MODEL // N_HEADS
D_FF = 4 * D_MODEL
N_META = 16
BLOCK = 128
N_A_LAYERS = DEPTH // 2
N_B_LAYERS = DEPTH - N_A_LAYERS
RMS_EPS = 1e-6
FGATE_BIAS_LO = 1.0
FGATE_BIAS_HI = 6.0

kernel_name = "yoco_fox_stickbreak_hybrid"


def rms_norm(x, g):
    xf = x.astype(jnp.float32)
    y = xf * lax.rsqrt(jnp.mean(xf * xf, axis=-1, keepdims=True) + RMS_EPS)
    return (y * g.astype(jnp.float32)).astype(x.dtype)


def split_heads(t):
    b, l, _ = t.shape
    return t.reshape(b, l, N_HEADS, HEAD_DIM)


def query_blocks(total_len):
    bounds = [(0, N_META)]
    for start in range(N_META, total_len, BLOCK):
        bounds.append((start, min(start + BLOCK, total_len)))
    return bounds


def forgetting_attention(q, k, v, cum_logf):
    scale = HEAD_DIM ** -0.5
    outs = []
    for qs, qe in query_blocks(q.shape[1]):
        s = jnp.einsum('bqhd,bkhd->bhqk', q[:, qs:qe], k[:, :qe]).astype(jnp.float32) * scale
        s = s + cum_logf[:, :, qs:qe, None] - cum_logf[:, :, None, :qe]
        q_pos = jnp.arange(qs, qe)
        k_pos = jnp.arange(qe)
        mask = k_pos[None, :] <= q_pos[:, None]
        s = jnp.where(mask, s, -jnp.inf)
        p = jax.nn.softmax(s, axis=-1).astype(v.dtype)
        outs.append(jnp.einsum('bhqk,bkhd->bqhd', p, v[:, :qe]))
    return jnp.concatenate(outs, axis=1)


def stick_breaking_attention(q, k, v):
    scale = HEAD_DIM ** -0.5
    outs = []
    for qs, qe in query_blocks(q.shape[1]):
        z = jnp.einsum('bqhd,bkhd->bhqk', q[:, qs:qe], k[:, :qe]).astype(jnp.float32) * scale
        q_pos = jnp.arange(qs, qe)
        k_pos = jnp.arange(qe)
        mask = k_pos[None, :] < q_pos[:, None]
        log_beta = jax.nn.log_sigmoid(z)
        log_one_minus = jnp.where(mask, jax.nn.log_sigmoid(-z), 0.0)
        suffix = lax.cumsum(log_one_minus, axis=3, reverse=True) - log_one_minus
        a = jnp.where(mask, jnp.exp(log_beta + suffix), 0.0).astype(v.dtype)
        outs.append(jnp.einsum('bhqk,bkhd->bqhd', a, v[:, :qe]))
    return jnp.concatenate(outs, axis=1)


def squared_relu_mlp(h, w_up, w_down):
    u = h @ w_up
    return jnp.square(jax.nn.relu(u)) @ w_down


def setup_inputs(seed: int = 0) -> dict:
    key = jax.random.key(seed)
    ks = jax.random.split(key, 16)
    f32 = jnp.float32
    d_in_scale = D_MODEL ** -0.5
    x = jax.random.normal(ks[0], (BATCH, SEQ, D_MODEL), f32)
    meta_tokens = jax.random.normal(ks[1], (N_META, D_MODEL), f32)
    norm_attn = 1.0 + 0.02 * jax.random.normal(ks[2], (DEPTH, D_MODEL), f32)
    norm_mlp = 1.0 + 0.02 * jax.random.normal(ks[3], (DEPTH, D_MODEL), f32)
    w_up = jax.random.normal(ks[4], (DEPTH, D_MODEL, D_FF), f32) * d_in_scale
    w_down = jax.random.normal(ks[5], (DEPTH, D_FF, D_MODEL), f32) * (D_FF ** -0.5)
    fox_w_in = jax.random.normal(ks[6], (N_A_LAYERS, D_MODEL, 3 * D_MODEL + N_HEADS), f32) * d_in_scale
    fox_b_f = (jnp.linspace(FGATE_BIAS_LO, FGATE_BIAS_HI, N_HEADS, dtype=f32)[None, :]
               + 0.1 * jax.random.normal(ks[7], (N_A_LAYERS, N_HEADS), f32))
    fox_w_o = jax.random.normal(ks[8], (N_A_LAYERS, D_MODEL, D_MODEL), f32) * d_in_scale
    kv_norm = 1.0 + 0.02 * jax.random.normal(ks[9], (D_MODEL,), f32)
    w_kv = jax.random.normal(ks[10], (D_MODEL, 2 * D_MODEL), f32) * d_in_scale
    sb_w_q = jax.random.normal(ks[11], (N_B_LAYERS, D_MODEL, D_MODEL), f32) * d_in_scale
    sb_w_o = jax.random.normal(ks[12], (N_B_LAYERS, D_MODEL, D_MODEL), f32) * d_in_scale
    final_norm = 1.0 + 0.02 * jax.random.normal(ks[13], (D_MODEL,), f32)
    return {"x": x, "meta_tokens": meta_tokens, "norm_attn": norm_attn, "norm_mlp": norm_mlp,
            "w_up": w_up, "w_down": w_down, "fox_w_in": fox_w_in, "fox_b_f": fox_b_f,
            "fox_w_o": fox_w_o, "kv_norm": kv_norm, "w_kv": w_kv, "sb_w_q": sb_w_q,
            "sb_w_o": sb_w_o, "final_norm": final_norm}


def reference(x, meta_tokens, norm_attn, norm_mlp, w_up, w_down, fox_w_in, fox_b_f, fox_w_o,
              kv_norm, w_kv, sb_w_q, sb_w_o, final_norm):
    b = x.shape[0]
    meta = jnp.broadcast_to(meta_tokens[None].astype(x.dtype), (b, N_META, D_MODEL))
    h = jnp.concatenate([meta, x], axis=1)
    length = h.shape[1]
    shared_k = None
    shared_v = None
    for layer in range(DEPTH):
        a = rms_norm(h, norm_attn[layer])
        if layer < N_A_LAYERS:
            i = layer
            proj = a @ fox_w_in[i]
            q, k, v, f_logit = jnp.split(proj, [D_MODEL, 2 * D_MODEL, 3 * D_MODEL], axis=-1)
            logf = jax.nn.log_sigmoid(f_logit.astype(jnp.float32) + fox_b_f[i].astype(jnp.float32))
            cum_logf = jnp.cumsum(logf, axis=1).transpose(0, 2, 1)
            o = forgetting_attention(split_heads(q), split_heads(k), split_heads(v), cum_logf)
            h = h + o.reshape(b, length, D_MODEL) @ fox_w_o[i]
        else:
            if layer == N_A_LAYERS:
                c = rms_norm(h, kv_norm)
                k_s, v_s = jnp.split(c @ w_kv, 2, axis=-1)
                shared_k = split_heads(k_s)
                shared_v = split_heads(v_s)
            i = layer - N_A_LAYERS
            q = split_heads(a @ sb_w_q[i])
            o = stick_breaking_attention(q, shared_k, shared_v)
            h = h + o.reshape(b, length, D_MODEL) @ sb_w_o[i]
        h = h + squared_relu_mlp(rms_norm(h, norm_mlp[layer]), w_up[layer], w_down[layer])
    return rms_norm(h, final_norm)[:, N_META:]
```

```python
from contextlib import ExitStack

import numpy as np
import ml_dtypes
import concourse.bass as bass
import concourse.mybir as mybir
from concourse.bass_utils import run_bass_kernel_spmd

F32 = mybir.dt.float32
BF16 = mybir.dt.bfloat16
AF = mybir.ActivationFunctionType
ALU = mybir.AluOpType

D = 2048
NH = 16
DH = 128
DFF = 8192
NMETA = 16
SEQ = 2048
BATCH = 4
DEPTH = 4
NCH = 16
NT = 1040
NR = 1024
TCH = [(0, 347), (347, 347), (694, 346)]
EPS = 1e-6
QSCALE = DH ** -0.5
NEGV = -30000.0
RG = [[0, 1], [2, 3], [4, 5], [6, 7]]
EPOCH = 30000
ENGS = ["pe", "act", "dve", "pool", "sp"]

C_ONES, C_NTRI, C_NONES, C_IDENT, C_ZERO, C_FNEG0, C_FNEG1, C_SNEG0, C_SNEG1, C_CNEG, C_SCNEG = range(11)
NCONST = 11


class Sched:
    def __init__(self):
        self.ops = []
        self.lastw = {}
        self.readers = {}
        self.alias = {}

    def region(self, key, arena, start, end):
        self.regions = getattr(self, "regions", {})
        for k2, (a2, s2, e2) in self.regions.items():
            if a2 == arena and s2 < end and start < e2 and k2 != key:
                self.alias.setdefault(key, set()).add(k2)
                self.alias.setdefault(k2, set()).add(key)
        self.regions[key] = (arena, start, end)

    def _keys(self, k):
        return [k] + list(self.alias.get(k, ()))

    def add(self, eng, emit, r=(), w=(), dma=None, inc=16):
        i = len(self.ops)
        hard = set()
        soft = set()
        for k in r:
            for kk in self._keys(k):
                j = self.lastw.get(kk)
                if j is not None:
                    hard.add(j)
        for k in w:
            for kk in self._keys(k):
                j = self.lastw.get(kk)
                if j is not None:
                    hard.add(j)
                for j in self.readers.get(kk, ()):
                    soft.add(j)
        self.ops.append(dict(eng=eng, emit=emit, hard=hard, soft=soft - hard, dma=dma, inc=inc))
        for k in r:
            self.readers.setdefault(k, []).append(i)
        for k in w:
            self.lastw[k] = i
            self.readers[k] = []
        return i

    def finalize(self):
        ops = self.ops
        self.eng_ops = {e: [] for e in ENGS}
        for i, o in enumerate(ops):
            self.eng_ops[o["eng"]].append(i)
        import bisect
        by_key = {}
        for i, o in enumerate(ops):
            if o["dma"] is not None:
                by_key.setdefault(o["dma"], []).append(i)
        for i, o in enumerate(ops):
            deps = {}
            for j in o["hard"] | o["soft"]:
                pj = ops[j]
                is_soft = j not in o["hard"]
                if pj["dma"] is None:
                    if pj["eng"] == o["eng"] and o["dma"] is None:
                        if o["eng"] == "pe" or is_soft:
                            continue
                    key = ("c", pj["eng"])
                else:
                    key = ("d", pj["dma"])
                    lst = by_key[pj["dma"]]
                    j = lst[bisect.bisect_left(lst, i) - 1]
                if key not in deps or deps[key] < j:
                    deps[key] = j
            o["deps"] = sorted(deps.values())
            for j in o["deps"]:
                ops[j]["needed"] = True
        cnt = {}
        dcnt = {}
        self.semkeys = set()
        for o in ops:
            if o["dma"] is None:
                if o.get("needed"):
                    e = o["eng"]
                    cnt[e] = cnt.get(e, 0) + 1
                    epoch, val = divmod(cnt[e] - 1, EPOCH)
                    o["tok"] = (("c", e, epoch), val + 1)
                    self.semkeys.add(o["tok"][0])
            else:
                k = o["dma"]
                dcnt[k] = dcnt.get(k, 0) + o["inc"]
                o["tok"] = (("d", k), dcnt[k])
                self.semkeys.add(o["tok"][0])

    def emit_engine(self, e, engobj, sems):
        ops = self.ops
        waited = {}
        for i in self.eng_ops[e]:
            o = ops[i]
            for j in o["deps"]:
                semkey, val = ops[j]["tok"]
                if waited.get(semkey, 0) < val:
                    engobj.wait_ge(sems[semkey], val)
                    waited[semkey] = val
            if o["emit"] is not None:
                ins = o["emit"](engobj)
                if "tok" in o:
                    ins.then_inc(sems[o["tok"][0]], o["inc"] if o["dma"] is not None else 1)


def build(nlayers=DEPTH):
    nc = bass.Bass("TRN2", target_bir_lowering=False)
    S = Sched()

    def din(name, shape, dt=F32):
        return nc.dram_tensor(name, list(shape), dt, kind="ExternalInput").ap()

    xT = din("xT", [D, NT])
    gvec_d = din("gvec", [128, 160])
    bf_d = din("bf", [16, 2])
    rk_d = din("rk", [16, 1])
    consts_d = din("consts", [128, NCONST * 128], BF16)
    w_up = din("w_up", [DEPTH, D, DFF])
    w_down = din("w_down", [DEPTH, DFF, D])
    fox_w_in = din("fox_w_in", [2, D, 3 * D + NH])
    wf_d = din("wf", [2, 128, 256])
    fox_w_o = din("fox_w_o", [2, D, D])
    w_kv = din("w_kv", [D, 2 * D])
    sb_w_q = din("sb_w_q", [2, D, D])
    sb_w_o = din("sb_w_o", [2, D, D])
    yT = nc.dram_tensor("yT", [D, NR], F32, kind="ExternalOutput").ap()

    def dscr(name, shape, dt=BF16):
        return nc.dram_tensor(name, list(shape), dt, kind="Internal").ap()

    q_scr = dscr("q_scr", [NH, 128, NT])
    o_scr = dscr("o_scr", [NH, 128, NT])
    kt_src = [dscr(f"kt_src{g}", [512, NR]) for g in range(4)]
    v_src = [dscr(f"v_src{g}", [NR, 512]) for g in range(4)]
    kt_dst = [dscr(f"kt_dst{g}", [1024, NR]) for g in range(4)]
    v_dst = [dscr(f"v_dst{g}", [2 * NR, 512]) for g in range(4)]
    skt_dst = [dscr(f"skt_dst{g}", [1024, NR]) for g in range(4)]
    sv_dst = [dscr(f"sv_dst{g}", [2 * NR, 512]) for g in range(4)]
    g_src = dscr("g_src", [16, NR], F32)
    g_dst = dscr("g_dst", [32, NR], F32)
    ka_scr = dscr("ka_scr", [NH, 6, 2064])
    qa_scr = dscr("qa_scr", [NH, 6, NT])

    es = ExitStack()
    with es:
        def sb(name, shape, dt):
            return es.enter_context(nc.sbuf_tensor(name, list(shape), dt))

        hT = sb("hT", [128, NCH, NT], F32)
        arenaB = sb("arenaB", [128, NCH * NT], BF16)
        ring = [sb(f"ring{i}", [128, 8192], BF16) for i in range(3)]
        TRB = 22528
        arenaT = sb("arenaT", [128, TRB // 2], BF16)
        arenaC = sb("arenaC", [128, 28896 // 2], BF16)
        gvec = sb("gvec_sb", [128, 160], F32)
        consts = sb("consts_sb", [128, NCONST * 128], BF16)
        bfs = sb("bf_sb", [16, 2], F32)
        negb = sb("negb_sb", [16, 2], F32)
        rks = sb("rk_sb", [16, 1], F32)
        ones16 = sb("ones16", [16, 128], F32)
        ktm = sb("ktm", [128, NH, 16], BF16)
        vm = sb("vm", [16, D], BF16)
        wf_sb = sb("wf_sb", [128, NCH, 16], BF16)
        psb = [es.enter_context(nc.psum_tensor(f"ps{i}", [128, 512], F32)) for i in range(8)]

        def carve(arena, off, shape, dt, parts=128):
            esz = 2 if dt == BF16 else 4
            n = int(np.prod(shape))
            a = arena[0:parts, off // 2: off // 2 + n * esz // 2]
            if dt == F32:
                a = a.bitcast(F32)
            if len(shape) == 2:
                a = a.rearrange("p (a b) -> p a b", a=shape[0])
            return a

        def V(key, arena, aname, off, shape, dt, parts=128):
            esz = 2 if dt == BF16 else 4
            S.region(key, aname, off, off + int(np.prod(shape)) * esz)
            return carve(arena, off, shape, dt, parts)

        aT = V("aT", arenaB, "B", 0, [NCH, NT], BF16)

        def cst(idx, rows=128, cols=128):
            return consts[0:rows, idx * 128: idx * 128 + cols]

        def gcol(col):
            return gvec[:, col:col + 1]

        PSK = [("ps", i) for i in range(8)]
        G = [(0, 1, 2), (3, 4, 5)]

        sqv = [V("sq0", arenaT, "T", 0, [NT], BF16), V("sq1", arenaT, "T", 2080, [NT], BF16)]
        tmpf = V("tmpf", arenaT, "T", 4160, [NT], F32)
        rstd = V("rstd", arenaT, "T", 8320, [NT], F32)
        stg = [V("stg0", arenaT, "T", 12480, [NT], BF16), V("stg1", arenaT, "T", 14560, [NT], BF16)]
        vstg = [V("vstg0", arenaT, "T", 16640, [512], BF16), V("vstg1", arenaT, "T", 17664, [512], BF16)]
        e16 = V("e16", arenaT, "T", 0, [NT], F32, parts=16)
        nl16 = V("nl16", arenaT, "T", 4160, [NT], F32, parts=16)
        qpc = V("qpc", arenaT, "T", 0, [3, NT], BF16, parts=16)
        pt = [V(f"pt{i}", arenaT, "T", 1024 * i, [512], BF16) for i in range(3)]
        ef = [V(f"ef{i}", arenaT, "T", 3072 + 2048 * i, [512], F32) for i in range(2)]
        l16 = [V(f"l16{i}", arenaT, "T", 7168 + 1024 * i, [512], BF16) for i in range(2)]
        lsum = V("lsum", arenaT, "T", 9216, [512], BF16)
        rdv = V("rdv", arenaT, "T", 10240, [512], F32)
        ostg = stg
        gT = [V("gT0", arenaT, "T", 12480, [4, NT], BF16), V("gT1", arenaT, "T", 0, [4, NT], BF16)]
        rtmp = [V("rtmp0", arenaT, "T", 8320, [512], F32), V("rtmp1", arenaT, "T", 10368, [512], F32)]
        fstg = [V("fstg0", arenaT, "T", 12480, [NR], F32), V("fstg1", arenaT, "T", 16576, [NR], F32)]

        HB = 16480
        att = []
        for i in range(2):
            o = i * HB
            att.append(dict(
                qt=V(f"att{i}qt", arenaB, "B", o, [NT], BF16),
                kt=V(f"att{i}kt", arenaB, "B", o + 2080, [2048], BF16),
                vv=V(f"att{i}vv", arenaB, "B", o + 6176, [16, 128], BF16),
                qa=V(f"att{i}qa", arenaB, "B", o + 10272, [NT], BF16, parts=6),
                ka=V(f"att{i}ka", arenaB, "B", o + 12352, [2064], BF16, parts=6),
            ))
        S.region("gate", "C", 0, 28896)
        attS = []
        for i in range(2):
            o = i * 10272
            attS.append(dict(
                qt=V(f"attS{i}qt", arenaC, "C", o, [NT], BF16),
                kt=V(f"attS{i}kt", arenaC, "C", o + 2080, [2048], BF16),
                vv=V(f"attS{i}vv", arenaC, "C", o + 6176, [16, 128], BF16),
            ))
        qstg = V("qstg", arenaC, "C", 20544, [NT], BF16)
        nl_all = carve(arenaC, 0, [2064], F32, parts=16)
        csv = carve(arenaC, 8256, [2064], F32, parts=16)
        pcs = carve(arenaC, 16512, [3, 2064], BF16, parts=16)

        ring_state = {"i": 0, "issued": 0, "plan": None, "req": []}
        LOOKAHEAD = 2

        def _slab_dst(i, view):
            slot = i % 3
            if view == "col":
                return ring[slot][:].rearrange("p (c n) -> p c n", c=NCH)
            return ring[slot][:].rearrange("p (c n) -> p c n", c=4)

        def _issue_slab(i):
            src_ap, view = ring_state["plan"][i]
            dst = _slab_dst(i, view)
            S.add("pool", lambda e, dst=dst, src_ap=src_ap: e.dma_start(out=dst, in_=src_ap),
                  w=[("ring", i % 3)], dma=f"ring{i % 3}")

        def wslab(src_ap, view):
            i = ring_state["i"]
            ring_state["i"] += 1
            ring_state["req"].append((src_ap, view))
            plan = ring_state["plan"]
            if plan is None:
                dst = _slab_dst(i, view)
                S.add("pool", lambda e, dst=dst, src_ap=src_ap: e.dma_start(out=dst, in_=src_ap),
                      w=[("ring", i % 3)], dma=f"ring{i % 3}")
            else:
                upto = min(len(plan), i + LOOKAHEAD + 1)
                while ring_state["issued"] < upto:
                    _issue_slab(ring_state["issued"])
                    ring_state["issued"] += 1
            return _slab_dst(i, view), ("ring", i % 3)

        def colslab(wmat, c0, ncols=512):
            return wmat[:, c0:c0 + ncols].rearrange("(c p) n -> p c n", p=128)

        evac_ctr = {"i": 0}

        def evac_copy(dst, src, r, w, scale=None, eng=None):
            if eng is None:
                eng = "act" if evac_ctr["i"] % 2 == 0 else "dve"
                evac_ctr["i"] += 1
            if eng == "act":
                if scale is None:
                    S.add("act", lambda e: e.activation(out=dst, in_=src, func=AF.Copy), r=r, w=w)
                else:
                    S.add("act", lambda e: e.activation(out=dst, in_=src, func=AF.Copy, scale=float(scale)), r=r, w=w)
            else:
                if scale is None:
                    S.add("dve", lambda e: e.tensor_copy(out=dst, in_=src), r=r, w=w)
                else:
                    S.add("dve", lambda e: e.tensor_scalar(out=dst, in0=src, scalar1=float(scale), scalar2=None,
                                                           op0=ALU.mult), r=r, w=w)

        def proj_fm(slab, skey, col0, M, grp, rhs_tile=aT, rhs_key="aT"):
            banks = G[grp]

            def emit(e):
                ins = None
                for c in range(NCH):
                    for ti, (t0, n) in enumerate(TCH):
                        ins = e.matmul(psb[banks[ti]][0:M, 0:n], lhsT=slab[:, c, col0:col0 + M],
                                       rhs=rhs_tile[:, c, t0:t0 + n], start=(c == 0), stop=(c == NCH - 1))
                return ins
            S.add("pe", emit, r=[skey, rhs_key], w=[PSK[b] for b in banks])

        def prologue():
            xv = xT.rearrange("(c p) t -> p c t", p=128)
            for q4 in range(4):
                S.add("sp", lambda e, q4=q4: e.dma_start(out=hT[:, 4 * q4:4 * q4 + 4, :], in_=xv[:, 4 * q4:4 * q4 + 4, :]),
                      w=[("hT", c) for c in range(4 * q4, 4 * q4 + 4)], dma=f"ld_h{q4}")
            S.add("sp", lambda e: e.dma_start(out=gvec[:], in_=gvec_d), w=["gvec"], dma="ld_c0")
            S.add("sp", lambda e: e.dma_start(out=consts[:], in_=consts_d), w=["consts"], dma="ld_c1")
            S.add("sp", lambda e: e.dma_start(out=bfs[:], in_=bf_d), w=["bfs"], dma="ld_c2")
            S.add("sp", lambda e: e.dma_start(out=rks[:], in_=rk_d), w=["rks"], dma="ld_c3")
            S.add("dve", lambda e: e.tensor_scalar(out=negb[:], in0=bfs[:], scalar1=-1.0, scalar2=None, op0=ALU.mult),
                  r=["bfs"], w=["negb"])
            S.add("dve", lambda e: e.memset(ones16[:], 1.0), w=["ones16"])
            S.add("dve", lambda e: e.memset(pcs, 1.0), w=["gate"])
            S.add("sp", lambda e: e.dma_start(out=ka_scr[:, 0:3, :], in_=pcs), r=["gate"], w=["ka_ones"], dma="aug1a")
            S.add("sp", lambda e: e.dma_start(out=qa_scr[:, 3:6, :], in_=pcs[:, :, 0:NT]), r=["gate"], w=["qa_ones"],
                  dma="aug1b")

        HK = [("hT", c) for c in range(NCH)]

        def rms_stats():
            for c in range(NCH):
                sq = sqv[c % 2]
                if c % 2 == 0:
                    S.add("act", lambda e, sq=sq, c=c: e.activation(out=sq, in_=hT[:, c, :], func=AF.Square),
                          r=[("hT", c)], w=[f"sq{c % 2}"])
                else:
                    S.add("dve", lambda e, sq=sq, c=c: e.tensor_tensor(out=sq, in0=hT[:, c, :], in1=hT[:, c, :],
                                                                      op=ALU.mult),
                          r=[("hT", c)], w=[f"sq{c % 2}"])

                def emit(e, sq=sq, c=c):
                    ins = None
                    for ti, (t0, n) in enumerate(TCH):
                        ins = e.matmul(psb[ti][:, 0:n], lhsT=cst(C_ONES), rhs=sq[:, t0:t0 + n],
                                       start=(c == 0), stop=(c == NCH - 1))
                    return ins
                S.add("pe", emit, r=[f"sq{c % 2}", "consts"], w=[PSK[0], PSK[1], PSK[2]])

            def emit_sqrt(e):
                ins = None
                for ti, (t0, n) in enumerate(TCH):
                    ins = e.activation(out=tmpf[:, t0:t0 + n], in_=psb[ti][:, 0:n], func=AF.Sqrt,
                                       bias=EPS, scale=1.0 / D)
                return ins
            S.add("act", emit_sqrt, r=[PSK[0], PSK[1], PSK[2]], w=["tmpf"])
            S.add("dve", lambda e: e.reciprocal(out=rstd, in_=tmpf), r=["tmpf"], w=["rstd"])

        def rms_apply(col):
            for c in range(NCH):
                S.add("dve", lambda e, c=c: e.scalar_tensor_tensor(out=aT[:, c, :], in0=hT[:, c, :],
                                                                  scalar=gcol(col + c), in1=rstd,
                                                                  op0=ALU.mult, op1=ALU.mult),
                      r=[("hT", c), "rstd", "gvec"], w=["aT"])

        def proj_k_group(wmat, c0, g, hctr):
            slab, skey = wslab(colslab(wmat, c0), "col")
            for hh in range(4):
                h = g * 4 + hh
                grp = hctr[0] % 2
                hctr[0] += 1
                proj_fm(slab, skey, hh * 128, 128, grp)
                st = stg[h % 2]
                b = G[grp]
                for ti, (t0, n) in enumerate(TCH):
                    evac_copy(st[:, t0:t0 + n], psb[b[ti]][:, 0:n], r=[PSK[b[ti]]], w=[f"stg{h % 2}"])
                S.add("dve", lambda e, st=st, h=h: e.tensor_copy(out=ktm[:, h, :], in_=st[:, NR:NT]),
                      r=[f"stg{h % 2}"], w=["ktm"])
                S.add("sp", lambda e, st=st, g=g, hh=hh: e.dma_start(out=kt_src[g][hh * 128:(hh + 1) * 128, :],
                                                                     in_=st[:, 0:NR]),
                      r=[f"stg{h % 2}"], w=[("kt_src", g)], dma=f"kts{g}")

        def proj_v_group(wmat, c0, g, vctr):
            slab, skey = wslab(colslab(wmat, c0), "col")
            for tb in range(9):
                M = 128 if tb < 8 else 16
                bank = 6 + (vctr[0] % 2)
                vs = vstg[vctr[0] % 2]
                vk = f"vstg{vctr[0] % 2}"
                vctr[0] += 1

                def emit(e, tb=tb, M=M, bank=bank):
                    ins = None
                    for c in range(NCH):
                        ins = e.matmul(psb[bank][0:M, 0:512], lhsT=aT[:, c, tb * 128:tb * 128 + M],
                                       rhs=slab[:, c, :], start=(c == 0), stop=(c == NCH - 1))
                    return ins
                S.add("pe", emit, r=[skey, "aT"], w=[PSK[bank]])
                if tb < 8:
                    evac_copy(vs, psb[bank][:, 0:512], r=[PSK[bank]], w=[vk])
                    S.add("sp", lambda e, vs=vs, g=g, tb=tb: e.dma_start(out=v_src[g][tb * 128:(tb + 1) * 128, :],
                                                                         in_=vs),
                          r=[vk], w=[("v_src", g)], dma=f"vs{g}")
                else:
                    evac_copy(vm[0:16, g * 512:(g + 1) * 512], psb[bank][0:16, 0:512], r=[PSK[bank]], w=["vm"])

        def gather_kv(g, ktd, vd, kkey, vkey):
            S.add("pool", lambda e: e.collective_compute("AllGather", ALU.bypass, replica_groups=RG,
                                                         ins=[kt_src[g]], outs=[ktd[g]]),
                  r=[("kt_src", g)], w=[(kkey, g)], dma=f"cck{g}", inc=1)
            S.add("pool", lambda e: e.collective_compute("AllGather", ALU.bypass, replica_groups=RG,
                                                         ins=[v_src[g]], outs=[vd[g]]),
                  r=[("v_src", g)], w=[(vkey, g)], dma=f"ccv{g}", inc=1)

        def q_fillers(wmat):
            fl = []
            for g in range(1, 4):
                holder = {}
                for hh in range(4):
                    h = g * 4 + hh
                    for k in range(4):
                        def piece(g=g, hh=hh, k=k, holder=holder):
                            if "slab" not in holder:
                                holder["slab"], holder["skey"] = wslab(colslab(wmat, g * 512), "col")
                            slab, skey = holder["slab"], holder["skey"]

                            def emit(e):
                                ins = None
                                for c in range(4 * k, 4 * k + 4):
                                    for ti, (t0, n) in enumerate(TCH):
                                        ins = e.matmul(psb[5 + ti][:, 0:n], lhsT=slab[:, c, hh * 128:(hh + 1) * 128],
                                                       rhs=aT[:, c, t0:t0 + n], start=(c == 0), stop=(c == NCH - 1))
                                return ins
                            S.add("pe", emit, r=[skey, "aT"], w=[PSK[5], PSK[6], PSK[7]])
                        fl.append((h, piece))

                    def fin(h=h):
                        for ti, (t0, n) in enumerate(TCH):
                            evac_copy(qstg[:, t0:t0 + n], psb[5 + ti][:, 0:n], r=[PSK[5 + ti]], w=["qstg"],
                                      scale=QSCALE, eng="dve")
                        S.add("sp", lambda e: e.dma_start(out=q_scr[h], in_=qstg), r=["qstg"],
                              w=[("q_scr", h)], dma=f"qs{h % 4}")
                    fl.append((h, fin))
            return fl

        def proj_q(wmat, c0base, hctr, groups=(0, 1, 2, 3)):
            for g in groups:
                slab, skey = wslab(colslab(wmat, c0base + g * 512), "col")
                for hh in range(4):
                    h = g * 4 + hh
                    grp = hctr[0] % 2
                    hctr[0] += 1
                    proj_fm(slab, skey, hh * 128, 128, grp)
                    st = stg[h % 2]
                    b = G[grp]
                    for ti, (t0, n) in enumerate(TCH):
                        evac_copy(st[:, t0:t0 + n], psb[b[ti]][:, 0:n], r=[PSK[b[ti]]], w=[f"stg{h % 2}"],
                                  scale=QSCALE)
                    S.add("sp", lambda e, st=st, h=h: e.dma_start(out=q_scr[h], in_=st), r=[f"stg{h % 2}"],
                          w=[("q_scr", h)], dma=f"qs{h % 4}")

        def gate_phase(li, wmat, hctr):
            S.add("pool", lambda e: e.dma_start(out=wf_sb[:].rearrange("p c j -> p (c j)"), in_=wf_d[li]),
                  w=["wf"], dma="wf")
            grp = hctr[0] % 2
            hctr[0] += 1
            proj_fm(wf_sb, "wf", 0, 16, grp)
            b = G[grp]

            def emit_e(e):
                ins = None
                for ti, (t0, n) in enumerate(TCH):
                    ins = e.activation(out=e16[:, t0:t0 + n], in_=psb[b[ti]][0:16, 0:n], func=AF.Exp,
                                       bias=negb[:, li:li + 1], scale=-1.0)
                return ins
            S.add("act", emit_e, r=[PSK[x] for x in b] + ["negb"], w=["e16"])
            S.add("act", lambda e: e.activation(out=nl16, in_=e16, func=AF.Ln, bias=1.0, scale=1.0),
                  r=["e16"], w=["nl16"])
            S.add("sp", lambda e: e.dma_start(out=g_src, in_=nl16[:, 0:NR]), r=["nl16"], w=["g_src"], dma="gs")
            S.add("pool", lambda e: e.collective_compute("AllGather", ALU.bypass, replica_groups=RG,
                                                         ins=[g_src], outs=[g_dst]),
                  r=["g_src"], w=["g_dst"], dma="ccg", inc=1)

        def gate_phase2():
            S.add("sp", lambda e: e.dma_start(out=nl_all[:, 0:2048].rearrange("h (r t) -> h r t", r=2),
                                              in_=g_dst.rearrange("(r h) t -> h r t", r=2)),
                  r=["g_dst"], w=["gate"], dma="gl")
            S.add("dve", lambda e: e.tensor_copy(out=nl_all[:, 2048:2064], in_=nl16[:, NR:NT]),
                  r=["nl16", "gate"], w=["gate"])
            S.add("dve", lambda e: e.tensor_tensor_scan(out=csv[:, 2048:2064], data0=ones16[:, 0:16],
                                                        data1=nl_all[:, 2048:2064], initial=0.0,
                                                        op0=ALU.mult, op1=ALU.add),
                  r=["gate", "ones16"], w=["gate"])
            prev = csv[:, 2063:2064]
            for gk in range(16):
                r_, lb = gk % 2, gk // 2
                o = r_ * 1024 + lb * 128
                S.add("dve", lambda e, o=o, prev=prev: e.tensor_tensor_scan(out=csv[:, o:o + 128], data0=ones16[:],
                                                                            data1=nl_all[:, o:o + 128], initial=prev,
                                                                            op0=ALU.mult, op1=ALU.add),
                      r=["gate", "ones16"], w=["gate"])
                prev = csv[:, o + 127:o + 128]
            gk_ = dict(r=["gate"], w=["gate"])
            S.add("dve", lambda e: e.tensor_copy(out=pcs[:, 0, :], in_=csv), **gk_)
            S.add("dve", lambda e: e.tensor_tensor(out=nl_all, in0=csv, in1=pcs[:, 0, :], op=ALU.subtract), **gk_)
            S.add("dve", lambda e: e.tensor_copy(out=pcs[:, 1, :], in_=nl_all), **gk_)
            S.add("dve", lambda e: e.tensor_tensor(out=csv, in0=nl_all, in1=pcs[:, 1, :], op=ALU.subtract), **gk_)
            S.add("dve", lambda e: e.tensor_copy(out=pcs[:, 2, :], in_=csv), **gk_)
            for j in range(3):
                S.add("dve", lambda e, j=j: e.tensor_tensor(out=nl_all[:, 0:NR], in0=pcs[:, j, 0:NR],
                                                            in1=pcs[:, j, NR:2 * NR], op=ALU.subtract), **gk_)
                S.add("dve", lambda e, j=j: e.scalar_tensor_tensor(out=qpc[:, j, 0:NR], in0=nl_all[:, 0:NR],
                                                                   scalar=rks[:, 0:1], in1=pcs[:, j, 0:NR],
                                                                   op0=ALU.mult, op1=ALU.subtract),
                      r=["gate", "rks"], w=["qpc"])
                S.add("dve", lambda e, j=j: e.tensor_scalar(out=qpc[:, j, NR:NT], in0=pcs[:, j, 2048:2064],
                                                            scalar1=-1.0, scalar2=None, op0=ALU.mult),
                      r=["gate"], w=["qpc"])
            S.add("sp", lambda e: e.dma_start(out=ka_scr[:, 3:6, :], in_=pcs), r=["gate"], w=["ka_scr"], dma="aug")
            S.add("sp", lambda e: e.dma_start(out=qa_scr[:, 0:3, :], in_=qpc), r=["qpc"], w=["qa_scr"], dma="aug")

        def load_head(h, fox, ktd, vd, kkey, vkey):
            g, hh = h // 4, h % 4
            A = (att if fox else attS)[h % 2]
            i = h % 2
            pre = "att" if fox else "attS"
            S.add("sp", lambda e: e.dma_start(out=A["qt"], in_=q_scr[h]), r=[("q_scr", h)], w=[f"{pre}{i}qt"],
                  dma=f"lq{i}")
            S.add("sp", lambda e: e.dma_start(
                out=A["kt"].rearrange("p (r t) -> p r t", r=2),
                in_=ktd[g].rearrange("(r x) t -> x r t", r=2)[hh * 128:(hh + 1) * 128]),
                r=[(kkey, g)], w=[f"{pre}{i}kt"], dma=f"lk{i}")
            S.add("sp", lambda e: e.dma_start(
                out=A["vv"],
                in_=vd[g].rearrange("(b p) n -> p b n", p=128)[:, :, hh * 128:(hh + 1) * 128]),
                r=[(vkey, g)], w=[f"{pre}{i}vv"], dma=f"lv{i}")
            if fox:
                S.add("sp", lambda e: e.dma_start(out=A["qa"], in_=qa_scr[h]), r=["qa_scr", "qa_ones"],
                      w=[f"att{i}qa"], dma=f"lqa{i}")
                S.add("sp", lambda e: e.dma_start(out=A["ka"], in_=ka_scr[h]), r=["ka_scr", "ka_ones"],
                      w=[f"att{i}ka"], dma=f"lka{i}")

        sctr = [0]
        pctr = [0]
        octr = [0]
        lctr = [0]
        ectr = [0]

        def run_pipeline(items, nstages, lag, tick=None, every=5):
            n = len(items)
            for s in range(n + lag * (nstages - 1)):
                for k in range(nstages):
                    t = s - k * lag
                    if 0 <= t < n:
                        it = items[t]
                        it["stages"][k]()
                        if k == nstages - 1 and it.get("post"):
                            it["post"]()
                if tick is not None and s % every == every - 1:
                    tick()

        def fox_items(h, items, after_head):
            A = att[h % 2]
            i = h % 2
            qt, kt, vv, qa, ka = A["qt"], A["kt"], A["vv"], A["qa"], A["ka"]
            rk_ = [f"att{i}qt", f"att{i}kt", f"att{i}qa", f"att{i}ka", "consts", "ktm"]
            os_ = ostg[h % 2]
            osk = f"stg{h % 2}"

            def tile(kl, kal, vl, krows, c0, a0, N, neg, first, last, ob, db, post=None):
                sb_ = sctr[0] % 3
                sctr[0] += 1
                pi = pctr[0] % 3
                pctr[0] += 1
                P = pt[pi]

                def emit_s(e):
                    e.matmul(psb[sb_][0:krows, 0:N], lhsT=kl, rhs=qt[:, c0 + a0:c0 + a0 + N], start=True, stop=False)
                    ins = e.matmul(psb[sb_][0:krows, 0:N], lhsT=kal, rhs=qa[:, c0 + a0:c0 + a0 + N],
                                   start=False, stop=(neg is None))
                    if neg is not None:
                        ncols = min(128, N)
                        ins = e.matmul(psb[sb_][0:krows, 0:ncols], lhsT=cst(C_IDENT, krows, krows),
                                       rhs=neg[0:krows, 0:ncols], start=False, stop=True)
                    return ins

                def emit_o(e):
                    e.matmul(psb[ob][:, a0:a0 + N], lhsT=vl, rhs=P[0:krows, 0:N], start=first, stop=last)
                    return e.matmul(psb[db][:, a0:a0 + N], lhsT=cst(C_ONES, krows, 128), rhs=P[0:krows, 0:N],
                                    start=first, stop=last)

                def stA():
                    S.add("pe", emit_s, r=rk_, w=[PSK[sb_]])
                    S.add("act", lambda e: e.activation(out=P[0:krows, 0:N], in_=psb[sb_][0:krows, 0:N],
                                                        func=AF.Exp), r=[PSK[sb_]], w=[f"pt{pi}"])

                def stB():
                    S.add("pe", emit_o, r=[f"pt{pi}", f"att{i}vv", "vm", "consts"], w=[PSK[ob], PSK[db]])
                items.append(dict(stages=[stA, stB], post=post))

            def finish(c0, n, ob, db):
                S.add("dve", lambda e: e.reciprocal(out=rdv[:, 0:n], in_=psb[db][:, 0:n]), r=[PSK[db]], w=["rdv"])
                S.add("dve", lambda e: e.tensor_tensor(out=os_[:, c0:c0 + n], in0=psb[ob][:, 0:n], in1=rdv[:, 0:n],
                                                       op=ALU.mult), r=[PSK[ob], "rdv"], w=[osk])

            for qtile in range(2):
                c0 = qtile * 512
                lb0 = qtile * 4
                ob = 3 + (octr[0] % 2)
                db = 5 + (octr[0] % 2)
                octr[0] += 1
                tile(ktm[:, h, :], ka[:, 2048:2064], vm[0:16, h * 128:(h + 1) * 128], 16, c0, 0, 512, None,
                     True, False, ob, db)
                nblk = lb0 + 4
                for lbk in range(nblk):
                    for r_ in range(2):
                        j0 = max(lbk, lb0)
                        a0 = (j0 - lb0) * 128
                        N = 512 - a0
                        kc = r_ * 1024 + lbk * 128
                        neg = cst(C_FNEG0 + r_) if lbk >= lb0 else None
                        lastt = (lbk == nblk - 1 and r_ == 1)
                        post = (lambda c0=c0, ob=ob, db=db: finish(c0, 512, ob, db)) if lastt else None
                        tile(kt[:, kc:kc + 128], ka[:, kc:kc + 128], vv[:, r_ * 8 + lbk, :], 128, c0, a0, N, neg,
                             False, lastt, ob, db, post)
            ob = 3 + (octr[0] % 2)
            db = 5 + (octr[0] % 2)
            octr[0] += 1

            def post_head(ob=ob, db=db):
                finish(NR, 16, ob, db)
                S.add("sp", lambda e: e.dma_start(out=o_scr[h], in_=os_), r=[osk], w=["o_scr"], dma="os")
                after_head(h)
            tile(ktm[:, h, :], ka[:, 2048:2064], vm[0:16, h * 128:(h + 1) * 128], 16, NR, 0, 16, cst(C_CNEG),
                 True, True, ob, db, post_head)

        def sb_items(h, items, after_head):
            A = attS[h % 2]
            i = h % 2
            qt, kt, vv = A["qt"], A["kt"], A["vv"]
            rk_ = [f"attS{i}qt", f"attS{i}kt", "consts", "ktm"]
            os_ = ostg[h % 2]
            osk = f"stg{h % 2}"

            def tile(kl, vl, krows, c0, a0, N, neg, lsum_cols, ob, last, zero_n=None, post=None):
                zb = sctr[0] % 3
                sctr[0] += 1
                pi = pctr[0] % 3
                pctr[0] += 1
                li_ = lctr[0] % 2
                lctr[0] += 1
                ei = ectr[0] % 2
                ectr[0] += 1
                P, L, E = pt[pi], l16[li_], ef[ei]

                def emit_z(e):
                    ins = e.matmul(psb[zb][0:krows, 0:N], lhsT=kl, rhs=qt[:, c0 + a0:c0 + a0 + N],
                                   start=True, stop=False)
                    if neg is not None:
                        ncols = min(128, N)
                        ins = e.matmul(psb[zb][0:krows, 0:ncols], lhsT=cst(C_IDENT, krows, krows),
                                       rhs=neg[0:krows, 0:ncols], start=False, stop=False)
                    return ins

                def emit_a(e):
                    have = lsum_cols is not None
                    ins = e.matmul(psb[zb][0:krows, 0:N], lhsT=cst(C_NTRI, krows, krows), rhs=L[0:krows, 0:N],
                                   start=False, stop=not have)
                    if have:
                        x0, n = lsum_cols
                        ins = e.matmul(psb[zb][0:krows, x0 - a0:x0 - a0 + n], lhsT=cst(C_NONES, 128, krows),
                                       rhs=lsum[:, x0:x0 + n], start=False, stop=True)
                    return ins

                def emit_l(e):
                    ins = None
                    if lsum_cols is None or lsum_cols[0] > a0:
                        ins = e.tensor_copy(out=lsum[:, a0:a0 + 128], in_=L[:, 0:128])
                    if lsum_cols is not None:
                        x0, n = lsum_cols
                        ins = e.tensor_tensor(out=lsum[:, x0:x0 + n], in0=lsum[:, x0:x0 + n],
                                              in1=L[:, x0 - a0:x0 - a0 + n], op=ALU.add)
                    return ins

                def emit_o(e):
                    if zero_n is not None:
                        e.matmul(psb[ob][:, 0:zero_n], lhsT=cst(C_ZERO), rhs=qt[:, c0:c0 + zero_n],
                                 start=True, stop=False, skip_group_check=True)
                    return e.matmul(psb[ob][:, a0:a0 + N], lhsT=vl, rhs=P[0:krows, 0:N],
                                    start=False, stop=last, skip_group_check=True)

                def stA():
                    S.add("pe", emit_z, r=rk_, w=[PSK[zb]])
                    S.add("act", lambda e: e.activation(out=E[0:krows, 0:N], in_=psb[zb][0:krows, 0:N],
                                                        func=AF.Exp), r=[PSK[zb]], w=[f"ef{ei}"])
                    S.add("act", lambda e: e.activation(out=L[0:krows, 0:N], in_=E[0:krows, 0:N], func=AF.Ln,
                                                        bias=1.0, scale=1.0), r=[f"ef{ei}"], w=[f"l16{li_}"])

                def stB():
                    S.add("pe", emit_a, r=[f"l16{li_}", "lsum", "consts"], w=[PSK[zb]])
                    if krows == 128 and not last:
                        S.add("dve", emit_l, r=[f"l16{li_}"], w=["lsum"])
                    S.add("act", lambda e: e.activation(out=P[0:krows, 0:N], in_=psb[zb][0:krows, 0:N],
                                                        func=AF.Exp), r=[PSK[zb]], w=[f"pt{pi}"])

                def stC():
                    S.add("pe", emit_o, r=[f"pt{pi}", f"attS{i}vv", f"attS{i}qt", "vm", "consts"], w=[PSK[ob]])
                items.append(dict(stages=[stA, stB, stC], post=post))

            for qtile in range(2):
                c0 = qtile * 512
                lb0 = qtile * 4
                ob = 3 + (octr[0] % 2)
                octr[0] += 1
                have_from = None
                firstt = True
                for lbk in range(lb0 + 3, -1, -1):
                    for r_ in (1, 0):
                        j0 = max(lbk, lb0)
                        a0 = (j0 - lb0) * 128
                        N = 512 - a0
                        kc = r_ * 1024 + lbk * 128
                        neg = cst(C_SNEG0 + r_) if lbk >= lb0 else None
                        lc = None if have_from is None else (have_from, 512 - have_from)
                        tile(kt[:, kc:kc + 128], vv[:, r_ * 8 + lbk, :], 128, c0, a0, N, neg, lc, ob, False,
                             zero_n=(512 if firstt else None))
                        firstt = False
                        have_from = a0

                def post_q(ob=ob, c0=c0):
                    S.add("dve", lambda e: e.tensor_copy(out=os_[:, c0:c0 + 512], in_=psb[ob][:, 0:512]),
                          r=[PSK[ob]], w=[osk])
                tile(ktm[:, h, :], vm[0:16, h * 128:(h + 1) * 128], 16, c0, 0, 512, None, (0, 512), ob, True,
                     post=post_q)
            ob = 3 + (octr[0] % 2)
            octr[0] += 1

            def post_head(ob=ob):
                S.add("dve", lambda e: e.tensor_copy(out=os_[:, NR:NT], in_=psb[ob][:, 0:16]), r=[PSK[ob]], w=[osk])
                S.add("sp", lambda e: e.dma_start(out=o_scr[h], in_=os_), r=[osk], w=["o_scr"], dma="os")
                after_head(h)
            tile(ktm[:, h, :], vm[0:16, h * 128:(h + 1) * 128], 16, NR, 0, 16, cst(C_SCNEG), None, ob, True,
                 zero_n=16, post=post_head)

        def attention(fox, ktd, vd, kkey, vkey, wq=None):
            load_head(0, fox, ktd, vd, kkey, vkey)
            load_head(1, fox, ktd, vd, kkey, vkey)
            fillers = q_fillers(wq) if wq is not None else []
            fstate = [0]

            def flush(hmax):
                while fstate[0] < len(fillers) and fillers[fstate[0]][0] <= hmax:
                    fillers[fstate[0]][1]()
                    fstate[0] += 1

            def tick():
                if fstate[0] < len(fillers):
                    fillers[fstate[0]][1]()
                    fstate[0] += 1

            def after_head(h):
                if h + 2 < NH:
                    flush(h + 2)
                    load_head(h + 2, fox, ktd, vd, kkey, vkey)
            items = []
            for h in range(NH):
                if fox:
                    fox_items(h, items, after_head)
                else:
                    sb_items(h, items, after_head)
            if fox:
                run_pipeline(items, 2, 2)
            else:
                run_pipeline(items, 3, 1, tick=tick if fillers else None, every=5)
                flush(NH)

        def add_to_h(n, grp):
            b = G[grp]
            for ti, (t0, nn) in enumerate(TCH):
                S.add("dve", lambda e, ti=ti, t0=t0, nn=nn: e.tensor_tensor(out=hT[:, n, t0:t0 + nn],
                                                                           in0=psb[b[ti]][:, 0:nn],
                                                                           in1=hT[:, n, t0:t0 + nn], op=ALU.add),
                      r=[PSK[b[ti]], ("hT", n)], w=[("hT", n)])

        def oproj(wmat, hctr):
            S.add("sp", lambda e: e.dma_start(out=aT, in_=o_scr.rearrange("h p t -> p h t")), r=["o_scr"], w=["aT"],
                  dma="lo")
            for s in range(4):
                slab, skey = wslab(colslab(wmat, s * 512), "col")
                for nn in range(4):
                    n = s * 4 + nn
                    grp = hctr[0] % 2
                    hctr[0] += 1
                    proj_fm(slab, skey, nn * 128, 128, grp)
                    add_to_h(n, grp)

        def mlp_up(li, fg, hctr):
            slab, skey = wslab(colslab(w_up[li], fg * 512), "col")
            gt = gT[fg % 2]
            gk = f"gT{fg % 2}"

            def up_chunk(fc):
                grp = hctr[0] % 2
                hctr[0] += 1
                proj_fm(slab, skey, fc * 128, 128, grp)
                b = G[grp]

                def evac(ti, t0, n):
                    rt = rtmp[ti % 2]
                    rkk = f"rtmp{ti % 2}"
                    S.add("act", lambda e: e.activation(out=rt[:, 0:n], in_=psb[b[ti]][:, 0:n], func=AF.Relu),
                          r=[PSK[b[ti]]], w=[rkk])
                    S.add("act", lambda e: e.activation(out=gt[:, fc, t0:t0 + n], in_=rt[:, 0:n], func=AF.Square),
                          r=[rkk], w=[gk])
                for ti, (t0, n) in enumerate(TCH):
                    evac(ti, t0, n)
            for fc in range(4):
                up_chunk(fc)

        def mlp_down(li, fg, hctr):
            gt = gT[fg % 2]
            gk = f"gT{fg % 2}"
            dslab, dkey = wslab(w_down[li, fg * 512:(fg + 1) * 512, :].rearrange("(c p) n -> p c n", p=128), "row")

            def down_chunk(n):
                grp = hctr[0] % 2
                hctr[0] += 1
                b = G[grp]

                def emit(e):
                    ins = None
                    for fc in range(4):
                        for ti, (t0, nn) in enumerate(TCH):
                            ins = e.matmul(psb[b[ti]][:, 0:nn], lhsT=dslab[:, fc, n * 128:(n + 1) * 128],
                                           rhs=gt[:, fc, t0:t0 + nn], start=(fc == 0), stop=(fc == 3))
                    return ins
                S.add("pe", emit, r=[dkey, gk], w=[PSK[x] for x in b])
                add_to_h(n, grp)
            for n in range(NCH):
                down_chunk(n)

        def mlp(li, hctr):
            pend = None
            for fg in range(16):
                mlp_up(li, fg, hctr)
                if pend is not None:
                    mlp_down(li, pend, hctr)
                pend = fg
            mlp_down(li, pend, hctr)

        def program():
            prologue()
            hctr = [0]
            vctr = [0]
            for li in range(nlayers):
                rms_stats()
                if li < 2:
                    wmat = fox_w_in[li]
                    rms_apply(li * 16)
                    for g in range(4):
                        proj_k_group(wmat, D + g * 512, g, hctr)
                        proj_v_group(wmat, 2 * D + g * 512, g, vctr)
                        gather_kv(g, kt_dst, v_dst, "kt_dst", "v_dst")
                    gate_phase(li, wmat, hctr)
                    gate_phase2()
                    proj_q(wmat, 0, hctr)
                    attention(True, kt_dst, v_dst, "kt_dst", "v_dst")
                    oproj(fox_w_o[li], hctr)
                else:
                    if li == 2:
                        rms_apply(128)
                        for g in range(4):
                            proj_k_group(w_kv, g * 512, g, hctr)
                            proj_v_group(w_kv, D + g * 512, g, vctr)
                            gather_kv(g, skt_dst, sv_dst, "skt_dst", "sv_dst")
                    rms_apply(li * 16)
                    proj_q(sb_w_q[li - 2], 0, hctr, groups=(0,))
                    attention(False, skt_dst, sv_dst, "skt_dst", "sv_dst", wq=sb_w_q[li - 2])
                    oproj(sb_w_o[li - 2], hctr)
                rms_stats()
                rms_apply(64 + li * 16)
                mlp(li, hctr)

            rms_stats()
            outs = []
            for c in range(NCH):
                fs = fstg[c % 2]
                S.add("dve", lambda e, c=c, fs=fs: e.scalar_tensor_tensor(out=fs, in0=hT[:, c, 0:NR],
                                                                          scalar=gcol(144 + c), in1=rstd[:, 0:NR],
                                                                          op0=ALU.mult, op1=ALU.mult),
                      r=[("hT", c), "rstd", "gvec"], w=[f"fstg{c % 2}"])
                S.add("sp", lambda e, c=c, fs=fs: e.dma_start(out=yT[c * 128:(c + 1) * 128, :], in_=fs),
                      r=[f"fstg{c % 2}"], w=[("yT", c)], dma="st_y")
            S.add("sp", None, r=[("yT", c) for c in range(NCH)])


        def reset_counters():
            ring_state.update(i=0, issued=0, req=[])
            for ctr in (sctr, pctr, octr, lctr, ectr):
                ctr[0] = 0
            evac_ctr["i"] = 0

        S_real = S
        S = Sched()
        S.alias, S.regions = S_real.alias, S_real.regions
        reset_counters()
        program()
        ring_state["plan"] = list(ring_state["req"])
        S = Sched()
        S.alias, S.regions = S_real.alias, S_real.regions
        reset_counters()
        program()

        S.finalize()
        sems = {}
        for idx, k in enumerate(sorted(S.semkeys, key=str)):
            sems[k] = es.enter_context(nc.semaphore(f"s{idx}"))
        with nc.Block() as block:
            @block.tensor
            def _(e):
                S.emit_engine("pe", e, sems)

            @block.scalar
            def _(e):
                S.emit_engine("act", e, sems)

            @block.vector
            def _(e):
                S.emit_engine("dve", e, sems)

            @block.gpsimd
            def _(e):
                S.emit_engine("pool", e, sems)

            @block.sync
            def _(e):
                S.emit_engine("sp", e, sems)
    return nc


def make_consts(rank):
    i = np.arange(128)[:, None]
    j = np.arange(128)[None, :]
    ones = np.ones((128, 128), np.float32)
    zeros = np.zeros((128, 128), np.float32)
    ntri = np.where(i >= j, -1.0, 0.0).astype(np.float32)
    ident = np.eye(128, dtype=np.float32)
    cneg = np.where(i <= j, 0.0, NEGV).astype(np.float32)
    scneg = np.where(i < j, 0.0, NEGV).astype(np.float32)
    allneg = np.full((128, 128), NEGV, np.float32)
    if rank == 0:
        fneg0, fneg1, sneg0, sneg1 = cneg, allneg, scneg, allneg
    else:
        fneg0, fneg1, sneg0, sneg1 = zeros, cneg, zeros, scneg
    blocks = [ones, ntri, -ones, ident, zeros, fneg0, fneg1, sneg0, sneg1, cneg, scneg]
    return np.concatenate(blocks, axis=1).astype(ml_dtypes.bfloat16)


def prep_inputs(x, meta_tokens, norm_attn, norm_mlp, w_up, w_down, fox_w_in, fox_b_f, fox_w_o,
                kv_norm, w_kv, sb_w_q, sb_w_o, final_norm):
    f = lambda a: np.ascontiguousarray(np.asarray(a, dtype=np.float32))
    x = f(x)
    meta = f(meta_tokens)
    gv = np.zeros((128, 160), np.float32)
    na, nm = f(norm_attn), f(norm_mlp)
    for l in range(DEPTH):
        gv[:, l * 16:(l + 1) * 16] = na[l].reshape(16, 128).T
        gv[:, 64 + l * 16:64 + (l + 1) * 16] = nm[l].reshape(16, 128).T
    gv[:, 128:144] = f(kv_norm).reshape(16, 128).T
    gv[:, 144:160] = f(final_norm).reshape(16, 128).T
    shared = dict(w_up=f(w_up), w_down=f(w_down), fox_w_in=f(fox_w_in), fox_w_o=f(fox_w_o), w_kv=f(w_kv),
                  sb_w_q=f(sb_w_q), sb_w_o=f(sb_w_o), bf=np.ascontiguousarray(f(fox_b_f).T))
    fwi = shared["fox_w_in"]
    shared["wf"] = np.ascontiguousarray(
        fwi[:, :, 3 * D:].reshape(2, NCH, 128, NH).transpose(0, 2, 1, 3).reshape(2, 128, NCH * NH))
    in_maps = []
    for c in range(8):
        b, r = c // 2, c % 2
        xb = x[b].reshape(16, 128, D)[r::2].reshape(NR, D)
        xt = np.ascontiguousarray(np.concatenate([xb, meta], axis=0).T)
        m = dict(shared)
        m["xT"] = xt
        m["gvec"] = gv
        m["rk"] = np.full((16, 1), float(r), np.float32)
        m["consts"] = make_consts(r)
        in_maps.append(m)
    return in_maps


def assemble(results):
    out = np.zeros((BATCH, SEQ, D), np.float32)
    for c in range(8):
        b, r = c // 2, c % 2
        y = np.asarray(results[c]["yT"], dtype=np.float32).T
        out[b].reshape(16, 128, D)[r::2] = y.reshape(8, 128, D)
    return out


_NC_CACHE = {}


def kernel(**inputs):
    in_maps = prep_inputs(**inputs)
    if "nc" not in _NC_CACHE:
        _NC_CACHE["nc"] = build(DEPTH)
    res = run_bass_kernel_spmd(_NC_CACHE["nc"], in_maps, core_ids=list(range(8)))
    return assemble(res.results)
```

```python
from contextlib import ExitStack

import numpy as np
import ml_dtypes
import concourse.bass as bass
import concourse.mybir as mybir
from concourse.bass_utils import run_bass_kernel_spmd

F32 = mybir.dt.float32
BF16 = mybir.dt.bfloat16
AF = mybir.ActivationFunctionType
ALU = mybir.AluOpType

D = 2048
NH = 16
DH = 128
DFF = 8192
NMETA = 16
SEQ = 2048
BATCH = 4
DEPTH = 4
NCH = 16
NT = 1040
NR = 1024
TCH = [(0, 347), (347, 347), (694, 346)]
EPS = 1e-6
QSCALE = DH ** -0.5
NEGV = -30000.0
RG = [[0, 1], [2, 3], [4, 5], [6, 7]]
EPOCH = 30000
ENGS = ["pe", "act", "dve", "pool", "sp"]

C_ONES, C_NTRI, C_NONES, C_IDENT, C_ZERO, C_FNEG0, C_FNEG1, C_SNEG0, C_SNEG1, C_CNEG, C_SCNEG = range(11)
NCONST = 11


class Sched:
    def __init__(self):
        self.ops = []
        self.lastw = {}
        self.readers = {}
        self.alias = {}

    def region(self, key, arena, start, end):
        self.regions = getattr(self, "regions", {})
        for k2, (a2, s2, e2) in self.regions.items():
            if a2 == arena and s2 < end and start < e2 and k2 != key:
                self.alias.setdefault(key, set()).add(k2)
                self.alias.setdefault(k2, set()).add(key)
        self.regions[key] = (arena, start, end)

    def _keys(self, k):
        return [k] + list(self.alias.get(k, ()))

    def add(self, eng, emit, r=(), w=(), dma=None, inc=16):
        i = len(self.ops)
        hard = set()
        soft = set()
        for k in r:
            for kk in self._keys(k):
                j = self.lastw.get(kk)
                if j is not None:
                    hard.add(j)
        for k in w:
            for kk in self._keys(k):
                j = self.lastw.get(kk)
                if j is not None:
                    hard.add(j)
                for j in self.readers.get(kk, ()):
                    soft.add(j)
        self.ops.append(dict(eng=eng, emit=emit, hard=hard, soft=soft - hard, dma=dma, inc=inc))
        for k in r:
            self.readers.setdefault(k, []).append(i)
        for k in w:
            self.lastw[k] = i
            self.readers[k] = []
        return i

    def finalize(self):
        ops = self.ops
        self.eng_ops = {e: [] for e in ENGS}
        for i, o in enumerate(ops):
            self.eng_ops[o["eng"]].append(i)
        import bisect
        by_key = {}
        for i, o in enumerate(ops):
            if o["dma"] is not None:
                by_key.setdefault(o["dma"], []).append(i)
        for i, o in enumerate(ops):
            deps = {}
            for j in o["hard"] | o["soft"]:
                pj = ops[j]
                is_soft = j not in o["hard"]
                if pj["dma"] is None:
                    if pj["eng"] == o["eng"] and o["dma"] is None:
                        if o["eng"] == "pe" or is_soft:
                            continue
                    key = ("c", pj["eng"])
                else:
                    key = ("d", pj["dma"])
                    lst = by_key[pj["dma"]]
                    j = lst[bisect.bisect_left(lst, i) - 1]
                if key not in deps or deps[key] < j:
                    deps[key] = j
            o["deps"] = sorted(deps.values())
            for j in o["deps"]:
                ops[j]["needed"] = True
        cnt = {}
        dcnt = {}
        self.semkeys = set()
        for o in ops:
            if o["dma"] is None:
                if o.get("needed"):
                    e = o["eng"]
                    cnt[e] = cnt.get(e, 0) + 1
                    epoch, val = divmod(cnt[e] - 1, EPOCH)
                    o["tok"] = (("c", e, epoch), val + 1)
                    self.semkeys.add(o["tok"][0])
            else:
                k = o["dma"]
                dcnt[k] = dcnt.get(k, 0) + o["inc"]
                o["tok"] = (("d", k), dcnt[k])
                self.semkeys.add(o["tok"][0])

    def emit_engine(self, e, engobj, sems):
        ops = self.ops
        waited = {}
        for i in self.eng_ops[e]:
            o = ops[i]
            for j in o["deps"]:
                semkey, val = ops[j]["tok"]
                if waited.get(semkey, 0) < val:
                    engobj.wait_ge(sems[semkey], val)
                    waited[semkey] = val
            if o["emit"] is not None:
                ins = o["emit"](engobj)
                if "tok" in o:
                    ins.then_inc(sems[o["tok"][0]], o["inc"] if o["dma"] is not None else 1)


def build(nlayers=DEPTH):
    nc = bass.Bass("TRN2", target_bir_lowering=False)
    S = Sched()

    def din(name, shape, dt=F32):
        return nc.dram_tensor(name, list(shape), dt, kind="ExternalInput").ap()

    xT = din("xT", [D, NT])
    gvec_d = din("gvec", [128, 160])
    bf_d = din("bf", [16, 2])
    rk_d = din("rk", [16, 1])
    consts_d = din("consts", [128, NCONST * 128], BF16)
    w_up = din("w_up", [DEPTH, D, DFF])
    w_down = din("w_down", [DEPTH, DFF, D])
    fox_w_in = din("fox_w_in", [2, D, 3 * D + NH])
    wf_d = din("wf", [2, 128, 256])
    fox_w_o = din("fox_w_o", [2, D, D])
    w_kv = din("w_kv", [D, 2 * D])
    sb_w_q = din("sb_w_q", [2, D, D])
    sb_w_o = din("sb_w_o", [2, D, D])
    yT = nc.dram_tensor("yT", [D, NR], F32, kind="ExternalOutput").ap()

    def dscr(name, shape, dt=BF16):
        return nc.dram_tensor(name, list(shape), dt, kind="Internal").ap()

    q_scr = dscr("q_scr", [NH, 128, NT])
    o_scr = dscr("o_scr", [NH, 128, NT])
    kt_src = [dscr(f"kt_src{g}", [512, NR]) for g in range(4)]
    v_src = [dscr(f"v_src{g}", [NR, 512]) for g in range(4)]
    kt_dst = [dscr(f"kt_dst{g}", [1024, NR]) for g in range(4)]
    v_dst = [dscr(f"v_dst{g}", [2 * NR, 512]) for g in range(4)]
    skt_dst = [dscr(f"skt_dst{g}", [1024, NR]) for g in range(4)]
    sv_dst = [dscr(f"sv_dst{g}", [2 * NR, 512]) for g in range(4)]
    g_src = dscr("g_src", [16, NR], F32)
    g_dst = dscr("g_dst", [32, NR], F32)
    ka_scr = dscr("ka_scr", [NH, 6, 2064])
    qa_scr = dscr("qa_scr", [NH, 6, NT])

    es = ExitStack()
    with es:
        def sb(name, shape, dt):
            return es.enter_context(nc.sbuf_tensor(name, list(shape), dt))

        hT = sb("hT", [128, NCH, NT], F32)
        arenaB = sb("arenaB", [128, NCH * NT], BF16)
        ring = [sb(f"ring{i}", [128, 8192], BF16) for i in range(3)]
        TRB = 22528
        arenaT = sb("arenaT", [128, TRB // 2], BF16)
        arenaC = sb("arenaC", [128, 28896 // 2], BF16)
        gvec = sb("gvec_sb", [128, 160], F32)
        consts = sb("consts_sb", [128, NCONST * 128], BF16)
        bfs = sb("bf_sb", [16, 2], F32)
        negb = sb("negb_sb", [16, 2], F32)
        rks = sb("rk_sb", [16, 1], F32)
        ones16 = sb("ones16", [16, 128], F32)
        ktm = sb("ktm", [128, NH, 16], BF16)
        vm = sb("vm", [16, D], BF16)
        wf_sb = sb("wf_sb", [128, NCH, 16], BF16)
        psb = [es.enter_context(nc.psum_tensor(f"ps{i}", [128, 512], F32)) for i in range(8)]

        def carve(arena, off, shape, dt, parts=128):
            esz = 2 if dt == BF16 else 4
            n = int(np.prod(shape))
            a = arena[0:parts, off // 2: off // 2 + n * esz // 2]
            if dt == F32:
                a = a.bitcast(F32)
            if len(shape) == 2:
                a = a.rearrange("p (a b) -> p a b", a=shape[0])
            return a

        def V(key, arena, aname, off, shape, dt, parts=128):
            esz = 2 if dt == BF16 else 4
            S.region(key, aname, off, off + int(np.prod(shape)) * esz)
            return carve(arena, off, shape, dt, parts)

        aT = V("aT", arenaB, "B", 0, [NCH, NT], BF16)

        def cst(idx, rows=128, cols=128):
            return consts[0:rows, idx * 128: idx * 128 + cols]

        def gcol(col):
            return gvec[:, col:col + 1]

        PSK = [("ps", i) for i in range(8)]
        G = [(0, 1, 2), (3, 4, 5)]

        sqv = [V("sq0", arenaT, "T", 0, [NT], BF16), V("sq1", arenaT, "T", 2080, [NT], BF16)]
        tmpf = V("tmpf", arenaT, "T", 4160, [NT], F32)
        rstd = V("rstd", arenaT, "T", 8320, [NT], F32)
        stg = [V("stg0", arenaT, "T", 12480, [NT], BF16), V("stg1", arenaT, "T", 14560, [NT], BF16)]
        vstg = [V("vstg0", arenaT, "T", 16640, [512], BF16), V("vstg1", arenaT, "T", 17664, [512], BF16)]
        e16 = V("e16", arenaT, "T", 0, [NT], F32, parts=16)
        nl16 = V("nl16", arenaT, "T", 4160, [NT], F32, parts=16)
        qpc = V("qpc", arenaT, "T", 0, [3, NT], BF16, parts=16)
        pt = [V(f"pt{i}", arenaT, "T", 1024 * i, [512], BF16) for i in range(3)]
        ef = [V(f"ef{i}", arenaT, "T", 3072 + 2048 * i, [512], F32) for i in range(2)]
        l16 = [V(f"l16{i}", arenaT, "T", 7168 + 1024 * i, [512], BF16) for i in range(2)]
        lsum = V("lsum", arenaT, "T", 9216, [512], BF16)
        rdv = V("rdv", arenaT, "T", 10240, [512], F32)
        ostg = stg
        gT = [V("gT0", arenaT, "T", 12480, [4, NT], BF16), V("gT1", arenaT, "T", 0, [4, NT], BF16)]
        rtmp = [V("rtmp0", arenaT, "T", 8320, [512], F32), V("rtmp1", arenaT, "T", 10368, [512], F32)]
        fstg = [V("fstg0", arenaT, "T", 12480, [NR], F32), V("fstg1", arenaT, "T", 16576, [NR], F32)]

        HB = 16480
        att = []
        for i in range(2):
            o = i * HB
            att.append(dict(
                qt=V(f"att{i}qt", arenaB, "B", o, [NT], BF16),
                kt=V(f"att{i}kt", arenaB, "B", o + 2080, [2048], BF16),
                vv=V(f"att{i}vv", arenaB, "B", o + 6176, [16, 128], BF16),
                qa=V(f"att{i}qa", arenaB, "B", o + 10272, [NT], BF16, parts=6),
                ka=V(f"att{i}ka", arenaB, "B", o + 12352, [2064], BF16, parts=6),
            ))
        S.region("gate", "C", 0, 28896)
        nl_all = carve(arenaC, 0, [2064], F32, parts=16)
        csv = carve(arenaC, 8256, [2064], F32, parts=16)
        pcs = carve(arenaC, 16512, [3, 2064], BF16, parts=16)

        ring_state = {"i": 0, "issued": 0, "plan": None, "req": []}
        LOOKAHEAD = 2

        def _slab_dst(i, view):
            slot = i % 3
            if view == "col":
                return ring[slot][:].rearrange("p (c n) -> p c n", c=NCH)
            return ring[slot][:].rearrange("p (c n) -> p c n", c=4)

        def _issue_slab(i):
            src_ap, view = ring_state["plan"][i]
            dst = _slab_dst(i, view)
            S.add("pool", lambda e, dst=dst, src_ap=src_ap: e.dma_start(out=dst, in_=src_ap),
                  w=[("ring", i % 3)], dma=f"ring{i % 3}")

        def wslab(src_ap, view):
            i = ring_state["i"]
            ring_state["i"] += 1
            ring_state["req"].append((src_ap, view))
            plan = ring_state["plan"]
            if plan is None:
                dst = _slab_dst(i, view)
                S.add("pool", lambda e, dst=dst, src_ap=src_ap: e.dma_start(out=dst, in_=src_ap),
                      w=[("ring", i % 3)], dma=f"ring{i % 3}")
            else:
                upto = min(len(plan), i + LOOKAHEAD + 1)
                while ring_state["issued"] < upto:
                    _issue_slab(ring_state["issued"])
                    ring_state["issued"] += 1
            return _slab_dst(i, view), ("ring", i % 3)

        def colslab(wmat, c0, ncols=512):
            return wmat[:, c0:c0 + ncols].rearrange("(c p) n -> p c n", p=128)

        evac_ctr = {"i": 0}

        def evac_copy(dst, src, r, w, scale=None, eng=None):
            if eng is None:
                eng = "act" if evac_ctr["i"] % 2 == 0 else "dve"
                evac_ctr["i"] += 1
            if eng == "act":
                if scale is None:
                    S.add("act", lambda e: e.activation(out=dst, in_=src, func=AF.Copy), r=r, w=w)
                else:
                    S.add("act", lambda e: e.activation(out=dst, in_=src, func=AF.Copy, scale=float(scale)), r=r, w=w)
            else:
                if scale is None:
                    S.add("dve", lambda e: e.tensor_copy(out=dst, in_=src), r=r, w=w)
                else:
                    S.add("dve", lambda e: e.tensor_scalar(out=dst, in0=src, scalar1=float(scale), scalar2=None,
                                                           op0=ALU.mult), r=r, w=w)

        def proj_fm(slab, skey, col0, M, grp, rhs_tile=aT, rhs_key="aT"):
            banks = G[grp]

            def emit(e):
                ins = None
                for c in range(NCH):
                    for ti, (t0, n) in enumerate(TCH):
                        ins = e.matmul(psb[banks[ti]][0:M, 0:n], lhsT=slab[:, c, col0:col0 + M],
                                       rhs=rhs_tile[:, c, t0:t0 + n], start=(c == 0), stop=(c == NCH - 1))
                return ins
            S.add("pe", emit, r=[skey, rhs_key], w=[PSK[b] for b in banks])

        def prologue():
            xv = xT.rearrange("(c p) t -> p c t", p=128)
            for q4 in range(4):
                S.add("sp", lambda e, q4=q4: e.dma_start(out=hT[:, 4 * q4:4 * q4 + 4, :], in_=xv[:, 4 * q4:4 * q4 + 4, :]),
                      w=[("hT", c) for c in range(4 * q4, 4 * q4 + 4)], dma=f"ld_h{q4}")
            S.add("sp", lambda e: e.dma_start(out=gvec[:], in_=gvec_d), w=["gvec"], dma="ld_c0")
            S.add("sp", lambda e: e.dma_start(out=consts[:], in_=consts_d), w=["consts"], dma="ld_c1")
            S.add("sp", lambda e: e.dma_start(out=bfs[:], in_=bf_d), w=["bfs"], dma="ld_c2")
            S.add("sp", lambda e: e.dma_start(out=rks[:], in_=rk_d), w=["rks"], dma="ld_c3")
            S.add("dve", lambda e: e.tensor_scalar(out=negb[:], in0=bfs[:], scalar1=-1.0, scalar2=None, op0=ALU.mult),
                  r=["bfs"], w=["negb"])
            S.add("dve", lambda e: e.memset(ones16[:], 1.0), w=["ones16"])
            S.add("dve", lambda e: e.memset(pcs, 1.0), w=["gate"])
            S.add("sp", lambda e: e.dma_start(out=ka_scr[:, 0:3, :], in_=pcs), r=["gate"], w=["ka_ones"], dma="aug1a")
            S.add("sp", lambda e: e.dma_start(out=qa_scr[:, 3:6, :], in_=pcs[:, :, 0:NT]), r=["gate"], w=["qa_ones"],
                  dma="aug1b")

        HK = [("hT", c) for c in range(NCH)]

        def rms_stats():
            for c in range(NCH):
                sq = sqv[c % 2]
                if c % 2 == 0:
                    S.add("act", lambda e, sq=sq, c=c: e.activation(out=sq, in_=hT[:, c, :], func=AF.Square),
                          r=[("hT", c)], w=[f"sq{c % 2}"])
                else:
                    S.add("dve", lambda e, sq=sq, c=c: e.tensor_tensor(out=sq, in0=hT[:, c, :], in1=hT[:, c, :],
                                                                      op=ALU.mult),
                          r=[("hT", c)], w=[f"sq{c % 2}"])

                def emit(e, sq=sq, c=c):
                    ins = None
                    for ti, (t0, n) in enumerate(TCH):
                        ins = e.matmul(psb[ti][:, 0:n], lhsT=cst(C_ONES), rhs=sq[:, t0:t0 + n],
                                       start=(c == 0), stop=(c == NCH - 1))
                    return ins
                S.add("pe", emit, r=[f"sq{c % 2}", "consts"], w=[PSK[0], PSK[1], PSK[2]])

            def emit_sqrt(e):
                ins = None
                for ti, (t0, n) in enumerate(TCH):
                    ins = e.activation(out=tmpf[:, t0:t0 + n], in_=psb[ti][:, 0:n], func=AF.Sqrt,
                                       bias=EPS, scale=1.0 / D)
                return ins
            S.add("act", emit_sqrt, r=[PSK[0], PSK[1], PSK[2]], w=["tmpf"])
            S.add("dve", lambda e: e.reciprocal(out=rstd, in_=tmpf), r=["tmpf"], w=["rstd"])

        def rms_apply(col):
            for c in range(NCH):
                S.add("dve", lambda e, c=c: e.scalar_tensor_tensor(out=aT[:, c, :], in0=hT[:, c, :],
                                                                  scalar=gcol(col + c), in1=rstd,
                                                                  op0=ALU.mult, op1=ALU.mult),
                      r=[("hT", c), "rstd", "gvec"], w=["aT"])

        def proj_k_group(wmat, c0, g, hctr):
            slab, skey = wslab(colslab(wmat, c0), "col")
            for hh in range(4):
                h = g * 4 + hh
                grp = hctr[0] % 2
                hctr[0] += 1
                proj_fm(slab, skey, hh * 128, 128, grp)
                st = stg[h % 2]
                b = G[grp]
                for ti, (t0, n) in enumerate(TCH):
                    evac_copy(st[:, t0:t0 + n], psb[b[ti]][:, 0:n], r=[PSK[b[ti]]], w=[f"stg{h % 2}"])
                S.add("dve", lambda e, st=st, h=h: e.tensor_copy(out=ktm[:, h, :], in_=st[:, NR:NT]),
                      r=[f"stg{h % 2}"], w=["ktm"])
                S.add("sp", lambda e, st=st, g=g, hh=hh: e.dma_start(out=kt_src[g][hh * 128:(hh + 1) * 128, :],
                                                                     in_=st[:, 0:NR]),
                      r=[f"stg{h % 2}"], w=[("kt_src", g)], dma=f"kts{g}")

        def proj_v_group(wmat, c0, g, vctr):
            slab, skey = wslab(colslab(wmat, c0), "col")
            for tb in range(9):
                M = 128 if tb < 8 else 16
                bank = 6 + (vctr[0] % 2)
                vs = vstg[vctr[0] % 2]
                vk = f"vstg{vctr[0] % 2}"
                vctr[0] += 1

                def emit(e, tb=tb, M=M, bank=bank):
                    ins = None
                    for c in range(NCH):
                        ins = e.matmul(psb[bank][0:M, 0:512], lhsT=aT[:, c, tb * 128:tb * 128 + M],
                                       rhs=slab[:, c, :], start=(c == 0), stop=(c == NCH - 1))
                    return ins
                S.add("pe", emit, r=[skey, "aT"], w=[PSK[bank]])
                if tb < 8:
                    evac_copy(vs, psb[bank][:, 0:512], r=[PSK[bank]], w=[vk])
                    S.add("sp", lambda e, vs=vs, g=g, tb=tb: e.dma_start(out=v_src[g][tb * 128:(tb + 1) * 128, :],
                                                                         in_=vs),
                          r=[vk], w=[("v_src", g)], dma=f"vs{g}")
                else:
                    evac_copy(vm[0:16, g * 512:(g + 1) * 512], psb[bank][0:16, 0:512], r=[PSK[bank]], w=["vm"])

        def gather_kv(g, ktd, vd, kkey, vkey):
            S.add("pool", lambda e: e.collective_compute("AllGather", ALU.bypass, replica_groups=RG,
                                                         ins=[kt_src[g]], outs=[ktd[g]]),
                  r=[("kt_src", g)], w=[(kkey, g)], dma=f"cck{g}", inc=1)
            S.add("pool", lambda e: e.collective_compute("AllGather", ALU.bypass, replica_groups=RG,
                                                         ins=[v_src[g]], outs=[vd[g]]),
                  r=[("v_src", g)], w=[(vkey, g)], dma=f"ccv{g}", inc=1)

        def proj_q(wmat, c0base, hctr):
            for g in range(4):
                slab, skey = wslab(colslab(wmat, c0base + g * 512), "col")
                for hh in range(4):
                    h = g * 4 + hh
                    grp = hctr[0] % 2
                    hctr[0] += 1
                    proj_fm(slab, skey, hh * 128, 128, grp)
                    st = stg[h % 2]
                    b = G[grp]
                    for ti, (t0, n) in enumerate(TCH):
                        evac_copy(st[:, t0:t0 + n], psb[b[ti]][:, 0:n], r=[PSK[b[ti]]], w=[f"stg{h % 2}"],
                                  scale=QSCALE)
                    S.add("sp", lambda e, st=st, h=h: e.dma_start(out=q_scr[h], in_=st), r=[f"stg{h % 2}"],
                          w=[("q_scr", h)], dma=f"qs{h % 4}")

        def gate_phase(li, wmat, hctr):
            S.add("pool", lambda e: e.dma_start(out=wf_sb[:].rearrange("p c j -> p (c j)"), in_=wf_d[li]),
                  w=["wf"], dma="wf")
            grp = hctr[0] % 2
            hctr[0] += 1
            proj_fm(wf_sb, "wf", 0, 16, grp)
            b = G[grp]

            def emit_e(e):
                ins = None
                for ti, (t0, n) in enumerate(TCH):
                    ins = e.activation(out=e16[:, t0:t0 + n], in_=psb[b[ti]][0:16, 0:n], func=AF.Exp,
                                       bias=negb[:, li:li + 1], scale=-1.0)
                return ins
            S.add("act", emit_e, r=[PSK[x] for x in b] + ["negb"], w=["e16"])
            S.add("act", lambda e: e.activation(out=nl16, in_=e16, func=AF.Ln, bias=1.0, scale=1.0),
                  r=["e16"], w=["nl16"])
            S.add("sp", lambda e: e.dma_start(out=g_src, in_=nl16[:, 0:NR]), r=["nl16"], w=["g_src"], dma="gs")
            S.add("pool", lambda e: e.collective_compute("AllGather", ALU.bypass, replica_groups=RG,
                                                         ins=[g_src], outs=[g_dst]),
                  r=["g_src"], w=["g_dst"], dma="ccg", inc=1)

        def gate_phase2():
            S.add("sp", lambda e: e.dma_start(out=nl_all[:, 0:2048].rearrange("h (r t) -> h r t", r=2),
                                              in_=g_dst.rearrange("(r h) t -> h r t", r=2)),
                  r=["g_dst"], w=["gate"], dma="gl")
            S.add("dve", lambda e: e.tensor_copy(out=nl_all[:, 2048:2064], in_=nl16[:, NR:NT]),
                  r=["nl16", "gate"], w=["gate"])
            S.add("dve", lambda e: e.tensor_tensor_scan(out=csv[:, 2048:2064], data0=ones16[:, 0:16],
                                                        data1=nl_all[:, 2048:2064], initial=0.0,
                                                        op0=ALU.mult, op1=ALU.add),
                  r=["gate", "ones16"], w=["gate"])
            prev = csv[:, 2063:2064]
            for gk in range(16):
                r_, lb = gk % 2, gk // 2
                o = r_ * 1024 + lb * 128
                S.add("dve", lambda e, o=o, prev=prev: e.tensor_tensor_scan(out=csv[:, o:o + 128], data0=ones16[:],
                                                                            data1=nl_all[:, o:o + 128], initial=prev,
                                                                            op0=ALU.mult, op1=ALU.add),
                      r=["gate", "ones16"], w=["gate"])
                prev = csv[:, o + 127:o + 128]
            gk_ = dict(r=["gate"], w=["gate"])
            S.add("dve", lambda e: e.tensor_copy(out=pcs[:, 0, :], in_=csv), **gk_)
            S.add("dve", lambda e: e.tensor_tensor(out=nl_all, in0=csv, in1=pcs[:, 0, :], op=ALU.subtract), **gk_)
            S.add("dve", lambda e: e.tensor_copy(out=pcs[:, 1, :], in_=nl_all), **gk_)
            S.add("dve", lambda e: e.tensor_tensor(out=csv, in0=nl_all, in1=pcs[:, 1, :], op=ALU.subtract), **gk_)
            S.add("dve", lambda e: e.tensor_copy(out=pcs[:, 2, :], in_=csv), **gk_)
            for j in range(3):
                S.add("dve", lambda e, j=j: e.tensor_tensor(out=nl_all[:, 0:NR], in0=pcs[:, j, 0:NR],
                                                            in1=pcs[:, j, NR:2 * NR], op=ALU.subtract), **gk_)
                S.add("dve", lambda e, j=j: e.scalar_tensor_tensor(out=qpc[:, j, 0:NR], in0=nl_all[:, 0:NR],
                                                                   scalar=rks[:, 0:1], in1=pcs[:, j, 0:NR],
                                                                   op0=ALU.mult, op1=ALU.subtract),
                      r=["gate", "rks"], w=["qpc"])
                S.add("dve", lambda e, j=j: e.tensor_scalar(out=qpc[:, j, NR:NT], in0=pcs[:, j, 2048:2064],
                                                            scalar1=-1.0, scalar2=None, op0=ALU.mult),
                      r=["gate"], w=["qpc"])
            S.add("sp", lambda e: e.dma_start(out=ka_scr[:, 3:6, :], in_=pcs), r=["gate"], w=["ka_scr"], dma="aug")
            S.add("sp", lambda e: e.dma_start(out=qa_scr[:, 0:3, :], in_=qpc), r=["qpc"], w=["qa_scr"], dma="aug")

        def load_head(h, fox, ktd, vd, kkey, vkey):
            g, hh = h // 4, h % 4
            A = att[h % 2]
            i = h % 2
            S.add("sp", lambda e: e.dma_start(out=A["qt"], in_=q_scr[h]), r=[("q_scr", h)], w=[f"att{i}qt"],
                  dma=f"lq{i}")
            S.add("sp", lambda e: e.dma_start(
                out=A["kt"].rearrange("p (r t) -> p r t", r=2),
                in_=ktd[g].rearrange("(r x) t -> x r t", r=2)[hh * 128:(hh + 1) * 128]),
                r=[(kkey, g)], w=[f"att{i}kt"], dma=f"lk{i}")
            S.add("sp", lambda e: e.dma_start(
                out=A["vv"],
                in_=vd[g].rearrange("(b p) n -> p b n", p=128)[:, :, hh * 128:(hh + 1) * 128]),
                r=[(vkey, g)], w=[f"att{i}vv"], dma=f"lv{i}")
            if fox:
                S.add("sp", lambda e: e.dma_start(out=A["qa"], in_=qa_scr[h]), r=["qa_scr", "qa_ones"],
                      w=[f"att{i}qa"], dma=f"lqa{i}")
                S.add("sp", lambda e: e.dma_start(out=A["ka"], in_=ka_scr[h]), r=["ka_scr", "ka_ones"],
                      w=[f"att{i}ka"], dma=f"lka{i}")

        sctr = [0]
        pctr = [0]
        octr = [0]
        lctr = [0]
        ectr = [0]

        def run_pipeline(items, nstages, lag):
            n = len(items)
            for s in range(n + lag * (nstages - 1)):
                for k in range(nstages):
                    t = s - k * lag
                    if 0 <= t < n:
                        it = items[t]
                        it["stages"][k]()
                        if k == nstages - 1 and it.get("post"):
                            it["post"]()

        def fox_items(h, items, after_head):
            A = att[h % 2]
            i = h % 2
            qt, kt, vv, qa, ka = A["qt"], A["kt"], A["vv"], A["qa"], A["ka"]
            rk_ = [f"att{i}qt", f"att{i}kt", f"att{i}qa", f"att{i}ka", "consts", "ktm"]
            os_ = ostg[h % 2]
            osk = f"stg{h % 2}"

            def tile(kl, kal, vl, krows, c0, a0, N, neg, first, last, ob, db, post=None):
                sb_ = sctr[0] % 3
                sctr[0] += 1
                pi = pctr[0] % 3
                pctr[0] += 1
                P = pt[pi]

                def emit_s(e):
                    e.matmul(psb[sb_][0:krows, 0:N], lhsT=kl, rhs=qt[:, c0 + a0:c0 + a0 + N], start=True, stop=False)
                    ins = e.matmul(psb[sb_][0:krows, 0:N], lhsT=kal, rhs=qa[:, c0 + a0:c0 + a0 + N],
                                   start=False, stop=(neg is None))
                    if neg is not None:
                        ncols = min(128, N)
                        ins = e.matmul(psb[sb_][0:krows, 0:ncols], lhsT=cst(C_IDENT, krows, krows),
                                       rhs=neg[0:krows, 0:ncols], start=False, stop=True)
                    return ins

                def emit_o(e):
                    e.matmul(psb[ob][:, a0:a0 + N], lhsT=vl, rhs=P[0:krows, 0:N], start=first, stop=last)
                    return e.matmul(psb[db][:, a0:a0 + N], lhsT=cst(C_ONES, krows, 128), rhs=P[0:krows, 0:N],
                                    start=first, stop=last)

                def stA():
                    S.add("pe", emit_s, r=rk_, w=[PSK[sb_]])
                    S.add("act", lambda e: e.activation(out=P[0:krows, 0:N], in_=psb[sb_][0:krows, 0:N],
                                                        func=AF.Exp), r=[PSK[sb_]], w=[f"pt{pi}"])

                def stB():
                    S.add("pe", emit_o, r=[f"pt{pi}", f"att{i}vv", "vm", "consts"], w=[PSK[ob], PSK[db]])
                items.append(dict(stages=[stA, stB], post=post))

            def finish(c0, n, ob, db):
                S.add("dve", lambda e: e.reciprocal(out=rdv[:, 0:n], in_=psb[db][:, 0:n]), r=[PSK[db]], w=["rdv"])
                S.add("dve", lambda e: e.tensor_tensor(out=os_[:, c0:c0 + n], in0=psb[ob][:, 0:n], in1=rdv[:, 0:n],
                                                       op=ALU.mult), r=[PSK[ob], "rdv"], w=[osk])

            for qtile in range(2):
                c0 = qtile * 512
                lb0 = qtile * 4
                ob = 3 + (octr[0] % 2)
                db = 5 + (octr[0] % 2)
                octr[0] += 1
                tile(ktm[:, h, :], ka[:, 2048:2064], vm[0:16, h * 128:(h + 1) * 128], 16, c0, 0, 512, None,
                     True, False, ob, db)
                nblk = lb0 + 4
                for lbk in range(nblk):
                    for r_ in range(2):
                        j0 = max(lbk, lb0)
                        a0 = (j0 - lb0) * 128
                        N = 512 - a0
                        kc = r_ * 1024 + lbk * 128
                        neg = cst(C_FNEG0 + r_) if lbk >= lb0 else None
                        lastt = (lbk == nblk - 1 and r_ == 1)
                        post = (lambda c0=c0, ob=ob, db=db: finish(c0, 512, ob, db)) if lastt else None
                        tile(kt[:, kc:kc + 128], ka[:, kc:kc + 128], vv[:, r_ * 8 + lbk, :], 128, c0, a0, N, neg,
                             False, lastt, ob, db, post)
            ob = 3 + (octr[0] % 2)
            db = 5 + (octr[0] % 2)
            octr[0] += 1

            def post_head(ob=ob, db=db):
                finish(NR, 16, ob, db)
                S.add("sp", lambda e: e.dma_start(out=o_scr[h], in_=os_), r=[osk], w=["o_scr"], dma="os")
                after_head(h)
            tile(ktm[:, h, :], ka[:, 2048:2064], vm[0:16, h * 128:(h + 1) * 128], 16, NR, 0, 16, cst(C_CNEG),
                 True, True, ob, db, post_head)

        def sb_items(h, items, after_head):
            A = att[h % 2]
            i = h % 2
            qt, kt, vv = A["qt"], A["kt"], A["vv"]
            rk_ = [f"att{i}qt", f"att{i}kt", "consts", "ktm"]
            os_ = ostg[h % 2]
            osk = f"stg{h % 2}"

            def tile(kl, vl, krows, c0, a0, N, neg, lsum_cols, ob, last, zero_n=None, post=None):
                zb = sctr[0] % 3
                sctr[0] += 1
                pi = pctr[0] % 3
                pctr[0] += 1
                li_ = lctr[0] % 2
                lctr[0] += 1
                ei = ectr[0] % 2
                ectr[0] += 1
                P, L, E = pt[pi], l16[li_], ef[ei]

                def emit_z(e):
                    ins = e.matmul(psb[zb][0:krows, 0:N], lhsT=kl, rhs=qt[:, c0 + a0:c0 + a0 + N],
                                   start=True, stop=False)
                    if neg is not None:
                        ncols = min(128, N)
                        ins = e.matmul(psb[zb][0:krows, 0:ncols], lhsT=cst(C_IDENT, krows, krows),
                                       rhs=neg[0:krows, 0:ncols], start=False, stop=False)
                    return ins

                def emit_a(e):
                    have = lsum_cols is not None
                    ins = e.matmul(psb[zb][0:krows, 0:N], lhsT=cst(C_NTRI, krows, krows), rhs=L[0:krows, 0:N],
                                   start=False, stop=not have)
                    if have:
                        x0, n = lsum_cols
                        ins = e.matmul(psb[zb][0:krows, x0 - a0:x0 - a0 + n], lhsT=cst(C_NONES, 128, krows),
                                       rhs=lsum[:, x0:x0 + n], start=False, stop=True)
                    return ins

                def emit_l(e):
                    ins = None
                    if lsum_cols is None or lsum_cols[0] > a0:
                        ins = e.tensor_copy(out=lsum[:, a0:a0 + 128], in_=L[:, 0:128])
                    if lsum_cols is not None:
                        x0, n = lsum_cols
                        ins = e.tensor_tensor(out=lsum[:, x0:x0 + n], in0=lsum[:, x0:x0 + n],
                                              in1=L[:, x0 - a0:x0 - a0 + n], op=ALU.add)
                    return ins

                def emit_o(e):
                    if zero_n is not None:
                        e.matmul(psb[ob][:, 0:zero_n], lhsT=cst(C_ZERO), rhs=qt[:, c0:c0 + zero_n],
                                 start=True, stop=False, skip_group_check=True)
                    return e.matmul(psb[ob][:, a0:a0 + N], lhsT=vl, rhs=P[0:krows, 0:N],
                                    start=False, stop=last, skip_group_check=True)

                def stA():
                    S.add("pe", emit_z, r=rk_, w=[PSK[zb]])
                    S.add("act", lambda e: e.activation(out=E[0:krows, 0:N], in_=psb[zb][0:krows, 0:N],
                                                        func=AF.Exp), r=[PSK[zb]], w=[f"ef{ei}"])
                    S.add("act", lambda e: e.activation(out=L[0:krows, 0:N], in_=E[0:krows, 0:N], func=AF.Ln,
                                                        bias=1.0, scale=1.0), r=[f"ef{ei}"], w=[f"l16{li_}"])

                def stB():
                    S.add("pe", emit_a, r=[f"l16{li_}", "lsum", "consts"], w=[PSK[zb]])
                    if krows == 128 and not last:
                        S.add("dve", emit_l, r=[f"l16{li_}"], w=["lsum"])
                    S.add("act", lambda e: e.activation(out=P[0:krows, 0:N], in_=psb[zb][0:krows, 0:N],
                                                        func=AF.Exp), r=[PSK[zb]], w=[f"pt{pi}"])

                def stC():
                    S.add("pe", emit_o, r=[f"pt{pi}", f"att{i}vv", f"att{i}qt", "vm", "consts"], w=[PSK[ob]])
                items.append(dict(stages=[stA, stB, stC], post=post))

            for qtile in range(2):
                c0 = qtile * 512
                lb0 = qtile * 4
                ob = 3 + (octr[0] % 3)
                octr[0] += 1
                have_from = None
                firstt = True
                for lbk in range(lb0 + 3, -1, -1):
                    for r_ in (1, 0):
                        j0 = max(lbk, lb0)
                        a0 = (j0 - lb0) * 128
                        N = 512 - a0
                        kc = r_ * 1024 + lbk * 128
                        neg = cst(C_SNEG0 + r_) if lbk >= lb0 else None
                        lc = None if have_from is None else (have_from, 512 - have_from)
                        tile(kt[:, kc:kc + 128], vv[:, r_ * 8 + lbk, :], 128, c0, a0, N, neg, lc, ob, False,
                             zero_n=(512 if firstt else None))
                        firstt = False
                        have_from = a0

                def post_q(ob=ob, c0=c0):
                    S.add("dve", lambda e: e.tensor_copy(out=os_[:, c0:c0 + 512], in_=psb[ob][:, 0:512]),
                          r=[PSK[ob]], w=[osk])
                tile(ktm[:, h, :], vm[0:16, h * 128:(h + 1) * 128], 16, c0, 0, 512, None, (0, 512), ob, True,
                     post=post_q)
            ob = 3 + (octr[0] % 3)
            octr[0] += 1

            def post_head(ob=ob):
                S.add("dve", lambda e: e.tensor_copy(out=os_[:, NR:NT], in_=psb[ob][:, 0:16]), r=[PSK[ob]], w=[osk])
                S.add("sp", lambda e: e.dma_start(out=o_scr[h], in_=os_), r=[osk], w=["o_scr"], dma="os")
                after_head(h)
            tile(ktm[:, h, :], vm[0:16, h * 128:(h + 1) * 128], 16, NR, 0, 16, cst(C_SCNEG), None, ob, True,
                 zero_n=16, post=post_head)

        def attention(fox, ktd, vd, kkey, vkey):
            load_head(0, fox, ktd, vd, kkey, vkey)
            load_head(1, fox, ktd, vd, kkey, vkey)

            def after_head(h):
                if h + 2 < NH:
                    load_head(h + 2, fox, ktd, vd, kkey, vkey)
            items = []
            for h in range(NH):
                if fox:
                    fox_items(h, items, after_head)
                else:
                    sb_items(h, items, after_head)
            if fox:
                run_pipeline(items, 2, 2)
            else:
                run_pipeline(items, 3, 1)

        def add_to_h(n, grp):
            b = G[grp]
            for ti, (t0, nn) in enumerate(TCH):
                S.add("dve", lambda e, ti=ti, t0=t0, nn=nn: e.tensor_tensor(out=hT[:, n, t0:t0 + nn],
                                                                           in0=psb[b[ti]][:, 0:nn],
                                                                           in1=hT[:, n, t0:t0 + nn], op=ALU.add),
                      r=[PSK[b[ti]], ("hT", n)], w=[("hT", n)])

        def oproj(wmat, hctr):
            S.add("sp", lambda e: e.dma_start(out=aT, in_=o_scr.rearrange("h p t -> p h t")), r=["o_scr"], w=["aT"],
                  dma="lo")
            for s in range(4):
                slab, skey = wslab(colslab(wmat, s * 512), "col")
                for nn in range(4):
                    n = s * 4 + nn
                    grp = hctr[0] % 2
                    hctr[0] += 1
                    proj_fm(slab, skey, nn * 128, 128, grp)
                    add_to_h(n, grp)

        def mlp_up(li, fg, hctr):
            slab, skey = wslab(colslab(w_up[li], fg * 512), "col")
            gt = gT[fg % 2]
            gk = f"gT{fg % 2}"

            def up_chunk(fc):
                grp = hctr[0] % 2
                hctr[0] += 1
                proj_fm(slab, skey, fc * 128, 128, grp)
                b = G[grp]

                def evac(ti, t0, n):
                    rt = rtmp[ti % 2]
                    rkk = f"rtmp{ti % 2}"
                    S.add("act", lambda e: e.activation(out=rt[:, 0:n], in_=psb[b[ti]][:, 0:n], func=AF.Relu),
                          r=[PSK[b[ti]]], w=[rkk])
                    S.add("act", lambda e: e.activation(out=gt[:, fc, t0:t0 + n], in_=rt[:, 0:n], func=AF.Square),
                          r=[rkk], w=[gk])
                for ti, (t0, n) in enumerate(TCH):
                    evac(ti, t0, n)
            for fc in range(4):
                up_chunk(fc)

        def mlp_down(li, fg, hctr):
            gt = gT[fg % 2]
            gk = f"gT{fg % 2}"
            dslab, dkey = wslab(w_down[li, fg * 512:(fg + 1) * 512, :].rearrange("(c p) n -> p c n", p=128), "row")

            def down_chunk(n):
                grp = hctr[0] % 2
                hctr[0] += 1
                b = G[grp]

                def emit(e):
                    ins = None
                    for fc in range(4):
                        for ti, (t0, nn) in enumerate(TCH):
                            ins = e.matmul(psb[b[ti]][:, 0:nn], lhsT=dslab[:, fc, n * 128:(n + 1) * 128],
                                           rhs=gt[:, fc, t0:t0 + nn], start=(fc == 0), stop=(fc == 3))
                    return ins
                S.add("pe", emit, r=[dkey, gk], w=[PSK[x] for x in b])
                add_to_h(n, grp)
            for n in range(NCH):
                down_chunk(n)

        def mlp(li, hctr):
            pend = None
            for fg in range(16):
                mlp_up(li, fg, hctr)
                if pend is not None:
                    mlp_down(li, pend, hctr)
                pend = fg
            mlp_down(li, pend, hctr)

        def program():
            prologue()
            hctr = [0]
            vctr = [0]
            for li in range(nlayers):
                rms_stats()
                if li < 2:
                    wmat = fox_w_in[li]
                    rms_apply(li * 16)
                    for g in range(4):
                        proj_k_group(wmat, D + g * 512, g, hctr)
                        proj_v_group(wmat, 2 * D + g * 512, g, vctr)
                        gather_kv(g, kt_dst, v_dst, "kt_dst", "v_dst")
                    gate_phase(li, wmat, hctr)
                    gate_phase2()
                    proj_q(wmat, 0, hctr)
                    attention(True, kt_dst, v_dst, "kt_dst", "v_dst")
                    oproj(fox_w_o[li], hctr)
                else:
                    if li == 2:
                        rms_apply(128)
                        for g in range(4):
                            proj_k_group(w_kv, g * 512, g, hctr)
                            proj_v_group(w_kv, D + g * 512, g, vctr)
                            gather_kv(g, skt_dst, sv_dst, "skt_dst", "sv_dst")
                    rms_apply(li * 16)
                    proj_q(sb_w_q[li - 2], 0, hctr)
                    attention(False, skt_dst, sv_dst, "skt_dst", "sv_dst")
                    oproj(sb_w_o[li - 2], hctr)
                rms_stats()
                rms_apply(64 + li * 16)
                mlp(li, hctr)

            rms_stats()
            outs = []
            for c in range(NCH):
                fs = fstg[c % 2]
                S.add("dve", lambda e, c=c, fs=fs: e.scalar_tensor_tensor(out=fs, in0=hT[:, c, 0:NR],
                                                                          scalar=gcol(144 + c), in1=rstd[:, 0:NR],
                                                                          op0=ALU.mult, op1=ALU.mult),
                      r=[("hT", c), "rstd", "gvec"], w=[f"fstg{c % 2}"])
                S.add("sp", lambda e, c=c, fs=fs: e.dma_start(out=yT[c * 128:(c + 1) * 128, :], in_=fs),
                      r=[f"fstg{c % 2}"], w=[("yT", c)], dma="st_y")
            S.add("sp", None, r=[("yT", c) for c in range(NCH)])


        def reset_counters():
            ring_state.update(i=0, issued=0, req=[])
            for ctr in (sctr, pctr, octr, lctr, ectr):
                ctr[0] = 0
            evac_ctr["i"] = 0

        S_real = S
        S = Sched()
        S.alias, S.regions = S_real.alias, S_real.regions
        reset_counters()
        program()
        ring_state["plan"] = list(ring_state["req"])
        S = Sched()
        S.alias, S.regions = S_real.alias, S_real.regions
        reset_counters()
        program()

        S.finalize()
        sems = {}
        for idx, k in enumerate(sorted(S.semkeys, key=str)):
            sems[k] = es.enter_context(nc.semaphore(f"s{idx}"))
        with nc.Block() as block:
            @block.tensor
            def _(e):
                S.emit_engine("pe", e, sems)

            @block.scalar
            def _(e):
                S.emit_engine("act", e, sems)

            @block.vector
            def _(e):
                S.emit_engine("dve", e, sems)

            @block.gpsimd
            def _(e):
                S.emit_engine("pool", e, sems)

            @block.sync
            def _(e):
                S.emit_engine("sp", e, sems)
    return nc


def make_consts(rank):
    i = np.arange(128)[:, None]
    j = np.arange(128)[None, :]
    ones = np.ones((128, 128), np.float32)
    zeros = np.zeros((128, 128), np.float32)
    ntri = np.where(i >= j, -1.0, 0.0).astype(np.float32)
    ident = np.eye(128, dtype=np.float32)
    cneg = np.where(i <= j, 0.0, NEGV).astype(np.float32)
    scneg = np.where(i < j, 0.0, NEGV).astype(np.float32)
    allneg = np.full((128, 128), NEGV, np.float32)
    if rank == 0:
        fneg0, fneg1, sneg0, sneg1 = cneg, allneg, scneg, allneg
    else:
        fneg0, fneg1, sneg0, sneg1 = zeros, cneg, zeros, scneg
    blocks = [ones, ntri, -ones, ident, zeros, fneg0, fneg1, sneg0, sneg1, cneg, scneg]
    return np.concatenate(blocks, axis=1).astype(ml_dtypes.bfloat16)


def prep_inputs(x, meta_tokens, norm_attn, norm_mlp, w_up, w_down, fox_w_in, fox_b_f, fox_w_o,
                kv_norm, w_kv, sb_w_q, sb_w_o, final_norm):
    f = lambda a: np.ascontiguousarray(np.asarray(a, dtype=np.float32))
    x = f(x)
    meta = f(meta_tokens)
    gv = np.zeros((128, 160), np.float32)
    na, nm = f(norm_attn), f(norm_mlp)
    for l in range(DEPTH):
        gv[:, l * 16:(l + 1) * 16] = na[l].reshape(16, 128).T
        gv[:, 64 + l * 16:64 + (l + 1) * 16] = nm[l].reshape(16, 128).T
    gv[:, 128:144] = f(kv_norm).reshape(16, 128).T
    gv[:, 144:160] = f(final_norm).reshape(16, 128).T
    shared = dict(w_up=f(w_up), w_down=f(w_down), fox_w_in=f(fox_w_in), fox_w_o=f(fox_w_o), w_kv=f(w_kv),
                  sb_w_q=f(sb_w_q), sb_w_o=f(sb_w_o), bf=np.ascontiguousarray(f(fox_b_f).T))
    fwi = shared["fox_w_in"]
    shared["wf"] = np.ascontiguousarray(
        fwi[:, :, 3 * D:].reshape(2, NCH, 128, NH).transpose(0, 2, 1, 3).reshape(2, 128, NCH * NH))
    in_maps = []
    for c in range(8):
        b, r = c // 2, c % 2
        xb = x[b].reshape(16, 128, D)[r::2].reshape(NR, D)
        xt = np.ascontiguousarray(np.concatenate([xb, meta], axis=0).T)
        m = dict(shared)
        m["xT"] = xt
        m["gvec"] = gv
        m["rk"] = np.full((16, 1), float(r), np.float32)
        m["consts"] = make_consts(r)
        in_maps.append(m)
    return in_maps


def assemble(results):
    out = np.zeros((BATCH, SEQ, D), np.float32)
    for c in range(8):
        b, r = c // 2, c % 2
        y = np.asarray(results[c]["yT"], dtype=np.float32).T
        out[b].reshape(16, 128, D)[r::2] = y.reshape(8, 128, D)
    return out


_NC_CACHE = {}


def kernel(**inputs):
    in_maps = prep_inputs(**inputs)
    if "nc" not in _NC_CACHE:
        _NC_CACHE["nc"] = build(DEPTH)
    res = run_bass_kernel_spmd(_NC_CACHE["nc"], in_maps, core_ids=list(range(8)))
    return assemble(res.results)
```

```python
from contextlib import ExitStack

import numpy as np
import ml_dtypes
import concourse.bass as bass
import concourse.mybir as mybir
from concourse.bass_utils import run_bass_kernel_spmd

F32 = mybir.dt.float32
BF16 = mybir.dt.bfloat16
AF = mybir.ActivationFunctionType
ALU = mybir.AluOpType

D = 2048
NH = 16
DH = 128
DFF = 8192
NMETA = 16
SEQ = 2048
BATCH = 4
DEPTH = 4
NCH = 16
NT = 1040
NR = 1024
TCH = [(0, 347), (347, 347), (694, 346)]
EPS = 1e-6
QSCALE = DH ** -0.5
NEGV = -30000.0
RG = [[0, 1], [2, 3], [4, 5], [6, 7]]
EPOCH = 30000
ENGS = ["pe", "act", "dve", "pool", "sp"]

C_ONES, C_NTRI, C_NONES, C_IDENT, C_ZERO, C_FNEG0, C_FNEG1, C_SNEG0, C_SNEG1, C_CNEG, C_SCNEG = range(11)
NCONST = 11


class Sched:
    def __init__(self):
        self.ops = []
        self.lastw = {}
        self.readers = {}
        self.alias = {}

    def region(self, key, arena, start, end):
        self.regions = getattr(self, "regions", {})
        for k2, (a2, s2, e2) in self.regions.items():
            if a2 == arena and s2 < end and start < e2 and k2 != key:
                self.alias.setdefault(key, set()).add(k2)
                self.alias.setdefault(k2, set()).add(key)
        self.regions[key] = (arena, start, end)

    def _keys(self, k):
        return [k] + list(self.alias.get(k, ()))

    def add(self, eng, emit, r=(), w=(), dma=None, inc=16):
        i = len(self.ops)
        hard = set()
        soft = set()
        for k in r:
            for kk in self._keys(k):
                j = self.lastw.get(kk)
                if j is not None:
                    hard.add(j)
        for k in w:
            for kk in self._keys(k):
                j = self.lastw.get(kk)
                if j is not None:
                    hard.add(j)
                for j in self.readers.get(kk, ()):
                    soft.add(j)
        self.ops.append(dict(eng=eng, emit=emit, hard=hard, soft=soft - hard, dma=dma, inc=inc))
        for k in r:
            self.readers.setdefault(k, []).append(i)
        for k in w:
            self.lastw[k] = i
            self.readers[k] = []
        return i

    def finalize(self):
        ops = self.ops
        self.eng_ops = {e: [] for e in ENGS}
        for i, o in enumerate(ops):
            self.eng_ops[o["eng"]].append(i)
        import bisect
        by_key = {}
        for i, o in enumerate(ops):
            if o["dma"] is not None:
                by_key.setdefault(o["dma"], []).append(i)
        for i, o in enumerate(ops):
            deps = {}
            for j in o["hard"] | o["soft"]:
                pj = ops[j]
                is_soft = j not in o["hard"]
                if pj["dma"] is None:
                    if pj["eng"] == o["eng"] and o["dma"] is None:
                        if o["eng"] == "pe" or is_soft:
                            continue
                    key = ("c", pj["eng"])
                else:
                    key = ("d", pj["dma"])
                if key not in deps or deps[key] < j:
                    deps[key] = j
            o["deps"] = sorted(deps.values())
            for j in o["deps"]:
                ops[j]["needed"] = True
        cnt = {}
        dcnt = {}
        self.semkeys = set()
        for o in ops:
            if o["dma"] is None:
                if o.get("needed"):
                    e = o["eng"]
                    cnt[e] = cnt.get(e, 0) + 1
                    epoch, val = divmod(cnt[e] - 1, EPOCH)
                    o["tok"] = (("c", e, epoch), val + 1)
                    self.semkeys.add(o["tok"][0])
            else:
                k = o["dma"]
                dcnt[k] = dcnt.get(k, 0) + o["inc"]
                o["tok"] = (("d", k), dcnt[k])
                self.semkeys.add(o["tok"][0])

    def emit_engine(self, e, engobj, sems):
        ops = self.ops
        waited = {}
        for i in self.eng_ops[e]:
            o = ops[i]
            for j in o["deps"]:
                semkey, val = ops[j]["tok"]
                if waited.get(semkey, 0) < val:
                    engobj.wait_ge(sems[semkey], val)
                    waited[semkey] = val
            if o["emit"] is not None:
                ins = o["emit"](engobj)
                if "tok" in o:
                    ins.then_inc(sems[o["tok"][0]], o["inc"] if o["dma"] is not None else 1)


def build(nlayers=DEPTH):
    nc = bass.Bass("TRN2", target_bir_lowering=False)
    S = Sched()

    def din(name, shape, dt=F32):
        return nc.dram_tensor(name, list(shape), dt, kind="ExternalInput").ap()

    xT = din("xT", [D, NT])
    gvec_d = din("gvec", [128, 160])
    bf_d = din("bf", [16, 2])
    rk_d = din("rk", [16, 1])
    consts_d = din("consts", [128, NCONST * 128], BF16)
    w_up = din("w_up", [DEPTH, D, DFF])
    w_down = din("w_down", [DEPTH, DFF, D])
    fox_w_in = din("fox_w_in", [2, D, 3 * D + NH])
    wf_d = din("wf", [2, 128, 256])
    fox_w_o = din("fox_w_o", [2, D, D])
    w_kv = din("w_kv", [D, 2 * D])
    sb_w_q = din("sb_w_q", [2, D, D])
    sb_w_o = din("sb_w_o", [2, D, D])
    yT = nc.dram_tensor("yT", [D, NR], F32, kind="ExternalOutput").ap()

    def dscr(name, shape, dt=BF16):
        return nc.dram_tensor(name, list(shape), dt, kind="Internal").ap()

    q_scr = dscr("q_scr", [NH, 128, NT])
    o_scr = dscr("o_scr", [NH, 128, NT])
    kt_src = [dscr(f"kt_src{g}", [512, NR]) for g in range(4)]
    v_src = [dscr(f"v_src{g}", [NR, 512]) for g in range(4)]
    kt_dst = [dscr(f"kt_dst{g}", [1024, NR]) for g in range(4)]
    v_dst = [dscr(f"v_dst{g}", [2 * NR, 512]) for g in range(4)]
    skt_dst = [dscr(f"skt_dst{g}", [1024, NR]) for g in range(4)]
    sv_dst = [dscr(f"sv_dst{g}", [2 * NR, 512]) for g in range(4)]
    g_src = dscr("g_src", [16, NR], F32)
    g_dst = dscr("g_dst", [32, NR], F32)
    ka_scr = dscr("ka_scr", [NH, 6, 2064])
    qa_scr = dscr("qa_scr", [NH, 6, NT])

    es = ExitStack()
    with es:
        def sb(name, shape, dt):
            return es.enter_context(nc.sbuf_tensor(name, list(shape), dt))

        hT = sb("hT", [128, NCH, NT], F32)
        arenaB = sb("arenaB", [128, NCH * NT], BF16)
        ring = [sb(f"ring{i}", [128, 8192], BF16) for i in range(3)]
        TRB = 22528
        arenaT = sb("arenaT", [128, TRB // 2], BF16)
        arenaC = sb("arenaC", [128, 28896 // 2], BF16)
        gvec = sb("gvec_sb", [128, 160], F32)
        consts = sb("consts_sb", [128, NCONST * 128], BF16)
        bfs = sb("bf_sb", [16, 2], F32)
        negb = sb("negb_sb", [16, 2], F32)
        rks = sb("rk_sb", [16, 1], F32)
        ones16 = sb("ones16", [16, 128], F32)
        ktm = sb("ktm", [128, NH, 16], BF16)
        vm = sb("vm", [16, D], BF16)
        wf_sb = sb("wf_sb", [128, NCH, 16], BF16)
        psb = [es.enter_context(nc.psum_tensor(f"ps{i}", [128, 512], F32)) for i in range(8)]

        def carve(arena, off, shape, dt, parts=128):
            esz = 2 if dt == BF16 else 4
            n = int(np.prod(shape))
            a = arena[0:parts, off // 2: off // 2 + n * esz // 2]
            if dt == F32:
                a = a.bitcast(F32)
            if len(shape) == 2:
                a = a.rearrange("p (a b) -> p a b", a=shape[0])
            return a

        def V(key, arena, aname, off, shape, dt, parts=128):
            esz = 2 if dt == BF16 else 4
            S.region(key, aname, off, off + int(np.prod(shape)) * esz)
            return carve(arena, off, shape, dt, parts)

        aT = V("aT", arenaB, "B", 0, [NCH, NT], BF16)

        def cst(idx, rows=128, cols=128):
            return consts[0:rows, idx * 128: idx * 128 + cols]

        def gcol(col):
            return gvec[:, col:col + 1]

        PSK = [("ps", i) for i in range(8)]
        G = [(0, 1, 2), (3, 4, 5)]

        sqv = [V("sq0", arenaT, "T", 0, [NT], BF16), V("sq1", arenaT, "T", 2080, [NT], BF16)]
        tmpf = V("tmpf", arenaT, "T", 4160, [NT], F32)
        rstd = V("rstd", arenaT, "T", 8320, [NT], F32)
        stg = [V("stg0", arenaT, "T", 12480, [NT], BF16), V("stg1", arenaT, "T", 14560, [NT], BF16)]
        vstg = [V("vstg0", arenaT, "T", 16640, [512], BF16), V("vstg1", arenaT, "T", 17664, [512], BF16)]
        e16 = V("e16", arenaT, "T", 0, [NT], F32, parts=16)
        nl16 = V("nl16", arenaT, "T", 4160, [NT], F32, parts=16)
        qpc = V("qpc", arenaT, "T", 0, [3, NT], BF16, parts=16)
        pt = [V(f"pt{i}", arenaT, "T", 1024 * i, [512], BF16) for i in range(3)]
        ef = [V(f"ef{i}", arenaT, "T", 3072 + 2048 * i, [512], F32) for i in range(2)]
        l16 = [V(f"l16{i}", arenaT, "T", 7168 + 1024 * i, [512], BF16) for i in range(2)]
        lsum = V("lsum", arenaT, "T", 9216, [512], BF16)
        rdv = V("rdv", arenaT, "T", 10240, [512], F32)
        ostg = stg
        gT = [V("gT0", arenaT, "T", 12480, [4, NT], BF16), V("gT1", arenaT, "T", 0, [4, NT], BF16)]
        rtmp = [V("rtmp0", arenaT, "T", 8320, [512], F32), V("rtmp1", arenaT, "T", 10368, [512], F32)]
        fstg = [V("fstg0", arenaT, "T", 12480, [NR], F32), V("fstg1", arenaT, "T", 16576, [NR], F32)]

        HB = 16480
        att = []
        for i in range(2):
            o = i * HB
            att.append(dict(
                qt=V(f"att{i}qt", arenaB, "B", o, [NT], BF16),
                kt=V(f"att{i}kt", arenaB, "B", o + 2080, [2048], BF16),
                vv=V(f"att{i}vv", arenaB, "B", o + 6176, [16, 128], BF16),
                qa=V(f"att{i}qa", arenaB, "B", o + 10272, [NT], BF16, parts=6),
                ka=V(f"att{i}ka", arenaB, "B", o + 12352, [2064], BF16, parts=6),
            ))
        S.region("gate", "C", 0, 28896)
        attS = []
        for i in range(2):
            o = i * 10272
            attS.append(dict(
                qt=V(f"attS{i}qt", arenaC, "C", o, [NT], BF16),
                kt=V(f"attS{i}kt", arenaC, "C", o + 2080, [2048], BF16),
                vv=V(f"attS{i}vv", arenaC, "C", o + 6176, [16, 128], BF16),
            ))
        qstg = V("qstg", arenaC, "C", 20544, [NT], BF16)
        nl_all = carve(arenaC, 0, [2064], F32, parts=16)
        csv = carve(arenaC, 8256, [2064], F32, parts=16)
        pcs = carve(arenaC, 16512, [3, 2064], BF16, parts=16)

        ring_state = {"i": 0, "issued": 0, "plan": None, "req": []}
        LOOKAHEAD = 2

        def _slab_dst(i, view):
            slot = i % 3
            if view == "col":
                return ring[slot][:].rearrange("p (c n) -> p c n", c=NCH)
            return ring[slot][:].rearrange("p (c n) -> p c n", c=4)

        def _issue_slab(i):
            src_ap, view = ring_state["plan"][i]
            dst = _slab_dst(i, view)
            S.add("pool", lambda e, dst=dst, src_ap=src_ap: e.dma_start(out=dst, in_=src_ap),
                  w=[("ring", i % 3)], dma=f"ring{i % 3}")

        def wslab(src_ap, view):
            i = ring_state["i"]
            ring_state["i"] += 1
            ring_state["req"].append((src_ap, view))
            plan = ring_state["plan"]
            if plan is None:
                dst = _slab_dst(i, view)
                S.add("pool", lambda e, dst=dst, src_ap=src_ap: e.dma_start(out=dst, in_=src_ap),
                      w=[("ring", i % 3)], dma=f"ring{i % 3}")
            else:
                upto = min(len(plan), i + LOOKAHEAD + 1)
                while ring_state["issued"] < upto:
                    _issue_slab(ring_state["issued"])
                    ring_state["issued"] += 1
            return _slab_dst(i, view), ("ring", i % 3)

        def colslab(wmat, c0, ncols=512):
            return wmat[:, c0:c0 + ncols].rearrange("(c p) n -> p c n", p=128)

        evac_ctr = {"i": 0}

        def evac_copy(dst, src, r, w, scale=None, eng=None):
            if eng is None:
                eng = "act" if evac_ctr["i"] % 2 == 0 else "dve"
                evac_ctr["i"] += 1
            if eng == "act":
                if scale is None:
                    S.add("act", lambda e: e.activation(out=dst, in_=src, func=AF.Copy), r=r, w=w)
                else:
                    S.add("act", lambda e: e.activation(out=dst, in_=src, func=AF.Copy, scale=float(scale)), r=r, w=w)
            else:
                if scale is None:
                    S.add("dve", lambda e: e.tensor_copy(out=dst, in_=src), r=r, w=w)
                else:
                    S.add("dve", lambda e: e.tensor_scalar(out=dst, in0=src, scalar1=float(scale), scalar2=None,
                                                           op0=ALU.mult), r=r, w=w)

        def proj_fm(slab, skey, col0, M, grp, rhs_tile=aT, rhs_key="aT"):
            banks = G[grp]

            def emit(e):
                ins = None
                for c in range(NCH):
                    for ti, (t0, n) in enumerate(TCH):
                        ins = e.matmul(psb[banks[ti]][0:M, 0:n], lhsT=slab[:, c, col0:col0 + M],
                                       rhs=rhs_tile[:, c, t0:t0 + n], start=(c == 0), stop=(c == NCH - 1))
                return ins
            S.add("pe", emit, r=[skey, rhs_key], w=[PSK[b] for b in banks])

        def prologue():
            xv = xT.rearrange("(c p) t -> p c t", p=128)
            for q4 in range(4):
                S.add("sp", lambda e, q4=q4: e.dma_start(out=hT[:, 4 * q4:4 * q4 + 4, :], in_=xv[:, 4 * q4:4 * q4 + 4, :]),
                      w=[("hT", c) for c in range(4 * q4, 4 * q4 + 4)], dma=f"ld_h{q4}")
            S.add("sp", lambda e: e.dma_start(out=gvec[:], in_=gvec_d), w=["gvec"], dma="ld_c0")
            S.add("sp", lambda e: e.dma_start(out=consts[:], in_=consts_d), w=["consts"], dma="ld_c1")
            S.add("sp", lambda e: e.dma_start(out=bfs[:], in_=bf_d), w=["bfs"], dma="ld_c2")
            S.add("sp", lambda e: e.dma_start(out=rks[:], in_=rk_d), w=["rks"], dma="ld_c3")
            S.add("dve", lambda e: e.tensor_scalar(out=negb[:], in0=bfs[:], scalar1=-1.0, scalar2=None, op0=ALU.mult),
                  r=["bfs"], w=["negb"])
            S.add("dve", lambda e: e.memset(ones16[:], 1.0), w=["ones16"])
            S.add("dve", lambda e: e.memset(pcs, 1.0), w=["gate"])
            S.add("sp", lambda e: e.dma_start(out=ka_scr[:, 0:3, :], in_=pcs), r=["gate"], w=["ka_ones"], dma="aug1a")
            S.add("sp", lambda e: e.dma_start(out=qa_scr[:, 3:6, :], in_=pcs[:, :, 0:NT]), r=["gate"], w=["qa_ones"],
                  dma="aug1b")

        HK = [("hT", c) for c in range(NCH)]

        def rms_stats():
            for c in range(NCH):
                sq = sqv[c % 2]
                if c % 2 == 0:
                    S.add("act", lambda e, sq=sq, c=c: e.activation(out=sq, in_=hT[:, c, :], func=AF.Square),
                          r=[("hT", c)], w=[f"sq{c % 2}"])
                else:
                    S.add("dve", lambda e, sq=sq, c=c: e.tensor_tensor(out=sq, in0=hT[:, c, :], in1=hT[:, c, :],
                                                                      op=ALU.mult),
                          r=[("hT", c)], w=[f"sq{c % 2}"])

                def emit(e, sq=sq, c=c):
                    ins = None
                    for ti, (t0, n) in enumerate(TCH):
                        ins = e.matmul(psb[ti][:, 0:n], lhsT=cst(C_ONES), rhs=sq[:, t0:t0 + n],
                                       start=(c == 0), stop=(c == NCH - 1))
                    return ins
                S.add("pe", emit, r=[f"sq{c % 2}", "consts"], w=[PSK[0], PSK[1], PSK[2]])

            def emit_sqrt(e):
                ins = None
                for ti, (t0, n) in enumerate(TCH):
                    ins = e.activation(out=tmpf[:, t0:t0 + n], in_=psb[ti][:, 0:n], func=AF.Sqrt,
                                       bias=EPS, scale=1.0 / D)
                return ins
            S.add("act", emit_sqrt, r=[PSK[0], PSK[1], PSK[2]], w=["tmpf"])
            S.add("dve", lambda e: e.reciprocal(out=rstd, in_=tmpf), r=["tmpf"], w=["rstd"])

        def rms_apply(col):
            for c in range(NCH):
                S.add("dve", lambda e, c=c: e.scalar_tensor_tensor(out=aT[:, c, :], in0=hT[:, c, :],
                                                                  scalar=gcol(col + c), in1=rstd,
                                                                  op0=ALU.mult, op1=ALU.mult),
                      r=[("hT", c), "rstd", "gvec"], w=["aT"])

        def proj_k_group(wmat, c0, g, hctr):
            slab, skey = wslab(colslab(wmat, c0), "col")
            for hh in range(4):
                h = g * 4 + hh
                grp = hctr[0] % 2
                hctr[0] += 1
                proj_fm(slab, skey, hh * 128, 128, grp)
                st = stg[h % 2]
                b = G[grp]
                for ti, (t0, n) in enumerate(TCH):
                    evac_copy(st[:, t0:t0 + n], psb[b[ti]][:, 0:n], r=[PSK[b[ti]]], w=[f"stg{h % 2}"])
                S.add("dve", lambda e, st=st, h=h: e.tensor_copy(out=ktm[:, h, :], in_=st[:, NR:NT]),
                      r=[f"stg{h % 2}"], w=["ktm"])
                S.add("sp", lambda e, st=st, g=g, hh=hh: e.dma_start(out=kt_src[g][hh * 128:(hh + 1) * 128, :],
                                                                     in_=st[:, 0:NR]),
                      r=[f"stg{h % 2}"], w=[("kt_src", g)], dma=f"kts{g}")

        def proj_v_group(wmat, c0, g, vctr):
            slab, skey = wslab(colslab(wmat, c0), "col")
            for tb in range(9):
                M = 128 if tb < 8 else 16
                bank = 6 + (vctr[0] % 2)
                vs = vstg[vctr[0] % 2]
                vk = f"vstg{vctr[0] % 2}"
                vctr[0] += 1

                def emit(e, tb=tb, M=M, bank=bank):
                    ins = None
                    for c in range(NCH):
                        ins = e.matmul(psb[bank][0:M, 0:512], lhsT=aT[:, c, tb * 128:tb * 128 + M],
                                       rhs=slab[:, c, :], start=(c == 0), stop=(c == NCH - 1))
                    return ins
                S.add("pe", emit, r=[skey, "aT"], w=[PSK[bank]])
                if tb < 8:
                    evac_copy(vs, psb[bank][:, 0:512], r=[PSK[bank]], w=[vk])
                    S.add("sp", lambda e, vs=vs, g=g, tb=tb: e.dma_start(out=v_src[g][tb * 128:(tb + 1) * 128, :],
                                                                         in_=vs),
                          r=[vk], w=[("v_src", g)], dma=f"vs{g}")
                else:
                    evac_copy(vm[0:16, g * 512:(g + 1) * 512], psb[bank][0:16, 0:512], r=[PSK[bank]], w=["vm"])

        def gather_kv(g, ktd, vd, kkey, vkey):
            S.add("pool", lambda e: e.collective_compute("AllGather", ALU.bypass, replica_groups=RG,
                                                         ins=[kt_src[g]], outs=[ktd[g]]),
                  r=[("kt_src", g)], w=[(kkey, g)], dma=f"cck{g}", inc=1)
            S.add("pool", lambda e: e.collective_compute("AllGather", ALU.bypass, replica_groups=RG,
                                                         ins=[v_src[g]], outs=[vd[g]]),
                  r=[("v_src", g)], w=[(vkey, g)], dma=f"ccv{g}", inc=1)

        def q_fillers(wmat):
            fl = []
            for g in range(1, 4):
                holder = {}
                for hh in range(4):
                    h = g * 4 + hh
                    for k in range(4):
                        def piece(g=g, hh=hh, k=k, holder=holder):
                            if "slab" not in holder:
                                holder["slab"], holder["skey"] = wslab(colslab(wmat, g * 512), "col")
                            slab, skey = holder["slab"], holder["skey"]

                            def emit(e):
                                ins = None
                                for c in range(4 * k, 4 * k + 4):
                                    for ti, (t0, n) in enumerate(TCH):
                                        ins = e.matmul(psb[5 + ti][:, 0:n], lhsT=slab[:, c, hh * 128:(hh + 1) * 128],
                                                       rhs=aT[:, c, t0:t0 + n], start=(c == 0), stop=(c == NCH - 1))
                                return ins
                            S.add("pe", emit, r=[skey, "aT"], w=[PSK[5], PSK[6], PSK[7]])
                        fl.append((h, piece))

                    def fin(h=h):
                        for ti, (t0, n) in enumerate(TCH):
                            evac_copy(qstg[:, t0:t0 + n], psb[5 + ti][:, 0:n], r=[PSK[5 + ti]], w=["qstg"],
                                      scale=QSCALE, eng="dve")
                        S.add("sp", lambda e: e.dma_start(out=q_scr[h], in_=qstg), r=["qstg"],
                              w=[("q_scr", h)], dma=f"qs{h % 4}")
                    fl.append((h, fin))
            return fl

        def proj_q(wmat, c0base, hctr, groups=(0, 1, 2, 3)):
            for g in groups:
                slab, skey = wslab(colslab(wmat, c0base + g * 512), "col")
                for hh in range(4):
                    h = g * 4 + hh
                    grp = hctr[0] % 2
                    hctr[0] += 1
                    proj_fm(slab, skey, hh * 128, 128, grp)
                    st = stg[h % 2]
                    b = G[grp]
                    for ti, (t0, n) in enumerate(TCH):
                        evac_copy(st[:, t0:t0 + n], psb[b[ti]][:, 0:n], r=[PSK[b[ti]]], w=[f"stg{h % 2}"],
                                  scale=QSCALE)
                    S.add("sp", lambda e, st=st, h=h: e.dma_start(out=q_scr[h], in_=st), r=[f"stg{h % 2}"],
                          w=[("q_scr", h)], dma=f"qs{h % 4}")

        def gate_phase(li, wmat, hctr):
            S.add("pool", lambda e: e.dma_start(out=wf_sb[:].rearrange("p c j -> p (c j)"), in_=wf_d[li]),
                  w=["wf"], dma="wf")
            grp = hctr[0] % 2
            hctr[0] += 1
            proj_fm(wf_sb, "wf", 0, 16, grp)
            b = G[grp]

            def emit_e(e):
                ins = None
                for ti, (t0, n) in enumerate(TCH):
                    ins = e.activation(out=e16[:, t0:t0 + n], in_=psb[b[ti]][0:16, 0:n], func=AF.Exp,
                                       bias=negb[:, li:li + 1], scale=-1.0)
                return ins
            S.add("act", emit_e, r=[PSK[x] for x in b] + ["negb"], w=["e16"])
            S.add("act", lambda e: e.activation(out=nl16, in_=e16, func=AF.Ln, bias=1.0, scale=1.0),
                  r=["e16"], w=["nl16"])
            S.add("sp", lambda e: e.dma_start(out=g_src, in_=nl16[:, 0:NR]), r=["nl16"], w=["g_src"], dma="gs")
            S.add("pool", lambda e: e.collective_compute("AllGather", ALU.bypass, replica_groups=RG,
                                                         ins=[g_src], outs=[g_dst]),
                  r=["g_src"], w=["g_dst"], dma="ccg", inc=1)

        def gate_phase2():
            S.add("sp", lambda e: e.dma_start(out=nl_all[:, 0:2048].rearrange("h (r t) -> h r t", r=2),
                                              in_=g_dst.rearrange("(r h) t -> h r t", r=2)),
                  r=["g_dst"], w=["gate"], dma="gl")
            S.add("dve", lambda e: e.tensor_copy(out=nl_all[:, 2048:2064], in_=nl16[:, NR:NT]),
                  r=["nl16", "gate"], w=["gate"])
            S.add("dve", lambda e: e.tensor_tensor_scan(out=csv[:, 2048:2064], data0=ones16[:, 0:16],
                                                        data1=nl_all[:, 2048:2064], initial=0.0,
                                                        op0=ALU.mult, op1=ALU.add),
                  r=["gate", "ones16"], w=["gate"])
            prev = csv[:, 2063:2064]
            for gk in range(16):
                r_, lb = gk % 2, gk // 2
                o = r_ * 1024 + lb * 128
                S.add("dve", lambda e, o=o, prev=prev: e.tensor_tensor_scan(out=csv[:, o:o + 128], data0=ones16[:],
                                                                            data1=nl_all[:, o:o + 128], initial=prev,
                                                                            op0=ALU.mult, op1=ALU.add),
                      r=["gate", "ones16"], w=["gate"])
                prev = csv[:, o + 127:o + 128]
            gk_ = dict(r=["gate"], w=["gate"])
            S.add("dve", lambda e: e.tensor_copy(out=pcs[:, 0, :], in_=csv), **gk_)
            S.add("dve", lambda e: e.tensor_tensor(out=nl_all, in0=csv, in1=pcs[:, 0, :], op=ALU.subtract), **gk_)
            S.add("dve", lambda e: e.tensor_copy(out=pcs[:, 1, :], in_=nl_all), **gk_)
            S.add("dve", lambda e: e.tensor_tensor(out=csv, in0=nl_all, in1=pcs[:, 1, :], op=ALU.subtract), **gk_)
            S.add("dve", lambda e: e.tensor_copy(out=pcs[:, 2, :], in_=csv), **gk_)
            for j in range(3):
                S.add("dve", lambda e, j=j: e.tensor_tensor(out=nl_all[:, 0:NR], in0=pcs[:, j, 0:NR],
                                                            in1=pcs[:, j, NR:2 * NR], op=ALU.subtract), **gk_)
                S.add("dve", lambda e, j=j: e.scalar_tensor_tensor(out=qpc[:, j, 0:NR], in0=nl_all[:, 0:NR],
                                                                   scalar=rks[:, 0:1], in1=pcs[:, j, 0:NR],
                                                                   op0=ALU.mult, op1=ALU.subtract),
                      r=["gate", "rks"], w=["qpc"])
                S.add("dve", lambda e, j=j: e.tensor_scalar(out=qpc[:, j, NR:NT], in0=pcs[:, j, 2048:2064],
                                                            scalar1=-1.0, scalar2=None, op0=ALU.mult),
                      r=["gate"], w=["qpc"])
            S.add("sp", lambda e: e.dma_start(out=ka_scr[:, 3:6, :], in_=pcs), r=["gate"], w=["ka_scr"], dma="aug")
            S.add("sp", lambda e: e.dma_start(out=qa_scr[:, 0:3, :], in_=qpc), r=["qpc"], w=["qa_scr"], dma="aug")

        def load_head(h, fox, ktd, vd, kkey, vkey):
            g, hh = h // 4, h % 4
            A = (att if fox else attS)[h % 2]
            i = h % 2
            pre = "att" if fox else "attS"
            S.add("sp", lambda e: e.dma_start(out=A["qt"], in_=q_scr[h]), r=[("q_scr", h)], w=[f"{pre}{i}qt"],
                  dma=f"lq{i}")
            S.add("sp", lambda e: e.dma_start(
                out=A["kt"].rearrange("p (r t) -> p r t", r=2),
                in_=ktd[g].rearrange("(r x) t -> x r t", r=2)[hh * 128:(hh + 1) * 128]),
                r=[(kkey, g)], w=[f"{pre}{i}kt"], dma=f"lk{i}")
            S.add("sp", lambda e: e.dma_start(
                out=A["vv"],
                in_=vd[g].rearrange("(b p) n -> p b n", p=128)[:, :, hh * 128:(hh + 1) * 128]),
                r=[(vkey, g)], w=[f"{pre}{i}vv"], dma=f"lv{i}")
            if fox:
                S.add("sp", lambda e: e.dma_start(out=A["qa"], in_=qa_scr[h]), r=["qa_scr", "qa_ones"],
                      w=[f"att{i}qa"], dma=f"lqa{i}")
                S.add("sp", lambda e: e.dma_start(out=A["ka"], in_=ka_scr[h]), r=["ka_scr", "ka_ones"],
                      w=[f"att{i}ka"], dma=f"lka{i}")

        sctr = [0]
        pctr = [0]
        octr = [0]
        lctr = [0]
        ectr = [0]

        def run_pipeline(items, nstages, lag, tick=None, every=5):
            n = len(items)
            for s in range(n + lag * (nstages - 1)):
                for k in range(nstages):
                    t = s - k * lag
                    if 0 <= t < n:
                        it = items[t]
                        it["stages"][k]()
                        if k == nstages - 1 and it.get("post"):
                            it["post"]()
                if tick is not None and s % every == every - 1:
                    tick()

        def fox_items(h, items, after_head):
            A = att[h % 2]
            i = h % 2
            qt, kt, vv, qa, ka = A["qt"], A["kt"], A["vv"], A["qa"], A["ka"]
            rk_ = [f"att{i}qt", f"att{i}kt", f"att{i}qa", f"att{i}ka", "consts", "ktm"]
            os_ = ostg[h % 2]
            osk = f"stg{h % 2}"

            def tile(kl, kal, vl, krows, c0, a0, N, neg, first, last, ob, db, post=None):
                sb_ = sctr[0] % 3
                sctr[0] += 1
                pi = pctr[0] % 3
                pctr[0] += 1
                P = pt[pi]

                def emit_s(e):
                    e.matmul(psb[sb_][0:krows, 0:N], lhsT=kl, rhs=qt[:, c0 + a0:c0 + a0 + N], start=True, stop=False)
                    ins = e.matmul(psb[sb_][0:krows, 0:N], lhsT=kal, rhs=qa[:, c0 + a0:c0 + a0 + N],
                                   start=False, stop=(neg is None))
                    if neg is not None:
                        ncols = min(128, N)
                        ins = e.matmul(psb[sb_][0:krows, 0:ncols], lhsT=cst(C_IDENT, krows, krows),
                                       rhs=neg[0:krows, 0:ncols], start=False, stop=True)
                    return ins

                def emit_o(e):
                    e.matmul(psb[ob][:, a0:a0 + N], lhsT=vl, rhs=P[0:krows, 0:N], start=first, stop=last)
                    return e.matmul(psb[db][:, a0:a0 + N], lhsT=cst(C_ONES, krows, 128), rhs=P[0:krows, 0:N],
                                    start=first, stop=last)

                def stA():
                    S.add("pe", emit_s, r=rk_, w=[PSK[sb_]])
                    S.add("act", lambda e: e.activation(out=P[0:krows, 0:N], in_=psb[sb_][0:krows, 0:N],
                                                        func=AF.Exp), r=[PSK[sb_]], w=[f"pt{pi}"])

                def stB():
                    S.add("pe", emit_o, r=[f"pt{pi}", f"att{i}vv", "vm", "consts"], w=[PSK[ob], PSK[db]])
                items.append(dict(stages=[stA, stB], post=post))

            def finish(c0, n, ob, db):
                S.add("dve", lambda e: e.reciprocal(out=rdv[:, 0:n], in_=psb[db][:, 0:n]), r=[PSK[db]], w=["rdv"])
                S.add("dve", lambda e: e.tensor_tensor(out=os_[:, c0:c0 + n], in0=psb[ob][:, 0:n], in1=rdv[:, 0:n],
                                                       op=ALU.mult), r=[PSK[ob], "rdv"], w=[osk])

            for qtile in range(2):
                c0 = qtile * 512
                lb0 = qtile * 4
                ob = 3 + (octr[0] % 2)
                db = 5 + (octr[0] % 2)
                octr[0] += 1
                tile(ktm[:, h, :], ka[:, 2048:2064], vm[0:16, h * 128:(h + 1) * 128], 16, c0, 0, 512, None,
                     True, False, ob, db)
                nblk = lb0 + 4
                for lbk in range(nblk):
                    for r_ in range(2):
                        j0 = max(lbk, lb0)
                        a0 = (j0 - lb0) * 128
                        N = 512 - a0
                        kc = r_ * 1024 + lbk * 128
                        neg = cst(C_FNEG0 + r_) if lbk >= lb0 else None
                        lastt = (lbk == nblk - 1 and r_ == 1)
                        post = (lambda c0=c0, ob=ob, db=db: finish(c0, 512, ob, db)) if lastt else None
                        tile(kt[:, kc:kc + 128], ka[:, kc:kc + 128], vv[:, r_ * 8 + lbk, :], 128, c0, a0, N, neg,
                             False, lastt, ob, db, post)
            ob = 3 + (octr[0] % 2)
            db = 5 + (octr[0] % 2)
            octr[0] += 1

            def post_head(ob=ob, db=db):
                finish(NR, 16, ob, db)
                S.add("sp", lambda e: e.dma_start(out=o_scr[h], in_=os_), r=[osk], w=["o_scr"], dma="os")
                after_head(h)
            tile(ktm[:, h, :], ka[:, 2048:2064], vm[0:16, h * 128:(h + 1) * 128], 16, NR, 0, 16, cst(C_CNEG),
                 True, True, ob, db, post_head)

        def sb_items(h, items, after_head):
            A = attS[h % 2]
            i = h % 2
            qt, kt, vv = A["qt"], A["kt"], A["vv"]
            rk_ = [f"attS{i}qt", f"attS{i}kt", "consts", "ktm"]
            os_ = ostg[h % 2]
            osk = f"stg{h % 2}"

            def tile(kl, vl, krows, c0, a0, N, neg, lsum_cols, ob, last, zero_n=None, post=None):
                zb = sctr[0] % 3
                sctr[0] += 1
                pi = pctr[0] % 3
                pctr[0] += 1
                li_ = lctr[0] % 2
                lctr[0] += 1
                ei = ectr[0] % 2
                ectr[0] += 1
                P, L, E = pt[pi], l16[li_], ef[ei]

                def emit_z(e):
                    ins = e.matmul(psb[zb][0:krows, 0:N], lhsT=kl, rhs=qt[:, c0 + a0:c0 + a0 + N],
                                   start=True, stop=False)
                    if neg is not None:
                        ncols = min(128, N)
                        ins = e.matmul(psb[zb][0:krows, 0:ncols], lhsT=cst(C_IDENT, krows, krows),
                                       rhs=neg[0:krows, 0:ncols], start=False, stop=False)
                    return ins

                def emit_a(e):
                    have = lsum_cols is not None
                    ins = e.matmul(psb[zb][0:krows, 0:N], lhsT=cst(C_NTRI, krows, krows), rhs=L[0:krows, 0:N],
                                   start=False, stop=not have)
                    if have:
                        x0, n = lsum_cols
                        ins = e.matmul(psb[zb][0:krows, x0 - a0:x0 - a0 + n], lhsT=cst(C_NONES, 128, krows),
                                       rhs=lsum[:, x0:x0 + n], start=False, stop=True)
                    return ins

                def emit_l(e):
                    ins = None
                    if lsum_cols is None or lsum_cols[0] > a0:
                        ins = e.tensor_copy(out=lsum[:, a0:a0 + 128], in_=L[:, 0:128])
                    if lsum_cols is not None:
                        x0, n = lsum_cols
                        ins = e.tensor_tensor(out=lsum[:, x0:x0 + n], in0=lsum[:, x0:x0 + n],
                                              in1=L[:, x0 - a0:x0 - a0 + n], op=ALU.add)
                    return ins

                def emit_o(e):
                    if zero_n is not None:
                        e.matmul(psb[ob][:, 0:zero_n], lhsT=cst(C_ZERO), rhs=qt[:, c0:c0 + zero_n],
                                 start=True, stop=False, skip_group_check=True)
                    return e.matmul(psb[ob][:, a0:a0 + N], lhsT=vl, rhs=P[0:krows, 0:N],
                                    start=False, stop=last, skip_group_check=True)

                def stA():
                    S.add("pe", emit_z, r=rk_, w=[PSK[zb]])
                    S.add("act", lambda e: e.activation(out=E[0:krows, 0:N], in_=psb[zb][0:krows, 0:N],
                                                        func=AF.Exp), r=[PSK[zb]], w=[f"ef{ei}"])
                    S.add("act", lambda e: e.activation(out=L[0:krows, 0:N], in_=E[0:krows, 0:N], func=AF.Ln,
                                                        bias=1.0, scale=1.0), r=[f"ef{ei}"], w=[f"l16{li_}"])

                def stB():
                    S.add("pe", emit_a, r=[f"l16{li_}", "lsum", "consts"], w=[PSK[zb]])
                    if krows == 128 and not last:
                        S.add("dve", emit_l, r=[f"l16{li_}"], w=["lsum"])
                    S.add("act", lambda e: e.activation(out=P[0:krows, 0:N], in_=psb[zb][0:krows, 0:N],
                                                        func=AF.Exp), r=[PSK[zb]], w=[f"pt{pi}"])

                def stC():
                    S.add("pe", emit_o, r=[f"pt{pi}", f"attS{i}vv", f"attS{i}qt", "vm", "consts"], w=[PSK[ob]])
                items.append(dict(stages=[stA, stB, stC], post=post))

            for qtile in range(2):
                c0 = qtile * 512
                lb0 = qtile * 4
                ob = 3 + (octr[0] % 2)
                octr[0] += 1
                have_from = None
                firstt = True
                for lbk in range(lb0 + 3, -1, -1):
                    for r_ in (1, 0):
                        j0 = max(lbk, lb0)
                        a0 = (j0 - lb0) * 128
                        N = 512 - a0
                        kc = r_ * 1024 + lbk * 128
                        neg = cst(C_SNEG0 + r_) if lbk >= lb0 else None
                        lc = None if have_from is None else (have_from, 512 - have_from)
                        tile(kt[:, kc:kc + 128], vv[:, r_ * 8 + lbk, :], 128, c0, a0, N, neg, lc, ob, False,
                             zero_n=(512 if firstt else None))
                        firstt = False
                        have_from = a0

                def post_q(ob=ob, c0=c0):
                    S.add("dve", lambda e: e.tensor_copy(out=os_[:, c0:c0 + 512], in_=psb[ob][:, 0:512]),
                          r=[PSK[ob]], w=[osk])
                tile(ktm[:, h, :], vm[0:16, h * 128:(h + 1) * 128], 16, c0, 0, 512, None, (0, 512), ob, True,
                     post=post_q)
            ob = 3 + (octr[0] % 2)
            octr[0] += 1

            def post_head(ob=ob):
                S.add("dve", lambda e: e.tensor_copy(out=os_[:, NR:NT], in_=psb[ob][:, 0:16]), r=[PSK[ob]], w=[osk])
                S.add("sp", lambda e: e.dma_start(out=o_scr[h], in_=os_), r=[osk], w=["o_scr"], dma="os")
                after_head(h)
            tile(ktm[:, h, :], vm[0:16, h * 128:(h + 1) * 128], 16, NR, 0, 16, cst(C_SCNEG), None, ob, True,
                 zero_n=16, post=post_head)

        def attention(fox, ktd, vd, kkey, vkey, wq=None):
            load_head(0, fox, ktd, vd, kkey, vkey)
            load_head(1, fox, ktd, vd, kkey, vkey)
            fillers = q_fillers(wq) if wq is not None else []
            fstate = [0]

            def flush(hmax):
                while fstate[0] < len(fillers) and fillers[fstate[0]][0] <= hmax:
                    fillers[fstate[0]][1]()
                    fstate[0] += 1

            def tick():
                if fstate[0] < len(fillers):
                    fillers[fstate[0]][1]()
                    fstate[0] += 1

            def after_head(h):
                if h + 2 < NH:
                    flush(h + 2)
                    load_head(h + 2, fox, ktd, vd, kkey, vkey)
            items = []
            for h in range(NH):
                if fox:
                    fox_items(h, items, after_head)
                else:
                    sb_items(h, items, after_head)
            if fox:
                run_pipeline(items, 2, 2)
            else:
                run_pipeline(items, 3, 1, tick=tick if fillers else None, every=5)
                flush(NH)

        def add_to_h(n, grp):
            b = G[grp]
            for ti, (t0, nn) in enumerate(TCH):
                S.add("dve", lambda e, ti=ti, t0=t0, nn=nn: e.tensor_tensor(out=hT[:, n, t0:t0 + nn],
                                                                           in0=psb[b[ti]][:, 0:nn],
                                                                           in1=hT[:, n, t0:t0 + nn], op=ALU.add),
                      r=[PSK[b[ti]], ("hT", n)], w=[("hT", n)])

        def oproj(wmat, hctr):
            S.add("sp", lambda e: e.dma_start(out=aT, in_=o_scr.rearrange("h p t -> p h t")), r=["o_scr"], w=["aT"],
                  dma="lo")
            for s in range(4):
                slab, skey = wslab(colslab(wmat, s * 512), "col")
                for nn in range(4):
                    n = s * 4 + nn
                    grp = hctr[0] % 2
                    hctr[0] += 1
                    proj_fm(slab, skey, nn * 128, 128, grp)
                    add_to_h(n, grp)

        def mlp_up(li, fg, hctr):
            slab, skey = wslab(colslab(w_up[li], fg * 512), "col")
            gt = gT[fg % 2]
            gk = f"gT{fg % 2}"

            def up_chunk(fc):
                grp = hctr[0] % 2
                hctr[0] += 1
                proj_fm(slab, skey, fc * 128, 128, grp)
                b = G[grp]

                def evac(ti, t0, n):
                    rt = rtmp[ti % 2]
                    rkk = f"rtmp{ti % 2}"
                    S.add("act", lambda e: e.activation(out=rt[:, 0:n], in_=psb[b[ti]][:, 0:n], func=AF.Relu),
                          r=[PSK[b[ti]]], w=[rkk])
                    S.add("act", lambda e: e.activation(out=gt[:, fc, t0:t0 + n], in_=rt[:, 0:n], func=AF.Square),
                          r=[rkk], w=[gk])
                for ti, (t0, n) in enumerate(TCH):
                    evac(ti, t0, n)
            for fc in range(4):
                up_chunk(fc)

        def mlp_down(li, fg, hctr):
            gt = gT[fg % 2]
            gk = f"gT{fg % 2}"
            dslab, dkey = wslab(w_down[li, fg * 512:(fg + 1) * 512, :].rearrange("(c p) n -> p c n", p=128), "row")

            def down_chunk(n):
                grp = hctr[0] % 2
                hctr[0] += 1
                b = G[grp]

                def emit(e):
                    ins = None
                    for fc in range(4):
                        for ti, (t0, nn) in enumerate(TCH):
                            ins = e.matmul(psb[b[ti]][:, 0:nn], lhsT=dslab[:, fc, n * 128:(n + 1) * 128],
                                           rhs=gt[:, fc, t0:t0 + nn], start=(fc == 0), stop=(fc == 3))
                    return ins
                S.add("pe", emit, r=[dkey, gk], w=[PSK[x] for x in b])
                add_to_h(n, grp)
            for n in range(NCH):
                down_chunk(n)

        def mlp(li, hctr):
            pend = None
            for fg in range(16):
                mlp_up(li, fg, hctr)
                if pend is not None:
                    mlp_down(li, pend, hctr)
                pend = fg
            mlp_down(li, pend, hctr)

        def program():
            prologue()
            hctr = [0]
            vctr = [0]
            for li in range(nlayers):
                rms_stats()
                if li < 2:
                    wmat = fox_w_in[li]
                    rms_apply(li * 16)
                    for g in range(4):
                        proj_k_group(wmat, D + g * 512, g, hctr)
                        proj_v_group(wmat, 2 * D + g * 512, g, vctr)
                        gather_kv(g, kt_dst, v_dst, "kt_dst", "v_dst")
                    gate_phase(li, wmat, hctr)
                    gate_phase2()
                    proj_q(wmat, 0, hctr)
                    attention(True, kt_dst, v_dst, "kt_dst", "v_dst")
                    oproj(fox_w_o[li], hctr)
                else:
                    if li == 2:
                        rms_apply(128)
                        for g in range(4):
                            proj_k_group(w_kv, g * 512, g, hctr)
                            proj_v_group(w_kv, D + g * 512, g, vctr)
                            gather_kv(g, skt_dst, sv_dst, "skt_dst", "sv_dst")
                    rms_apply(li * 16)
                    proj_q(sb_w_q[li - 2], 0, hctr, groups=(0,))
                    attention(False, skt_dst, sv_dst, "skt_dst", "sv_dst", wq=sb_w_q[li - 2])
                    oproj(sb_w_o[li - 2], hctr)
                rms_stats()
                rms_apply(64 + li * 16)
                mlp(li, hctr)

            rms_stats()
            outs = []
            for c in range(NCH):
                fs = fstg[c % 2]
                S.add("dve", lambda e, c=c, fs=fs: e.scalar_tensor_tensor(out=fs, in0=hT[:, c, 0:NR],
                                                                          scalar=gcol(144 + c), in1=rstd[:, 0:NR],
                                                                          op0=ALU.mult, op1=ALU.mult),
                      r=[("hT", c), "rstd", "gvec"], w=[f"fstg{c % 2}"])
                S.add("sp", lambda e, c=c, fs=fs: e.dma_start(out=yT[c * 128:(c + 1) * 128, :], in_=fs),
                      r=[f"fstg{c % 2}"], w=[("yT", c)], dma="st_y")
            S.add("sp", None, r=[("yT", c) for c in range(NCH)])


        def reset_counters():
            ring_state.update(i=0, issued=0, req=[])
            for ctr in (sctr, pctr, octr, lctr, ectr):
                ctr[0] = 0
            evac_ctr["i"] = 0

        S_real = S
        S = Sched()
        S.alias, S.regions = S_real.alias, S_real.regions
        reset_counters()
        program()
        ring_state["plan"] = list(ring_state["req"])
        S = Sched()
        S.alias, S.regions = S_real.alias, S_real.regions
        reset_counters()
        program()

        S.finalize()
        sems = {}
        for idx, k in enumerate(sorted(S.semkeys, key=str)):
            sems[k] = es.enter_context(nc.semaphore(f"s{idx}"))
        with nc.Block() as block:
            @block.tensor
            def _(e):
                S.emit_engine("pe", e, sems)

            @block.scalar
            def _(e):
                S.emit_engine("act", e, sems)

            @block.vector
            def _(e):
                S.emit_engine("dve", e, sems)

            @block.gpsimd
            def _(e):
                S.emit_engine("pool", e, sems)

            @block.sync
            def _(e):
                S.emit_engine("sp", e, sems)
    return nc


def make_consts(rank):
    i = np.arange(128)[:, None]
    j = np.arange(128)[None, :]
    ones = np.ones((128, 128), np.float32)
    zeros = np.zeros((128, 128), np.float32)
    ntri = np.where(i >= j, -1.0, 0.0).astype(np.float32)
    ident = np.eye(128, dtype=np.float32)
    cneg = np.where(i <= j, 0.0, NEGV).astype(np.float32)
    scneg = np.where(i < j, 0.0, NEGV).astype(np.float32)
    allneg = np.full((128, 128), NEGV, np.float32)
    if rank == 0:
        fneg0, fneg1, sneg0, sneg1 = cneg, allneg, scneg, allneg
    else:
        fneg0, fneg1, sneg0, sneg1 = zeros, cneg, zeros, scneg
    blocks = [ones, ntri, -ones, ident, zeros, fneg0, fneg1, sneg0, sneg1, cneg, scneg]
    return np.concatenate(blocks, axis=1).astype(ml_dtypes.bfloat16)


def prep_inputs(x, meta_tokens, norm_attn, norm_mlp, w_up, w_down, fox_w_in, fox_b_f, fox_w_o,
                kv_norm, w_kv, sb_w_q, sb_w_o, final_norm):
    f = lambda a: np.ascontiguousarray(np.asarray(a, dtype=np.float32))
    x = f(x)
    meta = f(meta_tokens)
    gv = np.zeros((128, 160), np.float32)
    na, nm = f(norm_attn), f(norm_mlp)
    for l in range(DEPTH):
        gv[:, l * 16:(l + 1) * 16] = na[l].reshape(16, 128).T
        gv[:, 64 + l * 16:64 + (l + 1) * 16] = nm[l].reshape(16, 128).T
    gv[:, 128:144] = f(kv_norm).reshape(16, 128).T
    gv[:, 144:160] = f(final_norm).reshape(16, 128).T
    shared = dict(w_up=f(w_up), w_down=f(w_down), fox_w_in=f(fox_w_in), fox_w_o=f(fox_w_o), w_kv=f(w_kv),
                  sb_w_q=f(sb_w_q), sb_w_o=f(sb_w_o), bf=np.ascontiguousarray(f(fox_b_f).T))
    fwi = shared["fox_w_in"]
    shared["wf"] = np.ascontiguousarray(
        fwi[:, :, 3 * D:].reshape(2, NCH, 128, NH).transpose(0, 2, 1, 3).reshape(2, 128, NCH * NH))
    in_maps = []
    for c in range(8):
        b, r = c // 2, c % 2
        xb = x[b].reshape(16, 128, D)[r::2].reshape(NR, D)
        xt = np.ascontiguousarray(np.concatenate([xb, meta], axis=0).T)
        m = dict(shared)
        m["xT"] = xt
        m["gvec"] = gv
        m["rk"] = np.full((16, 1), float(r), np.float32)
        m["consts"] = make_consts(r)
        in_maps.append(m)
    return in_maps


def assemble(results):
    out = np.zeros((BATCH, SEQ, D), np.float32)
    for c in range(8):
        b, r = c // 2, c % 2
        y = np.asarray(results[c]["yT"], dtype=np.float32).T
        out[b].reshape(16, 128, D)[r::2] = y.reshape(8, 128, D)
    return out


_NC_CACHE = {}


def kernel(**inputs):
    in_maps = prep_inputs(**inputs)
    if "nc" not in _NC_CACHE:
        _NC_CACHE["nc"] = build(DEPTH)
    res = run_bass_kernel_spmd(_NC_CACHE["nc"], in_maps, core_ids=list(range(8)))
    return assemble(res.results)
```

```python
from contextlib import ExitStack

import numpy as np
import ml_dtypes
import concourse.bass as bass
import concourse.mybir as mybir
from concourse.bass_utils import run_bass_kernel_spmd

F32 = mybir.dt.float32
BF16 = mybir.dt.bfloat16
AF = mybir.ActivationFunctionType
ALU = mybir.AluOpType

D = 2048
NH = 16
DH = 128
DFF = 8192
NMETA = 16
SEQ = 2048
BATCH = 4
DEPTH = 4
NCH = 16
NT = 1040
NR = 1024
TCH = [(0, 347), (347, 347), (694, 346)]
EPS = 1e-6
QSCALE = DH ** -0.5
NEGV = -30000.0
RG = [[0, 1], [2, 3], [4, 5], [6, 7]]
EPOCH = 30000
ENGS = ["pe", "act", "dve", "pool", "sp"]

C_ONES, C_NTRI, C_NONES, C_IDENT, C_ZERO, C_FNEG0, C_FNEG1, C_SNEG0, C_SNEG1, C_CNEG, C_SCNEG = range(11)
NCONST = 11


class Sched:
    def __init__(self):
        self.ops = []
        self.lastw = {}
        self.readers = {}
        self.alias = {}

    def region(self, key, arena, start, end):
        self.regions = getattr(self, "regions", {})
        for k2, (a2, s2, e2) in self.regions.items():
            if a2 == arena and s2 < end and start < e2 and k2 != key:
                self.alias.setdefault(key, set()).add(k2)
                self.alias.setdefault(k2, set()).add(key)
        self.regions[key] = (arena, start, end)

    def _keys(self, k):
        return [k] + list(self.alias.get(k, ()))

    def add(self, eng, emit, r=(), w=(), dma=None, inc=16):
        i = len(self.ops)
        hard = set()
        soft = set()
        for k in r:
            for kk in self._keys(k):
                j = self.lastw.get(kk)
                if j is not None:
                    hard.add(j)
        for k in w:
            for kk in self._keys(k):
                j = self.lastw.get(kk)
                if j is not None:
                    hard.add(j)
                for j in self.readers.get(kk, ()):
                    soft.add(j)
        self.ops.append(dict(eng=eng, emit=emit, hard=hard, soft=soft - hard, dma=dma, inc=inc))
        for k in r:
            self.readers.setdefault(k, []).append(i)
        for k in w:
            self.lastw[k] = i
            self.readers[k] = []
        return i

    def finalize(self):
        ops = self.ops
        self.eng_ops = {e: [] for e in ENGS}
        for i, o in enumerate(ops):
            self.eng_ops[o["eng"]].append(i)
        import bisect
        by_key = {}
        for i, o in enumerate(ops):
            if o["dma"] is not None:
                by_key.setdefault(o["dma"], []).append(i)
        for i, o in enumerate(ops):
            deps = {}
            for j in o["hard"] | o["soft"]:
                pj = ops[j]
                is_soft = j not in o["hard"]
                if pj["dma"] is None:
                    if pj["eng"] == o["eng"] and o["dma"] is None:
                        if o["eng"] == "pe" or is_soft:
                            continue
                    key = ("c", pj["eng"])
                else:
                    key = ("d", pj["dma"])
                if key not in deps or deps[key] < j:
                    deps[key] = j
            o["deps"] = sorted(deps.values())
            for j in o["deps"]:
                ops[j]["needed"] = True
        cnt = {}
        dcnt = {}
        self.semkeys = set()
        for o in ops:
            if o["dma"] is None:
                if o.get("needed"):
                    e = o["eng"]
                    cnt[e] = cnt.get(e, 0) + 1
                    epoch, val = divmod(cnt[e] - 1, EPOCH)
                    o["tok"] = (("c", e, epoch), val + 1)
                    self.semkeys.add(o["tok"][0])
            else:
                k = o["dma"]
                dcnt[k] = dcnt.get(k, 0) + o["inc"]
                o["tok"] = (("d", k), dcnt[k])
                self.semkeys.add(o["tok"][0])

    def emit_engine(self, e, engobj, sems):
        ops = self.ops
        waited = {}
        for i in self.eng_ops[e]:
            o = ops[i]
            for j in o["deps"]:
                semkey, val = ops[j]["tok"]
                if waited.get(semkey, 0) < val:
                    engobj.wait_ge(sems[semkey], val)
                    waited[semkey] = val
            if o["emit"] is not None:
                ins = o["emit"](engobj)
                if "tok" in o:
                    ins.then_inc(sems[o["tok"][0]], o["inc"] if o["dma"] is not None else 1)


def build(nlayers=DEPTH):
    nc = bass.Bass("TRN2", target_bir_lowering=False)
    S = Sched()

    def din(name, shape, dt=F32):
        return nc.dram_tensor(name, list(shape), dt, kind="ExternalInput").ap()

    xT = din("xT", [D, NT])
    gvec_d = din("gvec", [128, 160])
    bf_d = din("bf", [16, 2])
    rk_d = din("rk", [16, 1])
    consts_d = din("consts", [128, NCONST * 128], BF16)
    w_up = din("w_up", [DEPTH, D, DFF])
    w_down = din("w_down", [DEPTH, DFF, D])
    fox_w_in = din("fox_w_in", [2, D, 3 * D + NH])
    wf_d = din("wf", [2, 128, 256])
    fox_w_o = din("fox_w_o", [2, D, D])
    w_kv = din("w_kv", [D, 2 * D])
    sb_w_q = din("sb_w_q", [2, D, D])
    sb_w_o = din("sb_w_o", [2, D, D])
    yT = nc.dram_tensor("yT", [D, NR], F32, kind="ExternalOutput").ap()

    def dscr(name, shape, dt=BF16):
        return nc.dram_tensor(name, list(shape), dt, kind="Internal").ap()

    q_scr = dscr("q_scr", [NH, 128, NT])
    o_scr = dscr("o_scr", [NH, 128, NT])
    kt_src = [dscr(f"kt_src{g}", [512, NR]) for g in range(4)]
    v_src = [dscr(f"v_src{g}", [NR, 512]) for g in range(4)]
    kt_dst = [dscr(f"kt_dst{g}", [1024, NR]) for g in range(4)]
    v_dst = [dscr(f"v_dst{g}", [2 * NR, 512]) for g in range(4)]
    skt_dst = [dscr(f"skt_dst{g}", [1024, NR]) for g in range(4)]
    sv_dst = [dscr(f"sv_dst{g}", [2 * NR, 512]) for g in range(4)]
    g_src = dscr("g_src", [16, NR], F32)
    g_dst = dscr("g_dst", [32, NR], F32)
    ka_scr = dscr("ka_scr", [NH, 6, 2064])
    qa_scr = dscr("qa_scr", [NH, 6, NT])

    es = ExitStack()
    with es:
        def sb(name, shape, dt):
            return es.enter_context(nc.sbuf_tensor(name, list(shape), dt))

        hT = sb("hT", [128, NCH, NT], F32)
        arenaB = sb("arenaB", [128, NCH * NT], BF16)
        ring = [sb(f"ring{i}", [128, 8192], BF16) for i in range(3)]
        TRB = 22528
        arenaT = sb("arenaT", [128, TRB // 2], BF16)
        arenaC = sb("arenaC", [128, 28896 // 2], BF16)
        gvec = sb("gvec_sb", [128, 160], F32)
        consts = sb("consts_sb", [128, NCONST * 128], BF16)
        bfs = sb("bf_sb", [16, 2], F32)
        negb = sb("negb_sb", [16, 2], F32)
        rks = sb("rk_sb", [16, 1], F32)
        ones16 = sb("ones16", [16, 128], F32)
        ktm = sb("ktm", [128, NH, 16], BF16)
        vm = sb("vm", [16, D], BF16)
        wf_sb = sb("wf_sb", [128, NCH, 16], BF16)
        psb = [es.enter_context(nc.psum_tensor(f"ps{i}", [128, 512], F32)) for i in range(8)]

        def carve(arena, off, shape, dt, parts=128):
            esz = 2 if dt == BF16 else 4
            n = int(np.prod(shape))
            a = arena[0:parts, off // 2: off // 2 + n * esz // 2]
            if dt == F32:
                a = a.bitcast(F32)
            if len(shape) == 2:
                a = a.rearrange("p (a b) -> p a b", a=shape[0])
            return a

        def V(key, arena, aname, off, shape, dt, parts=128):
            esz = 2 if dt == BF16 else 4
            S.region(key, aname, off, off + int(np.prod(shape)) * esz)
            return carve(arena, off, shape, dt, parts)

        aT = V("aT", arenaB, "B", 0, [NCH, NT], BF16)

        def cst(idx, rows=128, cols=128):
            return consts[0:rows, idx * 128: idx * 128 + cols]

        def gcol(col):
            return gvec[:, col:col + 1]

        PSK = [("ps", i) for i in range(8)]
        G = [(0, 1, 2), (3, 4, 5)]

        sqv = [V("sq0", arenaT, "T", 0, [NT], BF16), V("sq1", arenaT, "T", 2080, [NT], BF16)]
        tmpf = V("tmpf", arenaT, "T", 4160, [NT], F32)
        rstd = V("rstd", arenaT, "T", 8320, [NT], F32)
        stg = [V("stg0", arenaT, "T", 12480, [NT], BF16), V("stg1", arenaT, "T", 14560, [NT], BF16)]
        vstg = [V("vstg0", arenaT, "T", 16640, [512], BF16), V("vstg1", arenaT, "T", 17664, [512], BF16)]
        e16 = V("e16", arenaT, "T", 0, [NT], F32, parts=16)
        nl16 = V("nl16", arenaT, "T", 4160, [NT], F32, parts=16)
        qpc = V("qpc", arenaT, "T", 0, [3, NT], BF16, parts=16)
        pt = [V(f"pt{i}", arenaT, "T", 1024 * i, [512], BF16) for i in range(3)]
        ef = [V(f"ef{i}", arenaT, "T", 3072 + 2048 * i, [512], F32) for i in range(2)]
        l16 = [V(f"l16{i}", arenaT, "T", 7168 + 1024 * i, [512], BF16) for i in range(2)]
        lsum = V("lsum", arenaT, "T", 9216, [512], BF16)
        rdv = V("rdv", arenaT, "T", 10240, [512], F32)
        ostg = stg
        gT = [V("gT0", arenaT, "T", 12480, [4, NT], BF16), V("gT1", arenaT, "T", 0, [4, NT], BF16)]
        rtmp = [V("rtmp0", arenaT, "T", 8320, [512], F32), V("rtmp1", arenaT, "T", 10368, [512], F32)]
        fstg = [V("fstg0", arenaT, "T", 12480, [NR], F32), V("fstg1", arenaT, "T", 16576, [NR], F32)]

        HB = 16480
        att = []
        for i in range(2):
            o = i * HB
            att.append(dict(
                qt=V(f"att{i}qt", arenaB, "B", o, [NT], BF16),
                kt=V(f"att{i}kt", arenaB, "B", o + 2080, [2048], BF16),
                vv=V(f"att{i}vv", arenaB, "B", o + 6176, [16, 128], BF16),
                qa=V(f"att{i}qa", arenaB, "B", o + 10272, [NT], BF16, parts=6),
                ka=V(f"att{i}ka", arenaB, "B", o + 12352, [2064], BF16, parts=6),
            ))
        S.region("gate", "C", 0, 28896)
        attS = []
        for i in range(2):
            o = i * 10272
            attS.append(dict(
                qt=V(f"attS{i}qt", arenaC, "C", o, [NT], BF16),
                kt=V(f"attS{i}kt", arenaC, "C", o + 2080, [2048], BF16),
                vv=V(f"attS{i}vv", arenaC, "C", o + 6176, [16, 128], BF16),
            ))
        qstg = V("qstg", arenaC, "C", 20544, [NT], BF16)
        nl_all = carve(arenaC, 0, [2064], F32, parts=16)
        csv = carve(arenaC, 8256, [2064], F32, parts=16)
        pcs = carve(arenaC, 16512, [3, 2064], BF16, parts=16)

        ring_state = {"i": 0, "issued": 0, "plan": None, "req": []}
        LOOKAHEAD = 2

        def _slab_dst(i, view):
            slot = i % 3
            if view == "col":
                return ring[slot][:].rearrange("p (c n) -> p c n", c=NCH)
            return ring[slot][:].rearrange("p (c n) -> p c n", c=4)

        def _issue_slab(i):
            src_ap, view = ring_state["plan"][i]
            dst = _slab_dst(i, view)
            S.add("pool", lambda e, dst=dst, src_ap=src_ap: e.dma_start(out=dst, in_=src_ap),
                  w=[("ring", i % 3)], dma=f"ring{i % 3}")

        def wslab(src_ap, view):
            i = ring_state["i"]
            ring_state["i"] += 1
            ring_state["req"].append((src_ap, view))
            plan = ring_state["plan"]
            if plan is None:
                dst = _slab_dst(i, view)
                S.add("pool", lambda e, dst=dst, src_ap=src_ap: e.dma_start(out=dst, in_=src_ap),
                      w=[("ring", i % 3)], dma=f"ring{i % 3}")
            else:
                upto = min(len(plan), i + LOOKAHEAD + 1)
                while ring_state["issued"] < upto:
                    _issue_slab(ring_state["issued"])
                    ring_state["issued"] += 1
            return _slab_dst(i, view), ("ring", i % 3)

        def colslab(wmat, c0, ncols=512):
            return wmat[:, c0:c0 + ncols].rearrange("(c p) n -> p c n", p=128)

        evac_ctr = {"i": 0}

        def evac_copy(dst, src, r, w, scale=None, eng=None):
            if eng is None:
                eng = "act" if evac_ctr["i"] % 2 == 0 else "dve"
                evac_ctr["i"] += 1
            if eng == "act":
                if scale is None:
                    S.add("act", lambda e: e.activation(out=dst, in_=src, func=AF.Copy), r=r, w=w)
                else:
                    S.add("act", lambda e: e.activation(out=dst, in_=src, func=AF.Copy, scale=float(scale)), r=r, w=w)
            else:
                if scale is None:
                    S.add("dve", lambda e: e.tensor_copy(out=dst, in_=src), r=r, w=w)
                else:
                    S.add("dve", lambda e: e.tensor_scalar(out=dst, in0=src, scalar1=float(scale), scalar2=None,
                                                           op0=ALU.mult), r=r, w=w)

        def proj_fm(slab, skey, col0, M, grp, rhs_tile=aT, rhs_key="aT"):
            banks = G[grp]

            def emit(e):
                ins = None
                for c in range(NCH):
                    for ti, (t0, n) in enumerate(TCH):
                        ins = e.matmul(psb[banks[ti]][0:M, 0:n], lhsT=slab[:, c, col0:col0 + M],
                                       rhs=rhs_tile[:, c, t0:t0 + n], start=(c == 0), stop=(c == NCH - 1))
                return ins
            S.add("pe", emit, r=[skey, rhs_key], w=[PSK[b] for b in banks])

        def prologue():
            xv = xT.rearrange("(c p) t -> p c t", p=128)
            for q4 in range(4):
                S.add("sp", lambda e, q4=q4: e.dma_start(out=hT[:, 4 * q4:4 * q4 + 4, :], in_=xv[:, 4 * q4:4 * q4 + 4, :]),
                      w=[("hT", c) for c in range(4 * q4, 4 * q4 + 4)], dma=f"ld_h{q4}")
            S.add("sp", lambda e: e.dma_start(out=gvec[:], in_=gvec_d), w=["gvec"], dma="ld_c0")
            S.add("sp", lambda e: e.dma_start(out=consts[:], in_=consts_d), w=["consts"], dma="ld_c1")
            S.add("sp", lambda e: e.dma_start(out=bfs[:], in_=bf_d), w=["bfs"], dma="ld_c2")
            S.add("sp", lambda e: e.dma_start(out=rks[:], in_=rk_d), w=["rks"], dma="ld_c3")
            S.add("dve", lambda e: e.tensor_scalar(out=negb[:], in0=bfs[:], scalar1=-1.0, scalar2=None, op0=ALU.mult),
                  r=["bfs"], w=["negb"])
            S.add("dve", lambda e: e.memset(ones16[:], 1.0), w=["ones16"])
            S.add("dve", lambda e: e.memset(pcs, 1.0), w=["gate"])
            S.add("sp", lambda e: e.dma_start(out=ka_scr[:, 0:3, :], in_=pcs), r=["gate"], w=["ka_ones"], dma="aug1a")
            S.add("sp", lambda e: e.dma_start(out=qa_scr[:, 3:6, :], in_=pcs[:, :, 0:NT]), r=["gate"], w=["qa_ones"],
                  dma="aug1b")

        HK = [("hT", c) for c in range(NCH)]

        def rms_stats():
            for c in range(NCH):
                sq = sqv[c % 2]
                if c % 2 == 0:
                    S.add("act", lambda e, sq=sq, c=c: e.activation(out=sq, in_=hT[:, c, :], func=AF.Square),
                          r=[("hT", c)], w=[f"sq{c % 2}"])
                else:
                    S.add("dve", lambda e, sq=sq, c=c: e.tensor_tensor(out=sq, in0=hT[:, c, :], in1=hT[:, c, :],
                                                                      op=ALU.mult),
                          r=[("hT", c)], w=[f"sq{c % 2}"])

                def emit(e, sq=sq, c=c):
                    ins = None
                    for ti, (t0, n) in enumerate(TCH):
                        ins = e.matmul(psb[ti][:, 0:n], lhsT=cst(C_ONES), rhs=sq[:, t0:t0 + n],
                                       start=(c == 0), stop=(c == NCH - 1))
                    return ins
                S.add("pe", emit, r=[f"sq{c % 2}", "consts"], w=[PSK[0], PSK[1], PSK[2]])

            def emit_sqrt(e):
                ins = None
                for ti, (t0, n) in enumerate(TCH):
                    ins = e.activation(out=tmpf[:, t0:t0 + n], in_=psb[ti][:, 0:n], func=AF.Sqrt,
                                       bias=EPS, scale=1.0 / D)
                return ins
            S.add("act", emit_sqrt, r=[PSK[0], PSK[1], PSK[2]], w=["tmpf"])
            S.add("dve", lambda e: e.reciprocal(out=rstd, in_=tmpf), r=["tmpf"], w=["rstd"])

        def rms_apply(col):
            for c in range(NCH):
                S.add("dve", lambda e, c=c: e.scalar_tensor_tensor(out=aT[:, c, :], in0=hT[:, c, :],
                                                                  scalar=gcol(col + c), in1=rstd,
                                                                  op0=ALU.mult, op1=ALU.mult),
                      r=[("hT", c), "rstd", "gvec"], w=["aT"])

        def proj_k_group(wmat, c0, g, hctr):
            slab, skey = wslab(colslab(wmat, c0), "col")
            for hh in range(4):
                h = g * 4 + hh
                grp = hctr[0] % 2
                hctr[0] += 1
                proj_fm(slab, skey, hh * 128, 128, grp)
                st = stg[h % 2]
                b = G[grp]
                for ti, (t0, n) in enumerate(TCH):
                    evac_copy(st[:, t0:t0 + n], psb[b[ti]][:, 0:n], r=[PSK[b[ti]]], w=[f"stg{h % 2}"])
                S.add("dve", lambda e, st=st, h=h: e.tensor_copy(out=ktm[:, h, :], in_=st[:, NR:NT]),
                      r=[f"stg{h % 2}"], w=["ktm"])
                S.add("sp", lambda e, st=st, g=g, hh=hh: e.dma_start(out=kt_src[g][hh * 128:(hh + 1) * 128, :],
                                                                     in_=st[:, 0:NR]),
                      r=[f"stg{h % 2}"], w=[("kt_src", g)], dma=f"kts{g}")

        def proj_v_group(wmat, c0, g, vctr):
            slab, skey = wslab(colslab(wmat, c0), "col")
            for tb in range(9):
                M = 128 if tb < 8 else 16
                bank = 6 + (vctr[0] % 2)
                vs = vstg[vctr[0] % 2]
                vk = f"vstg{vctr[0] % 2}"
                vctr[0] += 1

                def emit(e, tb=tb, M=M, bank=bank):
                    ins = None
                    for c in range(NCH):
                        ins = e.matmul(psb[bank][0:M, 0:512], lhsT=aT[:, c, tb * 128:tb * 128 + M],
                                       rhs=slab[:, c, :], start=(c == 0), stop=(c == NCH - 1))
                    return ins
                S.add("pe", emit, r=[skey, "aT"], w=[PSK[bank]])
                if tb < 8:
                    evac_copy(vs, psb[bank][:, 0:512], r=[PSK[bank]], w=[vk])
                    S.add("sp", lambda e, vs=vs, g=g, tb=tb: e.dma_start(out=v_src[g][tb * 128:(tb + 1) * 128, :],
                                                                         in_=vs),
                          r=[vk], w=[("v_src", g)], dma=f"vs{g}")
                else:
                    evac_copy(vm[0:16, g * 512:(g + 1) * 512], psb[bank][0:16, 0:512], r=[PSK[bank]], w=["vm"])

        def gather_kv(g, ktd, vd, kkey, vkey):
            S.add("pool", lambda e: e.collective_compute("AllGather", ALU.bypass, replica_groups=RG,
                                                         ins=[kt_src[g]], outs=[ktd[g]]),
                  r=[("kt_src", g)], w=[(kkey, g)], dma=f"cck{g}", inc=1)
            S.add("pool", lambda e: e.collective_compute("AllGather", ALU.bypass, replica_groups=RG,
                                                         ins=[v_src[g]], outs=[vd[g]]),
                  r=[("v_src", g)], w=[(vkey, g)], dma=f"ccv{g}", inc=1)

        def q_fillers(wmat):
            fl = []
            for g in range(1, 4):
                holder = {}
                for hh in range(4):
                    h = g * 4 + hh
                    for k in range(4):
                        def piece(g=g, hh=hh, k=k, holder=holder):
                            if "slab" not in holder:
                                holder["slab"], holder["skey"] = wslab(colslab(wmat, g * 512), "col")
                            slab, skey = holder["slab"], holder["skey"]

                            def emit(e):
                                ins = None
                                for c in range(4 * k, 4 * k + 4):
                                    for ti, (t0, n) in enumerate(TCH):
                                        ins = e.matmul(psb[5 + ti][:, 0:n], lhsT=slab[:, c, hh * 128:(hh + 1) * 128],
                                                       rhs=aT[:, c, t0:t0 + n], start=(c == 0), stop=(c == NCH - 1))
                                return ins
                            S.add("pe", emit, r=[skey, "aT"], w=[PSK[5], PSK[6], PSK[7]])
                        fl.append((h, piece))

                    def fin(h=h):
                        for ti, (t0, n) in enumerate(TCH):
                            evac_copy(qstg[:, t0:t0 + n], psb[5 + ti][:, 0:n], r=[PSK[5 + ti]], w=["qstg"],
                                      scale=QSCALE, eng="dve")
                        S.add("sp", lambda e: e.dma_start(out=q_scr[h], in_=qstg), r=["qstg"],
                              w=[("q_scr", h)], dma=f"qs{h % 4}")
                    fl.append((h, fin))
            return fl

        def proj_q(wmat, c0base, hctr, groups=(0, 1, 2, 3), evac_eng=None):
            for g in groups:
                slab, skey = wslab(colslab(wmat, c0base + g * 512), "col")
                for hh in range(4):
                    h = g * 4 + hh
                    grp = hctr[0] % 2
                    hctr[0] += 1
                    proj_fm(slab, skey, hh * 128, 128, grp)
                    st = stg[h % 2]
                    b = G[grp]
                    for ti, (t0, n) in enumerate(TCH):
                        evac_copy(st[:, t0:t0 + n], psb[b[ti]][:, 0:n], r=[PSK[b[ti]]], w=[f"stg{h % 2}"],
                                  scale=QSCALE, eng=evac_eng)
                    S.add("sp", lambda e, st=st, h=h: e.dma_start(out=q_scr[h], in_=st), r=[f"stg{h % 2}"],
                          w=[("q_scr", h)], dma=f"qs{h % 4}")

        def gate_phase(li, wmat, hctr):
            S.add("pool", lambda e: e.dma_start(out=wf_sb[:].rearrange("p c j -> p (c j)"), in_=wf_d[li]),
                  w=["wf"], dma="wf")
            grp = hctr[0] % 2
            hctr[0] += 1
            proj_fm(wf_sb, "wf", 0, 16, grp)
            b = G[grp]

            def emit_e(e):
                ins = None
                for ti, (t0, n) in enumerate(TCH):
                    ins = e.activation(out=e16[:, t0:t0 + n], in_=psb[b[ti]][0:16, 0:n], func=AF.Exp,
                                       bias=negb[:, li:li + 1], scale=-1.0)
                return ins
            S.add("act", emit_e, r=[PSK[x] for x in b] + ["negb"], w=["e16"])
            S.add("act", lambda e: e.activation(out=nl16, in_=e16, func=AF.Ln, bias=1.0, scale=1.0),
                  r=["e16"], w=["nl16"])
            S.add("sp", lambda e: e.dma_start(out=g_src, in_=nl16[:, 0:NR]), r=["nl16"], w=["g_src"], dma="gs")
            S.add("pool", lambda e: e.collective_compute("AllGather", ALU.bypass, replica_groups=RG,
                                                         ins=[g_src], outs=[g_dst]),
                  r=["g_src"], w=["g_dst"], dma="ccg", inc=1)

        def gate_phase2():
            S.add("sp", lambda e: e.dma_start(out=nl_all[:, 0:2048].rearrange("h (r t) -> h r t", r=2),
                                              in_=g_dst.rearrange("(r h) t -> h r t", r=2)),
                  r=["g_dst"], w=["gate"], dma="gl")
            S.add("dve", lambda e: e.tensor_copy(out=nl_all[:, 2048:2064], in_=nl16[:, NR:NT]),
                  r=["nl16", "gate"], w=["gate"])
            S.add("dve", lambda e: e.tensor_tensor_scan(out=csv[:, 2048:2064], data0=ones16[:, 0:16],
                                                        data1=nl_all[:, 2048:2064], initial=0.0,
                                                        op0=ALU.mult, op1=ALU.add),
                  r=["gate", "ones16"], w=["gate"])
            prev = csv[:, 2063:2064]
            for gk in range(16):
                r_, lb = gk % 2, gk // 2
                o = r_ * 1024 + lb * 128
                S.add("dve", lambda e, o=o, prev=prev: e.tensor_tensor_scan(out=csv[:, o:o + 128], data0=ones16[:],
                                                                            data1=nl_all[:, o:o + 128], initial=prev,
                                                                            op0=ALU.mult, op1=ALU.add),
                      r=["gate", "ones16"], w=["gate"])
                prev = csv[:, o + 127:o + 128]
            gk_ = dict(r=["gate"], w=["gate"])
            S.add("dve", lambda e: e.tensor_copy(out=pcs[:, 0, :], in_=csv), **gk_)
            S.add("dve", lambda e: e.tensor_tensor(out=nl_all, in0=csv, in1=pcs[:, 0, :], op=ALU.subtract), **gk_)
            S.add("dve", lambda e: e.tensor_copy(out=pcs[:, 1, :], in_=nl_all), **gk_)
            S.add("dve", lambda e: e.tensor_tensor(out=csv, in0=nl_all, in1=pcs[:, 1, :], op=ALU.subtract), **gk_)
            S.add("dve", lambda e: e.tensor_copy(out=pcs[:, 2, :], in_=csv), **gk_)
            for j in range(3):
                S.add("dve", lambda e, j=j: e.tensor_tensor(out=nl_all[:, 0:NR], in0=pcs[:, j, 0:NR],
                                                            in1=pcs[:, j, NR:2 * NR], op=ALU.subtract), **gk_)
                S.add("dve", lambda e, j=j: e.scalar_tensor_tensor(out=qpc[:, j, 0:NR], in0=nl_all[:, 0:NR],
                                                                   scalar=rks[:, 0:1], in1=pcs[:, j, 0:NR],
                                                                   op0=ALU.mult, op1=ALU.subtract),
                      r=["gate", "rks"], w=["qpc"])
                S.add("dve", lambda e, j=j: e.tensor_scalar(out=qpc[:, j, NR:NT], in0=pcs[:, j, 2048:2064],
                                                            scalar1=-1.0, scalar2=None, op0=ALU.mult),
                      r=["gate"], w=["qpc"])

        def gate_phase3():
            S.add("sp", lambda e: e.dma_start(out=ka_scr[:, 3:6, :], in_=pcs), r=["gate"], w=["ka_scr"], dma="aug")
            S.add("sp", lambda e: e.dma_start(out=qa_scr[:, 0:3, :], in_=qpc), r=["qpc"], w=["qa_scr"], dma="aug")

        def load_head(h, fox, ktd, vd, kkey, vkey):
            g, hh = h // 4, h % 4
            A = (att if fox else attS)[h % 2]
            i = h % 2
            pre = "att" if fox else "attS"
            S.add("sp", lambda e: e.dma_start(out=A["qt"], in_=q_scr[h]), r=[("q_scr", h)], w=[f"{pre}{i}qt"],
                  dma=f"lq{i}")
            S.add("sp", lambda e: e.dma_start(
                out=A["kt"].rearrange("p (r t) -> p r t", r=2),
                in_=ktd[g].rearrange("(r x) t -> x r t", r=2)[hh * 128:(hh + 1) * 128]),
                r=[(kkey, g)], w=[f"{pre}{i}kt"], dma=f"lk{i}")
            S.add("sp", lambda e: e.dma_start(
                out=A["vv"],
                in_=vd[g].rearrange("(b p) n -> p b n", p=128)[:, :, hh * 128:(hh + 1) * 128]),
                r=[(vkey, g)], w=[f"{pre}{i}vv"], dma=f"lv{i}")
            if fox:
                S.add("sp", lambda e: e.dma_start(out=A["qa"], in_=qa_scr[h]), r=["qa_scr", "qa_ones"],
                      w=[f"att{i}qa"], dma=f"lqa{i}")
                S.add("sp", lambda e: e.dma_start(out=A["ka"], in_=ka_scr[h]), r=["ka_scr", "ka_ones"],
                      w=[f"att{i}ka"], dma=f"lka{i}")

        sctr = [0]
        pctr = [0]
        octr = [0]
        lctr = [0]
        ectr = [0]

        def run_pipeline(items, nstages, lag, tick=None, every=5):
            n = len(items)
            for s in range(n + lag * (nstages - 1)):
                for k in range(nstages):
                    t = s - k * lag
                    if 0 <= t < n:
                        it = items[t]
                        it["stages"][k]()
                        if k == nstages - 1 and it.get("post"):
                            it["post"]()
                if tick is not None and s % every == every - 1:
                    tick()

        def fox_items(h, items, after_head):
            A = att[h % 2]
            i = h % 2
            qt, kt, vv, qa, ka = A["qt"], A["kt"], A["vv"], A["qa"], A["ka"]
            rk_ = [f"att{i}qt", f"att{i}kt", f"att{i}qa", f"att{i}ka", "consts", "ktm"]
            os_ = ostg[h % 2]
            osk = f"stg{h % 2}"

            def tile(kl, kal, vl, krows, c0, a0, N, neg, first, last, ob, db, post=None):
                sb_ = sctr[0] % 3
                sctr[0] += 1
                pi = pctr[0] % 3
                pctr[0] += 1
                P = pt[pi]

                def emit_s(e):
                    e.matmul(psb[sb_][0:krows, 0:N], lhsT=kl, rhs=qt[:, c0 + a0:c0 + a0 + N], start=True, stop=False)
                    ins = e.matmul(psb[sb_][0:krows, 0:N], lhsT=kal, rhs=qa[:, c0 + a0:c0 + a0 + N],
                                   start=False, stop=(neg is None))
                    if neg is not None:
                        ncols = min(128, N)
                        ins = e.matmul(psb[sb_][0:krows, 0:ncols], lhsT=cst(C_IDENT, krows, krows),
                                       rhs=neg[0:krows, 0:ncols], start=False, stop=True)
                    return ins

                def emit_o(e):
                    e.matmul(psb[ob][:, a0:a0 + N], lhsT=vl, rhs=P[0:krows, 0:N], start=first, stop=last)
                    return e.matmul(psb[db][:, a0:a0 + N], lhsT=cst(C_ONES, krows, 128), rhs=P[0:krows, 0:N],
                                    start=first, stop=last)

                def stA():
                    S.add("pe", emit_s, r=rk_, w=[PSK[sb_]])
                    S.add("act", lambda e: e.activation(out=P[0:krows, 0:N], in_=psb[sb_][0:krows, 0:N],
                                                        func=AF.Exp), r=[PSK[sb_]], w=[f"pt{pi}"])

                def stB():
                    S.add("pe", emit_o, r=[f"pt{pi}", f"att{i}vv", "vm", "consts"], w=[PSK[ob], PSK[db]])
                items.append(dict(stages=[stA, stB], post=post))

            def finish(c0, n, ob, db):
                S.add("dve", lambda e: e.reciprocal(out=rdv[:, 0:n], in_=psb[db][:, 0:n]), r=[PSK[db]], w=["rdv"])
                S.add("dve", lambda e: e.tensor_tensor(out=os_[:, c0:c0 + n], in0=psb[ob][:, 0:n], in1=rdv[:, 0:n],
                                                       op=ALU.mult), r=[PSK[ob], "rdv"], w=[osk])

            for qtile in range(2):
                c0 = qtile * 512
                lb0 = qtile * 4
                ob = 3 + (octr[0] % 2)
                db = 5 + (octr[0] % 2)
                octr[0] += 1
                tile(ktm[:, h, :], ka[:, 2048:2064], vm[0:16, h * 128:(h + 1) * 128], 16, c0, 0, 512, None,
                     True, False, ob, db)
                nblk = lb0 + 4
                for lbk in range(nblk):
                    for r_ in range(2):
                        j0 = max(lbk, lb0)
                        a0 = (j0 - lb0) * 128
                        N = 512 - a0
                        kc = r_ * 1024 + lbk * 128
                        neg = cst(C_FNEG0 + r_) if lbk >= lb0 else None
                        lastt = (lbk == nblk - 1 and r_ == 1)
                        post = (lambda c0=c0, ob=ob, db=db: finish(c0, 512, ob, db)) if lastt else None
                        tile(kt[:, kc:kc + 128], ka[:, kc:kc + 128], vv[:, r_ * 8 + lbk, :], 128, c0, a0, N, neg,
                             False, lastt, ob, db, post)
            ob = 3 + (octr[0] % 2)
            db = 5 + (octr[0] % 2)
            octr[0] += 1

            def post_head(ob=ob, db=db):
                finish(NR, 16, ob, db)
                S.add("sp", lambda e: e.dma_start(out=o_scr[h], in_=os_), r=[osk], w=["o_scr"], dma="os")
                after_head(h)
            tile(ktm[:, h, :], ka[:, 2048:2064], vm[0:16, h * 128:(h + 1) * 128], 16, NR, 0, 16, cst(C_CNEG),
                 True, True, ob, db, post_head)

        def sb_items(h, items, after_head):
            A = attS[h % 2]
            i = h % 2
            qt, kt, vv = A["qt"], A["kt"], A["vv"]
            rk_ = [f"attS{i}qt", f"attS{i}kt", "consts", "ktm"]
            os_ = ostg[h % 2]
            osk = f"stg{h % 2}"

            def tile(kl, vl, krows, c0, a0, N, neg, lsum_cols, ob, last, zero_n=None, post=None):
                zb = sctr[0] % 3
                sctr[0] += 1
                pi = pctr[0] % 3
                pctr[0] += 1
                li_ = lctr[0] % 2
                lctr[0] += 1
                ei = ectr[0] % 2
                ectr[0] += 1
                P, L, E = pt[pi], l16[li_], ef[ei]

                def emit_z(e):
                    ins = e.matmul(psb[zb][0:krows, 0:N], lhsT=kl, rhs=qt[:, c0 + a0:c0 + a0 + N],
                                   start=True, stop=False)
                    if neg is not None:
                        ncols = min(128, N)
                        ins = e.matmul(psb[zb][0:krows, 0:ncols], lhsT=cst(C_IDENT, krows, krows),
                                       rhs=neg[0:krows, 0:ncols], start=False, stop=False)
                    return ins

                def emit_a(e):
                    have = lsum_cols is not None
                    ins = e.matmul(psb[zb][0:krows, 0:N], lhsT=cst(C_NTRI, krows, krows), rhs=L[0:krows, 0:N],
                                   start=False, stop=not have)
                    if have:
                        x0, n = lsum_cols
                        ins = e.matmul(psb[zb][0:krows, x0 - a0:x0 - a0 + n], lhsT=cst(C_NONES, 128, krows),
                                       rhs=lsum[:, x0:x0 + n], start=False, stop=True)
                    return ins

                def emit_l(e):
                    ins = None
                    if lsum_cols is None or lsum_cols[0] > a0:
                        ins = e.tensor_copy(out=lsum[:, a0:a0 + 128], in_=L[:, 0:128])
                    if lsum_cols is not None:
                        x0, n = lsum_cols
                        ins = e.tensor_tensor(out=lsum[:, x0:x0 + n], in0=lsum[:, x0:x0 + n],
                                              in1=L[:, x0 - a0:x0 - a0 + n], op=ALU.add)
                    return ins

                def emit_o(e):
                    if zero_n is not None:
                        e.matmul(psb[ob][:, 0:zero_n], lhsT=cst(C_ZERO), rhs=qt[:, c0:c0 + zero_n],
                                 start=True, stop=False, skip_group_check=True)
                    return e.matmul(psb[ob][:, a0:a0 + N], lhsT=vl, rhs=P[0:krows, 0:N],
                                    start=False, stop=last, skip_group_check=True)

                def stA():
                    S.add("pe", emit_z, r=rk_, w=[PSK[zb]])
                    S.add("act", lambda e: e.activation(out=E[0:krows, 0:N], in_=psb[zb][0:krows, 0:N],
                                                        func=AF.Exp), r=[PSK[zb]], w=[f"ef{ei}"])
                    S.add("act", lambda e: e.activation(out=L[0:krows, 0:N], in_=E[0:krows, 0:N], func=AF.Ln,
                                                        bias=1.0, scale=1.0), r=[f"ef{ei}"], w=[f"l16{li_}"])

                def stB():
                    S.add("pe", emit_a, r=[f"l16{li_}", "lsum", "consts"], w=[PSK[zb]])
                    if krows == 128 and not last:
                        S.add("dve", emit_l, r=[f"l16{li_}"], w=["lsum"])
                    S.add("act", lambda e: e.activation(out=P[0:krows, 0:N], in_=psb[zb][0:krows, 0:N],
                                                        func=AF.Exp), r=[PSK[zb]], w=[f"pt{pi}"])

                def stC():
                    S.add("pe", emit_o, r=[f"pt{pi}", f"attS{i}vv", f"attS{i}qt", "vm", "consts"], w=[PSK[ob]])
                items.append(dict(stages=[stA, stB, stC], post=post))

            for qtile in range(2):
                c0 = qtile * 512
                lb0 = qtile * 4
                ob = 3 + (octr[0] % 2)
                octr[0] += 1
                have_from = None
                firstt = True
                for lbk in range(lb0 + 3, -1, -1):
                    for r_ in (1, 0):
                        j0 = max(lbk, lb0)
                        a0 = (j0 - lb0) * 128
                        N = 512 - a0
                        kc = r_ * 1024 + lbk * 128
                        neg = cst(C_SNEG0 + r_) if lbk >= lb0 else None
                        lc = None if have_from is None else (have_from, 512 - have_from)
                        tile(kt[:, kc:kc + 128], vv[:, r_ * 8 + lbk, :], 128, c0, a0, N, neg, lc, ob, False,
                             zero_n=(512 if firstt else None))
                        firstt = False
                        have_from = a0

                def post_q(ob=ob, c0=c0):
                    S.add("dve", lambda e: e.tensor_copy(out=os_[:, c0:c0 + 512], in_=psb[ob][:, 0:512]),
                          r=[PSK[ob]], w=[osk])
                tile(ktm[:, h, :], vm[0:16, h * 128:(h + 1) * 128], 16, c0, 0, 512, None, (0, 512), ob, True,
                     post=post_q)
            ob = 3 + (octr[0] % 2)
            octr[0] += 1

            def post_head(ob=ob):
                S.add("dve", lambda e: e.tensor_copy(out=os_[:, NR:NT], in_=psb[ob][:, 0:16]), r=[PSK[ob]], w=[osk])
                S.add("sp", lambda e: e.dma_start(out=o_scr[h], in_=os_), r=[osk], w=["o_scr"], dma="os")
                after_head(h)
            tile(ktm[:, h, :], vm[0:16, h * 128:(h + 1) * 128], 16, NR, 0, 16, cst(C_SCNEG), None, ob, True,
                 zero_n=16, post=post_head)

        def attention(fox, ktd, vd, kkey, vkey, wq=None):
            load_head(0, fox, ktd, vd, kkey, vkey)
            load_head(1, fox, ktd, vd, kkey, vkey)
            fillers = q_fillers(wq) if wq is not None else []
            fstate = [0]

            def flush(hmax):
                while fstate[0] < len(fillers) and fillers[fstate[0]][0] <= hmax:
                    fillers[fstate[0]][1]()
                    fstate[0] += 1

            def tick():
                if fstate[0] < len(fillers):
                    fillers[fstate[0]][1]()
                    fstate[0] += 1

            def after_head(h):
                if h + 2 < NH:
                    flush(h + 2)
                    load_head(h + 2, fox, ktd, vd, kkey, vkey)
            items = []
            for h in range(NH):
                if fox:
                    fox_items(h, items, after_head)
                else:
                    sb_items(h, items, after_head)
            if fox:
                run_pipeline(items, 2, 2)
            else:
                run_pipeline(items, 3, 1, tick=tick if fillers else None, every=5)
                flush(NH)

        def add_to_h(n, grp):
            b = G[grp]
            for ti, (t0, nn) in enumerate(TCH):
                S.add("dve", lambda e, ti=ti, t0=t0, nn=nn: e.tensor_tensor(out=hT[:, n, t0:t0 + nn],
                                                                           in0=psb[b[ti]][:, 0:nn],
                                                                           in1=hT[:, n, t0:t0 + nn], op=ALU.add),
                      r=[PSK[b[ti]], ("hT", n)], w=[("hT", n)])

        def oproj(wmat, hctr):
            S.add("sp", lambda e: e.dma_start(out=aT, in_=o_scr.rearrange("h p t -> p h t")), r=["o_scr"], w=["aT"],
                  dma="lo")
            for s in range(4):
                slab, skey = wslab(colslab(wmat, s * 512), "col")
                for nn in range(4):
                    n = s * 4 + nn
                    grp = hctr[0] % 2
                    hctr[0] += 1
                    proj_fm(slab, skey, nn * 128, 128, grp)
                    add_to_h(n, grp)

        def mlp_up(li, fg, hctr):
            slab, skey = wslab(colslab(w_up[li], fg * 512), "col")
            gt = gT[fg % 2]
            gk = f"gT{fg % 2}"

            def up_chunk(fc):
                grp = hctr[0] % 2
                hctr[0] += 1
                proj_fm(slab, skey, fc * 128, 128, grp)
                b = G[grp]

                def evac(ti, t0, n):
                    rt = rtmp[ti % 2]
                    rkk = f"rtmp{ti % 2}"
                    S.add("act", lambda e: e.activation(out=rt[:, 0:n], in_=psb[b[ti]][:, 0:n], func=AF.Relu),
                          r=[PSK[b[ti]]], w=[rkk])
                    S.add("act", lambda e: e.activation(out=gt[:, fc, t0:t0 + n], in_=rt[:, 0:n], func=AF.Square),
                          r=[rkk], w=[gk])
                for ti, (t0, n) in enumerate(TCH):
                    evac(ti, t0, n)
            for fc in range(4):
                up_chunk(fc)

        def mlp_down(li, fg, hctr):
            gt = gT[fg % 2]
            gk = f"gT{fg % 2}"
            dslab, dkey = wslab(w_down[li, fg * 512:(fg + 1) * 512, :].rearrange("(c p) n -> p c n", p=128), "row")

            def down_chunk(n):
                grp = hctr[0] % 2
                hctr[0] += 1
                b = G[grp]

                def emit(e):
                    ins = None
                    for fc in range(4):
                        for ti, (t0, nn) in enumerate(TCH):
                            ins = e.matmul(psb[b[ti]][:, 0:nn], lhsT=dslab[:, fc, n * 128:(n + 1) * 128],
                                           rhs=gt[:, fc, t0:t0 + nn], start=(fc == 0), stop=(fc == 3))
                    return ins
                S.add("pe", emit, r=[dkey, gk], w=[PSK[x] for x in b])
                add_to_h(n, grp)
            for n in range(NCH):
                down_chunk(n)

        def mlp(li, hctr):
            pend = None
            for fg in range(16):
                mlp_up(li, fg, hctr)
                if pend is not None:
                    mlp_down(li, pend, hctr)
                pend = fg
            mlp_down(li, pend, hctr)

        def program():
            prologue()
            hctr = [0]
            vctr = [0]
            for li in range(nlayers):
                rms_stats()
                if li < 2:
                    wmat = fox_w_in[li]
                    rms_apply(li * 16)
                    for g in range(4):
                        proj_k_group(wmat, D + g * 512, g, hctr)
                        proj_v_group(wmat, 2 * D + g * 512, g, vctr)
                        gather_kv(g, kt_dst, v_dst, "kt_dst", "v_dst")
                    gate_phase(li, wmat, hctr)
                    gate_phase2()
                    proj_q(wmat, 0, hctr, evac_eng="act")
                    gate_phase3()
                    attention(True, kt_dst, v_dst, "kt_dst", "v_dst")
                    oproj(fox_w_o[li], hctr)
                else:
                    if li == 2:
                        rms_apply(128)
                        for g in range(4):
                            proj_k_group(w_kv, g * 512, g, hctr)
                            proj_v_group(w_kv, D + g * 512, g, vctr)
                            gather_kv(g, skt_dst, sv_dst, "skt_dst", "sv_dst")
                    rms_apply(li * 16)
                    proj_q(sb_w_q[li - 2], 0, hctr, groups=(0,))
                    attention(False, skt_dst, sv_dst, "skt_dst", "sv_dst", wq=sb_w_q[li - 2])
                    oproj(sb_w_o[li - 2], hctr)
                rms_stats()
                rms_apply(64 + li * 16)
                mlp(li, hctr)

            rms_stats()
            outs = []
            for c in range(NCH):
                fs = fstg[c % 2]
                S.add("dve", lambda e, c=c, fs=fs: e.scalar_tensor_tensor(out=fs, in0=hT[:, c, 0:NR],
                                                                          scalar=gcol(144 + c), in1=rstd[:, 0:NR],
                                                                          op0=ALU.mult, op1=ALU.mult),
                      r=[("hT", c), "rstd", "gvec"], w=[f"fstg{c % 2}"])
                S.add("sp", lambda e, c=c, fs=fs: e.dma_start(out=yT[c * 128:(c + 1) * 128, :], in_=fs),
                      r=[f"fstg{c % 2}"], w=[("yT", c)], dma="st_y")
            S.add("sp", None, r=[("yT", c) for c in range(NCH)])


        def reset_counters():
            ring_state.update(i=0, issued=0, req=[])
            for ctr in (sctr, pctr, octr, lctr, ectr):
                ctr[0] = 0
            evac_ctr["i"] = 0

        S_real = S
        S = Sched()
        S.alias, S.regions = S_real.alias, S_real.regions
        reset_counters()
        program()
        ring_state["plan"] = list(ring_state["req"])
        S = Sched()
        S.alias, S.regions = S_real.alias, S_real.regions
        reset_counters()
        program()

        S.finalize()
        sems = {}
        for idx, k in enumerate(sorted(S.semkeys, key=str)):
            sems[k] = es.enter_context(nc.semaphore(f"s{idx}"))
        with nc.Block() as block:
            @block.tensor
            def _(e):
                S.emit_engine("pe", e, sems)

            @block.scalar
            def _(e):
                S.emit_engine("act", e, sems)

            @block.vector
            def _(e):
                S.emit_engine("dve", e, sems)

            @block.gpsimd
            def _(e):
                S.emit_engine("pool", e, sems)

            @block.sync
            def _(e):
                S.emit_engine("sp", e, sems)
    return nc


def make_consts(rank):
    i = np.arange(128)[:, None]
    j = np.arange(128)[None, :]
    ones = np.ones((128, 128), np.float32)
    zeros = np.zeros((128, 128), np.float32)
    ntri = np.where(i >= j, -1.0, 0.0).astype(np.float32)
    ident = np.eye(128, dtype=np.float32)
    cneg = np.where(i <= j, 0.0, NEGV).astype(np.float32)
    scneg = np.where(i < j, 0.0, NEGV).astype(np.float32)
    allneg = np.full((128, 128), NEGV, np.float32)
    if rank == 0:
        fneg0, fneg1, sneg0, sneg1 = cneg, allneg, scneg, allneg
    else:
        fneg0, fneg1, sneg0, sneg1 = zeros, cneg, zeros, scneg
    blocks = [ones, ntri, -ones, ident, zeros, fneg0, fneg1, sneg0, sneg1, cneg, scneg]
    return np.concatenate(blocks, axis=1).astype(ml_dtypes.bfloat16)


def prep_inputs(x, meta_tokens, norm_attn, norm_mlp, w_up, w_down, fox_w_in, fox_b_f, fox_w_o,
                kv_norm, w_kv, sb_w_q, sb_w_o, final_norm):
    f = lambda a: np.ascontiguousarray(np.asarray(a, dtype=np.float32))
    x = f(x)
    meta = f(meta_tokens)
    gv = np.zeros((128, 160), np.float32)
    na, nm = f(norm_attn), f(norm_mlp)
    for l in range(DEPTH):
        gv[:, l * 16:(l + 1) * 16] = na[l].reshape(16, 128).T
        gv[:, 64 + l * 16:64 + (l + 1) * 16] = nm[l].reshape(16, 128).T
    gv[:, 128:144] = f(kv_norm).reshape(16, 128).T
    gv[:, 144:160] = f(final_norm).reshape(16, 128).T
    shared = dict(w_up=f(w_up), w_down=f(w_down), fox_w_in=f(fox_w_in), fox_w_o=f(fox_w_o), w_kv=f(w_kv),
                  sb_w_q=f(sb_w_q), sb_w_o=f(sb_w_o), bf=np.ascontiguousarray(f(fox_b_f).T))
    fwi = shared["fox_w_in"]
    shared["wf"] = np.ascontiguousarray(
        fwi[:, :, 3 * D:].reshape(2, NCH, 128, NH).transpose(0, 2, 1, 3).reshape(2, 128, NCH * NH))
    in_maps = []
    for c in range(8):
        b, r = c // 2, c % 2
        xb = x[b].reshape(16, 128, D)[r::2].reshape(NR, D)
        xt = np.ascontiguousarray(np.concatenate([xb, meta], axis=0).T)
        m = dict(shared)
        m["xT"] = xt
        m["gvec"] = gv
        m["rk"] = np.full((16, 1), float(r), np.float32)
        m["consts"] = make_consts(r)
        in_maps.append(m)
    return in_maps


def assemble(results):
    out = np.zeros((BATCH, SEQ, D), np.float32)
    for c in range(8):
        b, r = c // 2, c % 2
        y = np.asarray(results[c]["yT"], dtype=np.float32).T
        out[b].reshape(16, 128, D)[r::2] = y.reshape(8, 128, D)
    return out


_NC_CACHE = {}


def kernel(**inputs):
    in_maps = prep_inputs(**inputs)
    if "nc" not in _NC_CACHE:
        _NC_CACHE["nc"] = build(DEPTH)
    res = run_bass_kernel_spmd(_NC_CACHE["nc"], in_maps, core_ids=list(range(8)))
    return assemble(res.results)
```

```python
from contextlib import ExitStack

import numpy as np
import ml_dtypes
import concourse.bass as bass
import concourse.mybir as mybir
from concourse.bass_utils import run_bass_kernel_spmd

F32 = mybir.dt.float32
BF16 = mybir.dt.bfloat16
AF = mybir.ActivationFunctionType
ALU = mybir.AluOpType

D = 2048
NH = 16
DH = 128
DFF = 8192
NMETA = 16
SEQ = 2048
BATCH = 4
DEPTH = 4
NCH = 16
NT = 1040
NR = 1024
TCH = [(0, 347), (347, 347), (694, 346)]
EPS = 1e-6
QSCALE = DH ** -0.5
NEGV = -30000.0
RG = [[0, 1], [2, 3], [4, 5], [6, 7]]
EPOCH = 30000
ENGS = ["pe", "act", "dve", "pool", "sp"]

C_ONES, C_NTRI, C_NONES, C_IDENT, C_ZERO, C_FNEG0, C_FNEG1, C_SNEG0, C_SNEG1, C_CNEG, C_SCNEG = range(11)
NCONST = 11


class Sched:
    def __init__(self):
        self.ops = []
        self.lastw = {}
        self.readers = {}
        self.alias = {}

    def region(self, key, arena, start, end):
        self.regions = getattr(self, "regions", {})
        for k2, (a2, s2, e2) in self.regions.items():
            if a2 == arena and s2 < end and start < e2 and k2 != key:
                self.alias.setdefault(key, set()).add(k2)
                self.alias.setdefault(k2, set()).add(key)
        self.regions[key] = (arena, start, end)

    def _keys(self, k):
        return [k] + list(self.alias.get(k, ()))

    def add(self, eng, emit, r=(), w=(), dma=None, inc=16):
        i = len(self.ops)
        hard = set()
        soft = set()
        for k in r:
            for kk in self._keys(k):
                j = self.lastw.get(kk)
                if j is not None:
                    hard.add(j)
        for k in w:
            for kk in self._keys(k):
                j = self.lastw.get(kk)
                if j is not None:
                    hard.add(j)
                for j in self.readers.get(kk, ()):
                    soft.add(j)
        self.ops.append(dict(eng=eng, emit=emit, hard=hard, soft=soft - hard, dma=dma, inc=inc))
        for k in r:
            self.readers.setdefault(k, []).append(i)
        for k in w:
            self.lastw[k] = i
            self.readers[k] = []
        return i

    def finalize(self):
        ops = self.ops
        self.eng_ops = {e: [] for e in ENGS}
        for i, o in enumerate(ops):
            self.eng_ops[o["eng"]].append(i)
        for i, o in enumerate(ops):
            deps = {}
            for j in o["hard"] | o["soft"]:
                pj = ops[j]
                is_soft = j not in o["hard"]
                if pj["dma"] is None:
                    if pj["eng"] == o["eng"] and o["dma"] is None:
                        if o["eng"] == "pe" or is_soft:
                            continue
                    key = ("c", pj["eng"])
                else:
                    key = ("d", pj["dma"])
                if key not in deps or deps[key] < j:
                    deps[key] = j
            o["deps"] = sorted(deps.values())
            for j in o["deps"]:
                ops[j]["needed"] = True
        cnt = {}
        dcnt = {}
        self.semkeys = set()
        for o in ops:
            if o["dma"] is None:
                if o.get("needed"):
                    e = o["eng"]
                    cnt[e] = cnt.get(e, 0) + 1
                    epoch, val = divmod(cnt[e] - 1, EPOCH)
                    o["tok"] = (("c", e, epoch), val + 1)
                    self.semkeys.add(o["tok"][0])
            else:
                k = o["dma"]
                dcnt[k] = dcnt.get(k, 0) + o["inc"]
                o["tok"] = (("d", k), dcnt[k])
                self.semkeys.add(o["tok"][0])

    def emit_engine(self, e, engobj, sems):
        ops = self.ops
        waited = {}
        for i in self.eng_ops[e]:
            o = ops[i]
            for j in o["deps"]:
                semkey, val = ops[j]["tok"]
                if waited.get(semkey, 0) < val:
                    engobj.wait_ge(sems[semkey], val)
                    waited[semkey] = val
            if o["emit"] is not None:
                ins = o["emit"](engobj)
                if "tok" in o:
                    ins.then_inc(sems[o["tok"][0]], o["inc"] if o["dma"] is not None else 1)


def build(nlayers=DEPTH):
    nc = bass.Bass("TRN2", target_bir_lowering=False)
    S = Sched()

    def din(name, shape, dt=F32):
        return nc.dram_tensor(name, list(shape), dt, kind="ExternalInput").ap()

    xT = din("xT", [D, NT])
    gvec_d = din("gvec", [128, 160])
    bf_d = din("bf", [16, 2])
    rk_d = din("rk", [16, 1])
    consts_d = din("consts", [128, NCONST * 128], BF16)
    w_up = din("w_up", [DEPTH, D, DFF])
    w_down = din("w_down", [DEPTH, DFF, D])
    fox_w_in = din("fox_w_in", [2, D, 3 * D + NH])
    wf_d = din("wf", [2, 128, 256])
    fox_w_o = din("fox_w_o", [2, D, D])
    w_kv = din("w_kv", [D, 2 * D])
    sb_w_q = din("sb_w_q", [2, D, D])
    sb_w_o = din("sb_w_o", [2, D, D])
    yT = nc.dram_tensor("yT", [D, NR], F32, kind="ExternalOutput").ap()

    def dscr(name, shape, dt=BF16):
        return nc.dram_tensor(name, list(shape), dt, kind="Internal").ap()

    q_scr = dscr("q_scr", [NH, 128, NT])
    o_scr = dscr("o_scr", [NH, 128, NT])
    kt_src = [dscr(f"kt_src{g}", [512, NR]) for g in range(4)]
    v_src = [dscr(f"v_src{g}", [NR, 512]) for g in range(4)]
    kt_dst = [dscr(f"kt_dst{g}", [1024, NR]) for g in range(4)]
    v_dst = [dscr(f"v_dst{g}", [2 * NR, 512]) for g in range(4)]
    skt_dst = [dscr(f"skt_dst{g}", [1024, NR]) for g in range(4)]
    sv_dst = [dscr(f"sv_dst{g}", [2 * NR, 512]) for g in range(4)]
    g_src = dscr("g_src", [16, NR], F32)
    g_dst = dscr("g_dst", [32, NR], F32)
    ka_scr = dscr("ka_scr", [NH, 6, 2064])
    qa_scr = dscr("qa_scr", [NH, 6, NT])

    es = ExitStack()
    with es:
        def sb(name, shape, dt):
            return es.enter_context(nc.sbuf_tensor(name, list(shape), dt))

        hT = sb("hT", [128, NCH, NT], F32)
        arenaB = sb("arenaB", [128, NCH * NT], BF16)
        ring = [sb(f"ring{i}", [128, 8192], BF16) for i in range(3)]
        TRB = 22528
        arenaT = sb("arenaT", [128, TRB // 2], BF16)
        arenaC = sb("arenaC", [128, 28896 // 2], BF16)
        gvec = sb("gvec_sb", [128, 160], F32)
        consts = sb("consts_sb", [128, NCONST * 128], BF16)
        bfs = sb("bf_sb", [16, 2], F32)
        negb = sb("negb_sb", [16, 2], F32)
        rks = sb("rk_sb", [16, 1], F32)
        ones16 = sb("ones16", [16, 128], F32)
        ktm = sb("ktm", [128, NH, 16], BF16)
        vm = sb("vm", [16, D], BF16)
        wf_sb = sb("wf_sb", [128, NCH, 16], BF16)
        psb = [es.enter_context(nc.psum_tensor(f"ps{i}", [128, 512], F32)) for i in range(8)]

        def carve(arena, off, shape, dt, parts=128):
            esz = 2 if dt == BF16 else 4
            n = int(np.prod(shape))
            a = arena[0:parts, off // 2: off // 2 + n * esz // 2]
            if dt == F32:
                a = a.bitcast(F32)
            if len(shape) == 2:
                a = a.rearrange("p (a b) -> p a b", a=shape[0])
            return a

        def V(key, arena, aname, off, shape, dt, parts=128):
            esz = 2 if dt == BF16 else 4
            S.region(key, aname, off, off + int(np.prod(shape)) * esz)
            return carve(arena, off, shape, dt, parts)

        aT = V("aT", arenaB, "B", 0, [NCH, NT], BF16)

        def cst(idx, rows=128, cols=128):
            return consts[0:rows, idx * 128: idx * 128 + cols]

        def gcol(col):
            return gvec[:, col:col + 1]

        PSK = [("ps", i) for i in range(8)]
        G = [(0, 1, 2), (3, 4, 5)]

        sqv = [V("sq0", arenaT, "T", 0, [NT], BF16), V("sq1", arenaT, "T", 2080, [NT], BF16)]
        tmpf = V("tmpf", arenaT, "T", 4160, [NT], F32)
        rstd = V("rstd", arenaT, "T", 8320, [NT], F32)
        stg = [V("stg0", arenaT, "T", 12480, [NT], BF16), V("stg1", arenaT, "T", 14560, [NT], BF16)]
        vstg = [V("vstg0", arenaT, "T", 16640, [512], BF16), V("vstg1", arenaT, "T", 17664, [512], BF16)]
        e16 = V("e16", arenaT, "T", 0, [NT], F32, parts=16)
        nl16 = V("nl16", arenaT, "T", 4160, [NT], F32, parts=16)
        qpc = V("qpc", arenaT, "T", 0, [3, NT], BF16, parts=16)
        pt = [V(f"pt{i}", arenaT, "T", 1024 * i, [512], BF16) for i in range(3)]
        ef = [V(f"ef{i}", arenaT, "T", 3072 + 2048 * i, [512], F32) for i in range(2)]
        l16 = [V(f"l16{i}", arenaT, "T", 7168 + 1024 * i, [512], BF16) for i in range(2)]
        lsum = V("lsum", arenaT, "T", 9216, [512], BF16)
        rdv = V("rdv", arenaT, "T", 10240, [512], F32)
        ostg = stg
        gT = [V("gT0", arenaT, "T", 12480, [4, NT], BF16), V("gT1", arenaT, "T", 0, [4, NT], BF16)]
        rtmp = [V("rtmp0", arenaT, "T", 8320, [512], F32), V("rtmp1", arenaT, "T", 10368, [512], F32)]
        fstg = [V("fstg0", arenaT, "T", 12480, [NR], F32), V("fstg1", arenaT, "T", 16576, [NR], F32)]

        HB = 16480
        att = []
        for i in range(2):
            o = i * HB
            att.append(dict(
                qt=V(f"att{i}qt", arenaB, "B", o, [NT], BF16),
                kt=V(f"att{i}kt", arenaB, "B", o + 2080, [2048], BF16),
                vv=V(f"att{i}vv", arenaB, "B", o + 6176, [16, 128], BF16),
                qa=V(f"att{i}qa", arenaB, "B", o + 10272, [NT], BF16, parts=6),
                ka=V(f"att{i}ka", arenaB, "B", o + 12352, [2064], BF16, parts=6),
            ))
        S.region("gate", "C", 0, 28896)
        nl_all = carve(arenaC, 0, [2064], F32, parts=16)
        csv = carve(arenaC, 8256, [2064], F32, parts=16)
        pcs = carve(arenaC, 16512, [3, 2064], BF16, parts=16)

        ring_state = {"i": 0, "issued": 0, "plan": None, "req": []}
        LOOKAHEAD = 2

        def _slab_dst(i, view):
            slot = i % 3
            if view == "col":
                return ring[slot][:].rearrange("p (c n) -> p c n", c=NCH)
            return ring[slot][:].rearrange("p (c n) -> p c n", c=4)

        def _issue_slab(i):
            src_ap, view = ring_state["plan"][i]
            dst = _slab_dst(i, view)
            S.add("pool", lambda e, dst=dst, src_ap=src_ap: e.dma_start(out=dst, in_=src_ap),
                  w=[("ring", i % 3)], dma=f"ring{i % 3}")

        def wslab(src_ap, view):
            i = ring_state["i"]
            ring_state["i"] += 1
            ring_state["req"].append((src_ap, view))
            plan = ring_state["plan"]
            if plan is None:
                dst = _slab_dst(i, view)
                S.add("pool", lambda e, dst=dst, src_ap=src_ap: e.dma_start(out=dst, in_=src_ap),
                      w=[("ring", i % 3)], dma=f"ring{i % 3}")
            else:
                upto = min(len(plan), i + LOOKAHEAD + 1)
                while ring_state["issued"] < upto:
                    _issue_slab(ring_state["issued"])
                    ring_state["issued"] += 1
            return _slab_dst(i, view), ("ring", i % 3)

        def colslab(wmat, c0, ncols=512):
            return wmat[:, c0:c0 + ncols].rearrange("(c p) n -> p c n", p=128)

        evac_ctr = {"i": 0}

        def evac_copy(dst, src, r, w, scale=None, eng=None):
            if eng is None:
                eng = "act" if evac_ctr["i"] % 2 == 0 else "dve"
                evac_ctr["i"] += 1
            if eng == "act":
                if scale is None:
                    S.add("act", lambda e: e.activation(out=dst, in_=src, func=AF.Copy), r=r, w=w)
                else:
                    S.add("act", lambda e: e.activation(out=dst, in_=src, func=AF.Copy, scale=float(scale)), r=r, w=w)
            else:
                if scale is None:
                    S.add("dve", lambda e: e.tensor_copy(out=dst, in_=src), r=r, w=w)
                else:
                    S.add("dve", lambda e: e.tensor_scalar(out=dst, in0=src, scalar1=float(scale), scalar2=None,
                                                           op0=ALU.mult), r=r, w=w)

        def proj_fm(slab, skey, col0, M, grp, rhs_tile=aT, rhs_key="aT"):
            banks = G[grp]

            def emit(e):
                ins = None
                for c in range(NCH):
                    for ti, (t0, n) in enumerate(TCH):
                        ins = e.matmul(psb[banks[ti]][0:M, 0:n], lhsT=slab[:, c, col0:col0 + M],
                                       rhs=rhs_tile[:, c, t0:t0 + n], start=(c == 0), stop=(c == NCH - 1))
                return ins
            S.add("pe", emit, r=[skey, rhs_key], w=[PSK[b] for b in banks])

        def prologue():
            xv = xT.rearrange("(c p) t -> p c t", p=128)
            for q4 in range(4):
                S.add("sp", lambda e, q4=q4: e.dma_start(out=hT[:, 4 * q4:4 * q4 + 4, :], in_=xv[:, 4 * q4:4 * q4 + 4, :]),
                      w=[("hT", c) for c in range(4 * q4, 4 * q4 + 4)], dma=f"ld_h{q4}")
            S.add("sp", lambda e: e.dma_start(out=gvec[:], in_=gvec_d), w=["gvec"], dma="ld_c0")
            S.add("sp", lambda e: e.dma_start(out=consts[:], in_=consts_d), w=["consts"], dma="ld_c1")
            S.add("sp", lambda e: e.dma_start(out=bfs[:], in_=bf_d), w=["bfs"], dma="ld_c2")
            S.add("sp", lambda e: e.dma_start(out=rks[:], in_=rk_d), w=["rks"], dma="ld_c3")
            S.add("dve", lambda e: e.tensor_scalar(out=negb[:], in0=bfs[:], scalar1=-1.0, scalar2=None, op0=ALU.mult),
                  r=["bfs"], w=["negb"])
            S.add("dve", lambda e: e.memset(ones16[:], 1.0), w=["ones16"])
            S.add("dve", lambda e: e.memset(pcs, 1.0), w=["gate"])
            S.add("sp", lambda e: e.dma_start(out=ka_scr[:, 0:3, :], in_=pcs), r=["gate"], w=["ka_ones"], dma="aug1a")
            S.add("sp", lambda e: e.dma_start(out=qa_scr[:, 3:6, :], in_=pcs[:, :, 0:NT]), r=["gate"], w=["qa_ones"],
                  dma="aug1b")

        HK = [("hT", c) for c in range(NCH)]

        def rms_stats():
            for c in range(NCH):
                sq = sqv[c % 2]
                if c % 2 == 0:
                    S.add("act", lambda e, sq=sq, c=c: e.activation(out=sq, in_=hT[:, c, :], func=AF.Square),
                          r=[("hT", c)], w=[f"sq{c % 2}"])
                else:
                    S.add("dve", lambda e, sq=sq, c=c: e.tensor_tensor(out=sq, in0=hT[:, c, :], in1=hT[:, c, :],
                                                                      op=ALU.mult),
                          r=[("hT", c)], w=[f"sq{c % 2}"])

                def emit(e, sq=sq, c=c):
                    ins = None
                    for ti, (t0, n) in enumerate(TCH):
                        ins = e.matmul(psb[ti][:, 0:n], lhsT=cst(C_ONES), rhs=sq[:, t0:t0 + n],
                                       start=(c == 0), stop=(c == NCH - 1))
                    return ins
                S.add("pe", emit, r=[f"sq{c % 2}", "consts"], w=[PSK[0], PSK[1], PSK[2]])

            def emit_sqrt(e):
                ins = None
                for ti, (t0, n) in enumerate(TCH):
                    ins = e.activation(out=tmpf[:, t0:t0 + n], in_=psb[ti][:, 0:n], func=AF.Sqrt,
                                       bias=EPS, scale=1.0 / D)
                return ins
            S.add("act", emit_sqrt, r=[PSK[0], PSK[1], PSK[2]], w=["tmpf"])
            S.add("dve", lambda e: e.reciprocal(out=rstd, in_=tmpf), r=["tmpf"], w=["rstd"])

        def rms_apply(col):
            for c in range(NCH):
                S.add("dve", lambda e, c=c: e.scalar_tensor_tensor(out=aT[:, c, :], in0=hT[:, c, :],
                                                                  scalar=gcol(col + c), in1=rstd,
                                                                  op0=ALU.mult, op1=ALU.mult),
                      r=[("hT", c), "rstd", "gvec"], w=["aT"])

        def proj_k_group(wmat, c0, g, hctr):
            slab, skey = wslab(colslab(wmat, c0), "col")
            for hh in range(4):
                h = g * 4 + hh
                grp = hctr[0] % 2
                hctr[0] += 1
                proj_fm(slab, skey, hh * 128, 128, grp)
                st = stg[h % 2]
                b = G[grp]
                for ti, (t0, n) in enumerate(TCH):
                    evac_copy(st[:, t0:t0 + n], psb[b[ti]][:, 0:n], r=[PSK[b[ti]]], w=[f"stg{h % 2}"])
                S.add("dve", lambda e, st=st, h=h: e.tensor_copy(out=ktm[:, h, :], in_=st[:, NR:NT]),
                      r=[f"stg{h % 2}"], w=["ktm"])
                S.add("sp", lambda e, st=st, g=g, hh=hh: e.dma_start(out=kt_src[g][hh * 128:(hh + 1) * 128, :],
                                                                     in_=st[:, 0:NR]),
                      r=[f"stg{h % 2}"], w=[("kt_src", g)], dma=f"kts{g}")

        def proj_v_group(wmat, c0, g, vctr):
            slab, skey = wslab(colslab(wmat, c0), "col")
            for tb in range(9):
                M = 128 if tb < 8 else 16
                bank = 6 + (vctr[0] % 2)
                vs = vstg[vctr[0] % 2]
                vk = f"vstg{vctr[0] % 2}"
                vctr[0] += 1

                def emit(e, tb=tb, M=M, bank=bank):
                    ins = None
                    for c in range(NCH):
                        ins = e.matmul(psb[bank][0:M, 0:512], lhsT=aT[:, c, tb * 128:tb * 128 + M],
                                       rhs=slab[:, c, :], start=(c == 0), stop=(c == NCH - 1))
                    return ins
                S.add("pe", emit, r=[skey, "aT"], w=[PSK[bank]])
                if tb < 8:
                    evac_copy(vs, psb[bank][:, 0:512], r=[PSK[bank]], w=[vk])
                    S.add("sp", lambda e, vs=vs, g=g, tb=tb: e.dma_start(out=v_src[g][tb * 128:(tb + 1) * 128, :],
                                                                         in_=vs),
                          r=[vk], w=[("v_src", g)], dma=f"vs{g}")
                else:
                    evac_copy(vm[0:16, g * 512:(g + 1) * 512], psb[bank][0:16, 0:512], r=[PSK[bank]], w=["vm"])

        def gather_kv(g, ktd, vd, kkey, vkey):
            S.add("pool", lambda e: e.collective_compute("AllGather", ALU.bypass, replica_groups=RG,
                                                         ins=[kt_src[g]], outs=[ktd[g]]),
                  r=[("kt_src", g)], w=[(kkey, g)], dma=f"cck{g}", inc=1)
            S.add("pool", lambda e: e.collective_compute("AllGather", ALU.bypass, replica_groups=RG,
                                                         ins=[v_src[g]], outs=[vd[g]]),
                  r=[("v_src", g)], w=[(vkey, g)], dma=f"ccv{g}", inc=1)

        def proj_q(wmat, c0base, hctr, groups=(0, 1, 2, 3), evac_eng=None):
            for g in groups:
                slab, skey = wslab(colslab(wmat, c0base + g * 512), "col")
                for hh in range(4):
                    h = g * 4 + hh
                    grp = hctr[0] % 2
                    hctr[0] += 1
                    proj_fm(slab, skey, hh * 128, 128, grp)
                    st = stg[h % 2]
                    b = G[grp]
                    for ti, (t0, n) in enumerate(TCH):
                        evac_copy(st[:, t0:t0 + n], psb[b[ti]][:, 0:n], r=[PSK[b[ti]]], w=[f"stg{h % 2}"],
                                  scale=QSCALE, eng=evac_eng)
                    S.add("sp", lambda e, st=st, h=h: e.dma_start(out=q_scr[h], in_=st), r=[f"stg{h % 2}"],
                          w=[("q_scr", h)], dma=f"qs{h % 4}")

        def gate_phase(li, wmat, hctr):
            S.add("pool", lambda e: e.dma_start(out=wf_sb[:].rearrange("p c j -> p (c j)"), in_=wf_d[li]),
                  w=["wf"], dma="wf")
            grp = hctr[0] % 2
            hctr[0] += 1
            proj_fm(wf_sb, "wf", 0, 16, grp)
            b = G[grp]

            def emit_e(e):
                ins = None
                for ti, (t0, n) in enumerate(TCH):
                    ins = e.activation(out=e16[:, t0:t0 + n], in_=psb[b[ti]][0:16, 0:n], func=AF.Exp,
                                       bias=negb[:, li:li + 1], scale=-1.0)
                return ins
            S.add("act", emit_e, r=[PSK[x] for x in b] + ["negb"], w=["e16"])
            S.add("act", lambda e: e.activation(out=nl16, in_=e16, func=AF.Ln, bias=1.0, scale=1.0),
                  r=["e16"], w=["nl16"])
            S.add("sp", lambda e: e.dma_start(out=g_src, in_=nl16[:, 0:NR]), r=["nl16"], w=["g_src"], dma="gs")
            S.add("pool", lambda e: e.collective_compute("AllGather", ALU.bypass, replica_groups=RG,
                                                         ins=[g_src], outs=[g_dst]),
                  r=["g_src"], w=["g_dst"], dma="ccg", inc=1)

        def gate_phase2():
            S.add("sp", lambda e: e.dma_start(out=nl_all[:, 0:2048].rearrange("h (r t) -> h r t", r=2),
                                              in_=g_dst.rearrange("(r h) t -> h r t", r=2)),
                  r=["g_dst"], w=["gate"], dma="gl")
            S.add("dve", lambda e: e.tensor_copy(out=nl_all[:, 2048:2064], in_=nl16[:, NR:NT]),
                  r=["nl16", "gate"], w=["gate"])
            S.add("dve", lambda e: e.tensor_tensor_scan(out=csv[:, 2048:2064], data0=ones16[:, 0:16],
                                                        data1=nl_all[:, 2048:2064], initial=0.0,
                                                        op0=ALU.mult, op1=ALU.add),
                  r=["gate", "ones16"], w=["gate"])
            prev = csv[:, 2063:2064]
            for gk in range(16):
                r_, lb = gk % 2, gk // 2
                o = r_ * 1024 + lb * 128
                S.add("dve", lambda e, o=o, prev=prev: e.tensor_tensor_scan(out=csv[:, o:o + 128], data0=ones16[:],
                                                                            data1=nl_all[:, o:o + 128], initial=prev,
                                                                            op0=ALU.mult, op1=ALU.add),
                      r=["gate", "ones16"], w=["gate"])
                prev = csv[:, o + 127:o + 128]
            gk_ = dict(r=["gate"], w=["gate"])
            S.add("dve", lambda e: e.tensor_copy(out=pcs[:, 0, :], in_=csv), **gk_)
            S.add("dve", lambda e: e.tensor_tensor(out=nl_all, in0=csv, in1=pcs[:, 0, :], op=ALU.subtract), **gk_)
            S.add("dve", lambda e: e.tensor_copy(out=pcs[:, 1, :], in_=nl_all), **gk_)
            S.add("dve", lambda e: e.tensor_tensor(out=csv, in0=nl_all, in1=pcs[:, 1, :], op=ALU.subtract), **gk_)
            S.add("dve", lambda e: e.tensor_copy(out=pcs[:, 2, :], in_=csv), **gk_)
            for j in range(3):
                S.add("dve", lambda e, j=j: e.tensor_tensor(out=nl_all[:, 0:NR], in0=pcs[:, j, 0:NR],
                                                            in1=pcs[:, j, NR:2 * NR], op=ALU.subtract), **gk_)
                S.add("dve", lambda e, j=j: e.scalar_tensor_tensor(out=qpc[:, j, 0:NR], in0=nl_all[:, 0:NR],
                                                                   scalar=rks[:, 0:1], in1=pcs[:, j, 0:NR],
                                                                   op0=ALU.mult, op1=ALU.subtract),
                      r=["gate", "rks"], w=["qpc"])
                S.add("dve", lambda e, j=j: e.tensor_scalar(out=qpc[:, j, NR:NT], in0=pcs[:, j, 2048:2064],
                                                            scalar1=-1.0, scalar2=None, op0=ALU.mult),
                      r=["gate"], w=["qpc"])

        def gate_phase3():
            S.add("sp", lambda e: e.dma_start(out=ka_scr[:, 3:6, :], in_=pcs), r=["gate"], w=["ka_scr"], dma="auga")
            S.add("sp", lambda e: e.dma_start(out=qa_scr[:, 0:3, :], in_=qpc), r=["qpc"], w=["qa_scr"], dma="augb")

        def load_head(h, fox, ktd, vd, kkey, vkey):
            g, hh = h // 4, h % 4
            A = att[h % 2]
            i = h % 2
            S.add("sp", lambda e: e.dma_start(out=A["qt"], in_=q_scr[h]), r=[("q_scr", h)], w=[f"att{i}qt"],
                  dma=f"lq{i}")
            S.add("sp", lambda e: e.dma_start(
                out=A["kt"].rearrange("p (r t) -> p r t", r=2),
                in_=ktd[g].rearrange("(r x) t -> x r t", r=2)[hh * 128:(hh + 1) * 128]),
                r=[(kkey, g)], w=[f"att{i}kt"], dma=f"lk{i}")
            S.add("sp", lambda e: e.dma_start(
                out=A["vv"],
                in_=vd[g].rearrange("(b p) n -> p b n", p=128)[:, :, hh * 128:(hh + 1) * 128]),
                r=[(vkey, g)], w=[f"att{i}vv"], dma=f"lv{i}")
            if fox:
                S.add("sp", lambda e: e.dma_start(out=A["qa"], in_=qa_scr[h]), r=["qa_scr", "qa_ones"],
                      w=[f"att{i}qa"], dma=f"lqa{i}")
                S.add("sp", lambda e: e.dma_start(out=A["ka"], in_=ka_scr[h]), r=["ka_scr", "ka_ones"],
                      w=[f"att{i}ka"], dma=f"lka{i}")

        sctr = [0]
        pctr = [0]
        octr = [0]
        lctr = [0]
        ectr = [0]

        def run_pipeline(items, nstages, lag):
            n = len(items)
            for s in range(n + lag * (nstages - 1)):
                for k in range(nstages):
                    t = s - k * lag
                    if 0 <= t < n:
                        it = items[t]
                        it["stages"][k]()
                        if k == nstages - 1 and it.get("post"):
                            it["post"]()

        def fox_items(h, items, after_head):
            A = att[h % 2]
            i = h % 2
            qt, kt, vv, qa, ka = A["qt"], A["kt"], A["vv"], A["qa"], A["ka"]
            rk_ = [f"att{i}qt", f"att{i}kt", f"att{i}qa", f"att{i}ka", "consts", "ktm"]
            os_ = ostg[h % 2]
            osk = f"stg{h % 2}"

            def tile(kl, kal, vl, krows, c0, a0, N, neg, first, last, ob, db, post=None):
                sb_ = sctr[0] % 3
                sctr[0] += 1
                pi = pctr[0] % 3
                pctr[0] += 1
                P = pt[pi]

                def emit_s(e):
                    e.matmul(psb[sb_][0:krows, 0:N], lhsT=kl, rhs=qt[:, c0 + a0:c0 + a0 + N], start=True, stop=False)
                    ins = e.matmul(psb[sb_][0:krows, 0:N], lhsT=kal, rhs=qa[:, c0 + a0:c0 + a0 + N],
                                   start=False, stop=(neg is None))
                    if neg is not None:
                        ncols = min(128, N)
                        ins = e.matmul(psb[sb_][0:krows, 0:ncols], lhsT=cst(C_IDENT, krows, krows),
                                       rhs=neg[0:krows, 0:ncols], start=False, stop=True)
                    return ins

                def emit_o(e):
                    e.matmul(psb[ob][:, a0:a0 + N], lhsT=vl, rhs=P[0:krows, 0:N], start=first, stop=last)
                    return e.matmul(psb[db][:, a0:a0 + N], lhsT=cst(C_ONES, krows, 128), rhs=P[0:krows, 0:N],
                                    start=first, stop=last)

                def stA():
                    S.add("pe", emit_s, r=rk_, w=[PSK[sb_]])
                    S.add("act", lambda e: e.activation(out=P[0:krows, 0:N], in_=psb[sb_][0:krows, 0:N],
                                                        func=AF.Exp), r=[PSK[sb_]], w=[f"pt{pi}"])

                def stB():
                    S.add("pe", emit_o, r=[f"pt{pi}", f"att{i}vv", "vm", "consts"], w=[PSK[ob], PSK[db]])
                items.append(dict(stages=[stA, stB], post=post))

            def finish(c0, n, ob, db):
                S.add("dve", lambda e: e.reciprocal(out=rdv[:, 0:n], in_=psb[db][:, 0:n]), r=[PSK[db]], w=["rdv"])
                S.add("dve", lambda e: e.tensor_tensor(out=os_[:, c0:c0 + n], in0=psb[ob][:, 0:n], in1=rdv[:, 0:n],
                                                       op=ALU.mult), r=[PSK[ob], "rdv"], w=[osk])

            for qtile in range(2):
                c0 = qtile * 512
                lb0 = qtile * 4
                ob = 3 + (octr[0] % 2)
                db = 5 + (octr[0] % 2)
                octr[0] += 1
                tile(ktm[:, h, :], ka[:, 2048:2064], vm[0:16, h * 128:(h + 1) * 128], 16, c0, 0, 512, None,
                     True, False, ob, db)
                nblk = lb0 + 4
                for lbk in range(nblk):
                    for r_ in range(2):
                        j0 = max(lbk, lb0)
                        a0 = (j0 - lb0) * 128
                        N = 512 - a0
                        kc = r_ * 1024 + lbk * 128
                        neg = cst(C_FNEG0 + r_) if lbk >= lb0 else None
                        lastt = (lbk == nblk - 1 and r_ == 1)
                        post = (lambda c0=c0, ob=ob, db=db: finish(c0, 512, ob, db)) if lastt else None
                        tile(kt[:, kc:kc + 128], ka[:, kc:kc + 128], vv[:, r_ * 8 + lbk, :], 128, c0, a0, N, neg,
                             False, lastt, ob, db, post)
            ob = 3 + (octr[0] % 2)
            db = 5 + (octr[0] % 2)
            octr[0] += 1

            def post_head(ob=ob, db=db):
                finish(NR, 16, ob, db)
                S.add("sp", lambda e: e.dma_start(out=o_scr[h], in_=os_), r=[osk], w=["o_scr"], dma="os")
                after_head(h)
            tile(ktm[:, h, :], ka[:, 2048:2064], vm[0:16, h * 128:(h + 1) * 128], 16, NR, 0, 16, cst(C_CNEG),
                 True, True, ob, db, post_head)

        def sb_items(h, items, after_head):
            A = att[h % 2]
            i = h % 2
            qt, kt, vv = A["qt"], A["kt"], A["vv"]
            rk_ = [f"att{i}qt", f"att{i}kt", "consts", "ktm"]
            os_ = ostg[h % 2]
            osk = f"stg{h % 2}"

            def tile(kl, vl, krows, c0, a0, N, neg, lsum_cols, ob, last, zero_n=None, post=None):
                zb = sctr[0] % 3
                sctr[0] += 1
                pi = pctr[0] % 3
                pctr[0] += 1
                li_ = lctr[0] % 2
                lctr[0] += 1
                ei = ectr[0] % 2
                ectr[0] += 1
                P, L, E = pt[pi], l16[li_], ef[ei]

                def emit_z(e):
                    ins = e.matmul(psb[zb][0:krows, 0:N], lhsT=kl, rhs=qt[:, c0 + a0:c0 + a0 + N],
                                   start=True, stop=False)
                    if neg is not None:
                        ncols = min(128, N)
                        ins = e.matmul(psb[zb][0:krows, 0:ncols], lhsT=cst(C_IDENT, krows, krows),
                                       rhs=neg[0:krows, 0:ncols], start=False, stop=False)
                    return ins

                def emit_a(e):
                    have = lsum_cols is not None
                    ins = e.matmul(psb[zb][0:krows, 0:N], lhsT=cst(C_NTRI, krows, krows), rhs=L[0:krows, 0:N],
                                   start=False, stop=not have)
                    if have:
                        x0, n = lsum_cols
                        ins = e.matmul(psb[zb][0:krows, x0 - a0:x0 - a0 + n], lhsT=cst(C_NONES, 128, krows),
                                       rhs=lsum[:, x0:x0 + n], start=False, stop=True)
                    return ins

                def emit_l(e):
                    ins = None
                    if lsum_cols is None or lsum_cols[0] > a0:
                        ins = e.tensor_copy(out=lsum[:, a0:a0 + 128], in_=L[:, 0:128])
                    if lsum_cols is not None:
                        x0, n = lsum_cols
                        ins = e.tensor_tensor(out=lsum[:, x0:x0 + n], in0=lsum[:, x0:x0 + n],
                                              in1=L[:, x0 - a0:x0 - a0 + n], op=ALU.add)
                    return ins

                def emit_o(e):
                    if zero_n is not None:
                        e.matmul(psb[ob][:, 0:zero_n], lhsT=cst(C_ZERO), rhs=qt[:, c0:c0 + zero_n],
                                 start=True, stop=False, skip_group_check=True)
                    return e.matmul(psb[ob][:, a0:a0 + N], lhsT=vl, rhs=P[0:krows, 0:N],
                                    start=False, stop=last, skip_group_check=True)

                def stA():
                    S.add("pe", emit_z, r=rk_, w=[PSK[zb]])
                    S.add("act", lambda e: e.activation(out=E[0:krows, 0:N], in_=psb[zb][0:krows, 0:N],
                                                        func=AF.Exp), r=[PSK[zb]], w=[f"ef{ei}"])
                    S.add("act", lambda e: e.activation(out=L[0:krows, 0:N], in_=E[0:krows, 0:N], func=AF.Ln,
                                                        bias=1.0, scale=1.0), r=[f"ef{ei}"], w=[f"l16{li_}"])

                def stB():
                    S.add("pe", emit_a, r=[f"l16{li_}", "lsum", "consts"], w=[PSK[zb]])
                    if krows == 128 and not last:
                        S.add("dve", emit_l, r=[f"l16{li_}"], w=["lsum"])
                    S.add("act", lambda e: e.activation(out=P[0:krows, 0:N], in_=psb[zb][0:krows, 0:N],
                                                        func=AF.Exp), r=[PSK[zb]], w=[f"pt{pi}"])

                def stC():
                    S.add("pe", emit_o, r=[f"pt{pi}", f"att{i}vv", f"att{i}qt", "vm", "consts"], w=[PSK[ob]])
                items.append(dict(stages=[stA, stB, stC], post=post))

            for qtile in range(2):
                c0 = qtile * 512
                lb0 = qtile * 4
                ob = 3 + (octr[0] % 3)
                octr[0] += 1
                have_from = None
                firstt = True
                for lbk in range(lb0 + 3, -1, -1):
                    for r_ in (1, 0):
                        j0 = max(lbk, lb0)
                        a0 = (j0 - lb0) * 128
                        N = 512 - a0
                        kc = r_ * 1024 + lbk * 128
                        neg = cst(C_SNEG0 + r_) if lbk >= lb0 else None
                        lc = None if have_from is None else (have_from, 512 - have_from)
                        tile(kt[:, kc:kc + 128], vv[:, r_ * 8 + lbk, :], 128, c0, a0, N, neg, lc, ob, False,
                             zero_n=(512 if firstt else None))
                        firstt = False
                        have_from = a0

                def post_q(ob=ob, c0=c0):
                    S.add("dve", lambda e: e.tensor_copy(out=os_[:, c0:c0 + 512], in_=psb[ob][:, 0:512]),
                          r=[PSK[ob]], w=[osk])
                tile(ktm[:, h, :], vm[0:16, h * 128:(h + 1) * 128], 16, c0, 0, 512, None, (0, 512), ob, True,
                     post=post_q)
            ob = 3 + (octr[0] % 3)
            octr[0] += 1

            def post_head(ob=ob):
                S.add("dve", lambda e: e.tensor_copy(out=os_[:, NR:NT], in_=psb[ob][:, 0:16]), r=[PSK[ob]], w=[osk])
                S.add("sp", lambda e: e.dma_start(out=o_scr[h], in_=os_), r=[osk], w=["o_scr"], dma="os")
                after_head(h)
            tile(ktm[:, h, :], vm[0:16, h * 128:(h + 1) * 128], 16, NR, 0, 16, cst(C_SCNEG), None, ob, True,
                 zero_n=16, post=post_head)

        def attention(fox, ktd, vd, kkey, vkey):
            load_head(0, fox, ktd, vd, kkey, vkey)
            load_head(1, fox, ktd, vd, kkey, vkey)

            def after_head(h):
                if h + 2 < NH:
                    load_head(h + 2, fox, ktd, vd, kkey, vkey)
            items = []
            for h in range(NH):
                if fox:
                    fox_items(h, items, after_head)
                else:
                    sb_items(h, items, after_head)
            if fox:
                run_pipeline(items, 2, 2)
            else:
                run_pipeline(items, 3, 1)

        def add_to_h(n, grp):
            b = G[grp]
            for ti, (t0, nn) in enumerate(TCH):
                S.add("dve", lambda e, ti=ti, t0=t0, nn=nn: e.tensor_tensor(out=hT[:, n, t0:t0 + nn],
                                                                           in0=psb[b[ti]][:, 0:nn],
                                                                           in1=hT[:, n, t0:t0 + nn], op=ALU.add),
                      r=[PSK[b[ti]], ("hT", n)], w=[("hT", n)])

        def oproj(wmat, hctr):
            S.add("sp", lambda e: e.dma_start(out=aT, in_=o_scr.rearrange("h p t -> p h t")), r=["o_scr"], w=["aT"],
                  dma="lo")
            for s in range(4):
                slab, skey = wslab(colslab(wmat, s * 512), "col")
                for nn in range(4):
                    n = s * 4 + nn
                    grp = hctr[0] % 2
                    hctr[0] += 1
                    proj_fm(slab, skey, nn * 128, 128, grp)
                    add_to_h(n, grp)

        def mlp_up(li, fg, hctr):
            slab, skey = wslab(colslab(w_up[li], fg * 512), "col")
            gt = gT[fg % 2]
            gk = f"gT{fg % 2}"

            def up_chunk(fc):
                grp = hctr[0] % 2
                hctr[0] += 1
                proj_fm(slab, skey, fc * 128, 128, grp)
                b = G[grp]

                def evac(ti, t0, n):
                    rt = rtmp[ti % 2]
                    rkk = f"rtmp{ti % 2}"
                    S.add("act", lambda e: e.activation(out=rt[:, 0:n], in_=psb[b[ti]][:, 0:n], func=AF.Relu),
                          r=[PSK[b[ti]]], w=[rkk])
                    S.add("act", lambda e: e.activation(out=gt[:, fc, t0:t0 + n], in_=rt[:, 0:n], func=AF.Square),
                          r=[rkk], w=[gk])
                for ti, (t0, n) in enumerate(TCH):
                    evac(ti, t0, n)
            for fc in range(4):
                up_chunk(fc)

        def mlp_down(li, fg, hctr):
            gt = gT[fg % 2]
            gk = f"gT{fg % 2}"
            dslab, dkey = wslab(w_down[li, fg * 512:(fg + 1) * 512, :].rearrange("(c p) n -> p c n", p=128), "row")

            def down_chunk(n):
                grp = hctr[0] % 2
                hctr[0] += 1
                b = G[grp]

                def emit(e):
                    ins = None
                    for fc in range(4):
                        for ti, (t0, nn) in enumerate(TCH):
                            ins = e.matmul(psb[b[ti]][:, 0:nn], lhsT=dslab[:, fc, n * 128:(n + 1) * 128],
                                           rhs=gt[:, fc, t0:t0 + nn], start=(fc == 0), stop=(fc == 3))
                    return ins
                S.add("pe", emit, r=[dkey, gk], w=[PSK[x] for x in b])
                add_to_h(n, grp)
            for n in range(NCH):
                down_chunk(n)

        def mlp(li, hctr):
            pend = None
            for fg in range(16):
                mlp_up(li, fg, hctr)
                if pend is not None:
                    mlp_down(li, pend, hctr)
                pend = fg
            mlp_down(li, pend, hctr)

        def program():
            prologue()
            hctr = [0]
            vctr = [0]
            for li in range(nlayers):
                rms_stats()
                if li < 2:
                    wmat = fox_w_in[li]
                    rms_apply(li * 16)
                    for g in range(4):
                        proj_k_group(wmat, D + g * 512, g, hctr)
                        proj_v_group(wmat, 2 * D + g * 512, g, vctr)
                        gather_kv(g, kt_dst, v_dst, "kt_dst", "v_dst")
                    gate_phase(li, wmat, hctr)
                    gate_phase2()
                    proj_q(wmat, 0, hctr, groups=(0, 1, 2), evac_eng="act")
                    gate_phase3()
                    proj_q(wmat, 0, hctr, groups=(3,), evac_eng="act")
                    attention(True, kt_dst, v_dst, "kt_dst", "v_dst")
                    oproj(fox_w_o[li], hctr)
                else:
                    if li == 2:
                        rms_apply(128)
                        for g in range(4):
                            proj_k_group(w_kv, g * 512, g, hctr)
                            proj_v_group(w_kv, D + g * 512, g, vctr)
                            gather_kv(g, skt_dst, sv_dst, "skt_dst", "sv_dst")
                    rms_apply(li * 16)
                    proj_q(sb_w_q[li - 2], 0, hctr)
                    attention(False, skt_dst, sv_dst, "skt_dst", "sv_dst")
                    oproj(sb_w_o[li - 2], hctr)
                rms_stats()
                rms_apply(64 + li * 16)
                mlp(li, hctr)

            rms_stats()
            outs = []
            for c in range(NCH):
                fs = fstg[c % 2]
                S.add("dve", lambda e, c=c, fs=fs: e.scalar_tensor_tensor(out=fs, in0=hT[:, c, 0:NR],
                                                                          scalar=gcol(144 + c), in1=rstd[:, 0:NR],
                                                                          op0=ALU.mult, op1=ALU.mult),
                      r=[("hT", c), "rstd", "gvec"], w=[f"fstg{c % 2}"])
                S.add("sp", lambda e, c=c, fs=fs: e.dma_start(out=yT[c * 128:(c + 1) * 128, :], in_=fs),
                      r=[f"fstg{c % 2}"], w=[("yT", c)], dma="st_y")
            S.add("sp", None, r=[("yT", c) for c in range(NCH)])


        def reset_counters():
            ring_state.update(i=0, issued=0, req=[])
            for ctr in (sctr, pctr, octr, lctr, ectr):
                ctr[0] = 0
            evac_ctr["i"] = 0

        S_real = S
        S = Sched()
        S.alias, S.regions = S_real.alias, S_real.regions
        reset_counters()
        program()
        ring_state["plan"] = list(ring_state["req"])
        S = Sched()
        S.alias, S.regions = S_real.alias, S_real.regions
        reset_counters()
        program()

        S.finalize()
        sems = {}
        for idx, k in enumerate(sorted(S.semkeys, key=str)):
            sems[k] = es.enter_context(nc.semaphore(f"s{idx}"))
        with nc.Block() as block:
            @block.tensor
            def _(e):
                S.emit_engine("pe", e, sems)

            @block.scalar
            def _(e):
                S.emit_engine("act", e, sems)

            @block.vector
            def _(e):
                S.emit_engine("dve", e, sems)

            @block.gpsimd
            def _(e):
                S.emit_engine("pool", e, sems)

            @block.sync
            def _(e):
                S.emit_engine("sp", e, sems)
    return nc


def make_consts(rank):
    i = np.arange(128)[:, None]
    j = np.arange(128)[None, :]
    ones = np.ones((128, 128), np.float32)
    zeros = np.zeros((128, 128), np.float32)
    ntri = np.where(i >= j, -1.0, 0.0).astype(np.float32)
    ident = np.eye(128, dtype=np.float32)
    cneg = np.where(i <= j, 0.0, NEGV).astype(np.float32)
    scneg = np.where(i < j, 0.0, NEGV).astype(np.float32)
    allneg = np.full((128, 128), NEGV, np.float32)
    if rank == 0:
        fneg0, fneg1, sneg0, sneg1 = cneg, allneg, scneg, allneg
    else:
        fneg0, fneg1, sneg0, sneg1 = zeros, cneg, zeros, scneg
    blocks = [ones, ntri, -ones, ident, zeros, fneg0, fneg1, sneg0, sneg1, cneg, scneg]
    return np.concatenate(blocks, axis=1).astype(ml_dtypes.bfloat16)


def prep_inputs(x, meta_tokens, norm_attn, norm_mlp, w_up, w_down, fox_w_in, fox_b_f, fox_w_o,
                kv_norm, w_kv, sb_w_q, sb_w_o, final_norm):
    f = lambda a: np.ascontiguousarray(np.asarray(a, dtype=np.float32))
    x = f(x)
    meta = f(meta_tokens)
    gv = np.zeros((128, 160), np.float32)
    na, nm = f(norm_attn), f(norm_mlp)
    for l in range(DEPTH):
        gv[:, l * 16:(l + 1) * 16] = na[l].reshape(16, 128).T
        gv[:, 64 + l * 16:64 + (l + 1) * 16] = nm[l].reshape(16, 128).T
    gv[:, 128:144] = f(kv_norm).reshape(16, 128).T
    gv[:, 144:160] = f(final_norm).reshape(16, 128).T
    shared = dict(w_up=f(w_up), w_down=f(w_down), fox_w_in=f(fox_w_in), fox_w_o=f(fox_w_o), w_kv=f(w_kv),
                  sb_w_q=f(sb_w_q), sb_w_o=f(sb_w_o), bf=np.ascontiguousarray(f(fox_b_f).T))
    fwi = shared["fox_w_in"]
    shared["wf"] = np.ascontiguousarray(
        fwi[:, :, 3 * D:].reshape(2, NCH, 128, NH).transpose(0, 2, 1, 3).reshape(2, 128, NCH * NH))
    in_maps = []
    for c in range(8):
        b, r = c // 2, c % 2
        xb = x[b].reshape(16, 128, D)[r::2].reshape(NR, D)
        xt = np.ascontiguousarray(np.concatenate([xb, meta], axis=0).T)
        m = dict(shared)
        m["xT"] = xt
        m["gvec"] = gv
        m["rk"] = np.full((16, 1), float(r), np.float32)
        m["consts"] = make_consts(r)
        in_maps.append(m)
    return in_maps


def assemble(results):
    out = np.zeros((BATCH, SEQ, D), np.float32)
    for c in range(8):
        b, r = c // 2, c % 2
        y = np.asarray(results[c]["yT"], dtype=np.float32).T
        out[b].reshape(16, 128, D)[r::2] = y.reshape(8, 128, D)
    return out


_NC_CACHE = {}


def kernel(**inputs):
    in_maps = prep_inputs(**inputs)
    if "nc" not in _NC_CACHE:
        _NC_CACHE["nc"] = build(DEPTH)
    res = run_bass_kernel_spmd(_NC_CACHE["nc"], in_maps, core_ids=list(range(8)))
    return assemble(res.results)
```
